# Optimizing a Trainium2 kernel written in Bass

```python
import jax, jax.numpy as jnp
from jax import lax
import numpy as np

D_MODEL = 1024
BATCH = 8
SEQ = 2048
DEPTH = 1
DEC_BATCH = 128
DEC_SEQ = 8
PAST_LEN = 8192
PAGE_SIZE = 128

N_HEADS = 16
N_KV_HEADS = 4
HEAD_DIM = D_MODEL // N_HEADS
GROUP = N_HEADS // N_KV_HEADS
WINDOW = 128
Q_DIM = N_HEADS * HEAD_DIM
KV_DIM = N_KV_HEADS * HEAD_DIM
CONV_CH = D_MODEL
CONV_WIDTH = 31
D_FF = 2816
FFN_CONV_WIDTH = 3
EPS = 1e-6
NEG_INF = -1e30
SPLITS = (Q_DIM, KV_DIM, KV_DIM, CONV_CH, CONV_CH, D_MODEL, D_MODEL)
IN_DIM = sum(SPLITS)
SPLIT_IDX = tuple(int(i) for i in np.cumsum(SPLITS)[:-1])

kernel_name = 'hybrid_swa_sink_conformer_convffn_step'


def rmsnorm(x, g):
    xf = x.astype(jnp.float32)
    y = xf * lax.rsqrt(jnp.mean(xf * xf, axis=-1, keepdims=True) + EPS)
    return (y * g.astype(jnp.float32)).astype(x.dtype)


def layernorm(x, g, b):
    xf = x.astype(jnp.float32)
    mu = jnp.mean(xf, axis=-1, keepdims=True)
    var = jnp.mean(jnp.square(xf - mu), axis=-1, keepdims=True)
    y = (xf - mu) * lax.rsqrt(var + EPS)
    return (y * g.astype(jnp.float32) + b.astype(jnp.float32)).astype(x.dtype)


def dwconv_valid(x, w, b):
    c = x.shape[-1]
    y = lax.conv_general_dilated(x, w.astype(x.dtype)[:, None, :], window_strides=(1,), padding='VALID',
                                 dimension_numbers=('NWC', 'WIO', 'NWC'), feature_group_count=c)
    return y + b.astype(x.dtype)


def attend(q, k, v, mask, sinks):
    s = jnp.einsum('bnqkgd,bnskd->bnkgqs', q, k).astype(jnp.float32) * (HEAD_DIM ** -0.5)
    s = jnp.where(mask[None, :, None, None], s, NEG_INF)
    sink = sinks.astype(jnp.float32).reshape(N_KV_HEADS, GROUP)[None, None, :, :, None, None]
    sink = jnp.broadcast_to(sink, s.shape[:-1] + (1,))
    p = jax.nn.softmax(jnp.concatenate([s, sink], axis=-1), axis=-1)[..., :-1]
    return jnp.einsum('bnkgqs,bnskd->bnqkgd', p.astype(v.dtype), v)


def swa_prompt(q, k, v, sinks):
    b, s = q.shape[:2]
    nb = s // WINDOW
    qb = q.reshape(b, nb, WINDOW, N_KV_HEADS, GROUP, HEAD_DIM)
    kb = k.reshape(b, nb, WINDOW, N_KV_HEADS, HEAD_DIM)
    vb = v.reshape(b, nb, WINDOW, N_KV_HEADS, HEAD_DIM)
    kk = jnp.concatenate([jnp.concatenate([jnp.zeros_like(kb[:, :1]), kb[:, :-1]], axis=1), kb], axis=2)
    vv = jnp.concatenate([jnp.concatenate([jnp.zeros_like(vb[:, :1]), vb[:, :-1]], axis=1), vb], axis=2)
    i = jnp.arange(WINDOW)[:, None]
    j = jnp.arange(2 * WINDOW)[None, :]
    delta = i + WINDOW - j
    band = (delta >= 0) & (delta < WINDOW)
    blk = jnp.arange(nb)[:, None, None]
    mask = band[None] & (blk * WINDOW + j[None] - WINDOW >= 0)
    o = attend(qb, kk, vv, mask, sinks).reshape(b, s, Q_DIM)
    return o, k[:, -WINDOW:], v[:, -WINDOW:]


def swa_sample(q, k, v, k_past, v_past, sinks):
    b, t = q.shape[:2]
    kk = jnp.concatenate([k_past, k], axis=1)
    vv = jnp.concatenate([v_past, v], axis=1)
    i = jnp.arange(t)[:, None]
    j = jnp.arange(WINDOW + t)[None, :]
    delta = i + WINDOW - j
    mask = ((delta >= 0) & (delta < WINDOW))[None]
    qb = q.reshape(b, 1, t, N_KV_HEADS, GROUP, HEAD_DIM)
    o = attend(qb, kk[:, None], vv[:, None], mask, sinks).reshape(b, t, Q_DIM)
    return o, kk[:, -WINDOW:], vv[:, -WINDOW:]


def hybrid_layer(x, k_past, v_past, conv_b_past, conv_f_past, norm1_g, w_in, sinks, w_o_attn,
                 conv_b_w, conv_b_bias, ln_b_g, ln_b_b, w_pb, w_out, norm2_g, w_up,
                 ffn_conv_w, ffn_conv_b, w_down, is_prompt):
    b, t, _ = x.shape
    xn = rmsnorm(x, norm1_g)
    proj = xn @ w_in
    q, k, v, glu_a, glu_b, gate_a, gate_b = jnp.split(proj, SPLIT_IDX, axis=-1)
    q = q.reshape(b, t, N_HEADS, HEAD_DIM)
    k = k.reshape(b, t, N_KV_HEADS, HEAD_DIM)
    v = v.reshape(b, t, N_KV_HEADS, HEAD_DIM)
    if is_prompt:
        attn, k_new, v_new = swa_prompt(q, k, v, sinks)
    else:
        attn, k_new, v_new = swa_sample(q, k, v, k_past, v_past, sinks)
    branch_a = attn @ w_o_attn
    u = glu_a * jax.nn.sigmoid(glu_b)
    if is_prompt:
        conv_b_past = jnp.zeros((b, CONV_WIDTH - 1, CONV_CH), u.dtype)
    u_ext = jnp.concatenate([conv_b_past, u], axis=1)
    conv_b_new = u_ext[:, -(CONV_WIDTH - 1):]
    c = dwconv_valid(u_ext, conv_b_w, conv_b_bias)
    branch_b = jax.nn.silu(layernorm(c, ln_b_g, ln_b_b)) @ w_pb
    merged = jax.nn.sigmoid(gate_a) * branch_a + jax.nn.sigmoid(gate_b) * branch_b
    h = x + merged @ w_out
    up = rmsnorm(h, norm2_g) @ w_up
    if is_prompt:
        conv_f_past = jnp.zeros((b, FFN_CONV_WIDTH - 1, 2 * D_FF), up.dtype)
    up_ext = jnp.concatenate([conv_f_past, up], axis=1)
    conv_f_new = up_ext[:, -(FFN_CONV_WIDTH - 1):]
    uc = dwconv_valid(up_ext, ffn_conv_w, ffn_conv_b)
    gate, val = jnp.split(uc, 2, axis=-1)
    y = h + (jax.nn.silu(gate) * val) @ w_down
    return y, k_new, v_new, conv_b_new, conv_f_new


def setup_inputs(seed: int = 0) -> dict:
    key = jax.random.key(seed)
    ks = jax.random.split(key, 24)
    f32 = jnp.float32
    nrm = lambda k, shape, scale: jax.random.normal(k, shape, f32) * scale
    return {
        'x_prompt': nrm(ks[0], (BATCH, SEQ, D_MODEL), 1.0),
        'x_sample': nrm(ks[1], (DEC_BATCH, DEC_SEQ, D_MODEL), 1.0),
        'cache_k_win': nrm(ks[2], (DEPTH, DEC_BATCH, WINDOW, N_KV_HEADS, HEAD_DIM), 1.0),
        'cache_v_win': nrm(ks[3], (DEPTH, DEC_BATCH, WINDOW, N_KV_HEADS, HEAD_DIM), 1.0),
        'state_conv_b': nrm(ks[4], (DEPTH, DEC_BATCH, CONV_WIDTH - 1, CONV_CH), 0.5),
        'state_conv_ffn': nrm(ks[5], (DEPTH, DEC_BATCH, FFN_CONV_WIDTH - 1, 2 * D_FF), 0.5),
        'norm1_g': 1.0 + nrm(ks[6], (DEPTH, D_MODEL), 0.02),
        'w_in': nrm(ks[7], (DEPTH, D_MODEL, IN_DIM), D_MODEL ** -0.5),
        'sinks': nrm(ks[8], (DEPTH, N_HEADS), 0.5),
        'w_o_attn': nrm(ks[9], (DEPTH, Q_DIM, D_MODEL), Q_DIM ** -0.5),
        'conv_b_w': nrm(ks[10], (DEPTH, CONV_WIDTH, CONV_CH), CONV_WIDTH ** -0.5),
        'conv_b_bias': nrm(ks[11], (DEPTH, CONV_CH), 0.02),
        'ln_b_g': 1.0 + nrm(ks[12], (DEPTH, CONV_CH), 0.02),
        'ln_b_b': nrm(ks[13], (DEPTH, CONV_CH), 0.02),
        'w_pb': nrm(ks[14], (DEPTH, CONV_CH, D_MODEL), CONV_CH ** -0.5),
        'w_out': nrm(ks[15], (DEPTH, D_MODEL, D_MODEL), D_MODEL ** -0.5),
        'norm2_g': 1.0 + nrm(ks[16], (DEPTH, D_MODEL), 0.02),
        'w_up': nrm(ks[17], (DEPTH, D_MODEL, 2 * D_FF), D_MODEL ** -0.5),
        'ffn_conv_w': nrm(ks[18], (DEPTH, FFN_CONV_WIDTH, 2 * D_FF), FFN_CONV_WIDTH ** -0.5),
        'ffn_conv_b': nrm(ks[19], (DEPTH, 2 * D_FF), 0.02),
        'w_down': nrm(ks[20], (DEPTH, D_FF, D_MODEL), D_FF ** -0.5),
        'norm_f_g': 1.0 + nrm(ks[21], (D_MODEL,), 0.02),
    }


def reference(x_prompt, x_sample, cache_k_win, cache_v_win, state_conv_b, state_conv_ffn,
              norm1_g, w_in, sinks, w_o_attn, conv_b_w, conv_b_bias, ln_b_g, ln_b_b, w_pb, w_out,
              norm2_g, w_up, ffn_conv_w, ffn_conv_b, w_down, norm_f_g):
    hp, hs = x_prompt, x_sample
    kp_l, vp_l, cbp_l, cfp_l = [], [], [], []
    ks_l, vs_l, cbs_l, cfs_l = [], [], [], []
    for l in range(DEPTH):
        weights = (norm1_g[l], w_in[l], sinks[l], w_o_attn[l], conv_b_w[l], conv_b_bias[l], ln_b_g[l],
                   ln_b_b[l], w_pb[l], w_out[l], norm2_g[l], w_up[l], ffn_conv_w[l], ffn_conv_b[l], w_down[l])
        hp, kp, vp, cbp, cfp = hybrid_layer(hp, None, None, None, None, *weights, is_prompt=True)
        hs, kk, vv, cbs, cfs = hybrid_layer(hs, cache_k_win[l], cache_v_win[l], state_conv_b[l],
                                            state_conv_ffn[l], *weights, is_prompt=False)
        kp_l.append(kp); vp_l.append(vp); cbp_l.append(cbp); cfp_l.append(cfp)
        ks_l.append(kk); vs_l.append(vv); cbs_l.append(cbs); cfs_l.append(cfs)
    y_prompt = rmsnorm(hp, norm_f_g)
    y_sample = rmsnorm(hs, norm_f_g)
    return (y_prompt, y_sample,
            jnp.stack(kp_l), jnp.stack(vp_l), jnp.stack(cbp_l), jnp.stack(cfp_l),
            jnp.stack(ks_l), jnp.stack(vs_l), jnp.stack(cbs_l), jnp.stack(cfs_l))
```

```python
from contextlib import ExitStack
import os
import numpy as np
import concourse.bass as bass
import concourse.mybir as mybir
from concourse.bass_utils import run_bass_kernel_spmd

F32 = mybir.dt.float32
BF16 = mybir.dt.bfloat16
AF = mybir.ActivationFunctionType
ALU = mybir.AluOpType
EPOCH = 4000
NCORES = 8
EPS = 1e-6


class Buf:
    __slots__ = ("name", "last_w", "readers")

    def __init__(self, name):
        self.name = name
        self.last_w = None
        self.readers = []


class Actor:
    def __init__(self, name, handle=None, is_chan=False):
        self.name = name
        self.handle = handle
        self.is_chan = is_chan
        self.n = 0
        self.last_op = -1
        self.marked = set()
        self.sems = []
        self.rank = {}
        self.seen = {}


class Prog:
    def __init__(self, nc, ctx):
        self.nc = nc
        self.ctx = ctx
        self.ops = []
        self.limit = int(os.environ["KOPLIMIT"]) if os.environ.get("KOPLIMIT") else None
        self.actors = {}
        for nm, h in (("pe", nc.tensor), ("act", nc.scalar), ("dve", nc.vector),
                      ("pool", nc.gpsimd), ("sp", nc.sync)):
            self.actors[nm] = Actor(nm, h)

    def chan(self, name):
        if name not in self.actors:
            self.actors[name] = Actor(name, None, is_chan=True)
        return self.actors[name]

    def _deps(self, eng, reads, writes, skip_chan=None):
        deps = {}

        def add(a, idx):
            if deps.get(a.name, (None, -1))[1] < idx:
                deps[a.name] = (a, idx)
        for b in reads:
            if b.last_w is not None:
                add(*b.last_w)
        for b in writes:
            if b.last_w is not None:
                add(*b.last_w)
            for r in b.readers:
                add(*r)
        waits = []
        for nm, (a, idx) in deps.items():
            if a is eng and eng.name == "pe":
                continue
            if a is skip_chan:
                continue
            if eng.seen.get(nm, -1) >= idx:
                continue
            if a.is_chan and idx != a.n - 1:
                raise RuntimeError("unsafe wait on shared DMA channel %s (%d of %d)" % (nm, idx, a.n))
            eng.seen[nm] = idx
            a.marked.add(idx)
            waits.append((a, idx))
        return waits

    def op(self, engname, fn, reads=(), writes=()):
        if self.limit is not None and len(self.ops) >= self.limit:
            return
        eng = self.actors[engname]
        waits = self._deps(eng, reads, writes)
        idx = eng.n
        eng.n += 1
        eng.last_op = idx
        self.ops.append(("op", eng, idx, fn(), waits, None))
        for b in reads:
            b.readers.append((eng, idx))
        for b in writes:
            b.last_w = (eng, idx)
            b.readers = []

    def dma(self, qname, channame, fn, reads=(), writes=(), waw=False):
        if self.limit is not None and len(self.ops) >= self.limit:
            return
        q = self.actors[qname]
        ch = self.chan(channame)
        waits = self._deps(q, reads, writes, skip_chan=None if waw else ch)
        idx = ch.n
        ch.n += 1
        qidx = q.n
        q.n += 1
        self.ops.append(("dma", q, qidx, fn(), waits, (ch, idx)))
        for b in reads:
            b.readers.append((ch, idx))
        for b in writes:
            b.last_w = (ch, idx)
            b.readers = []

    def barrier(self, bufs=()):
        comp = [self.actors[n] for n in ("pe", "act", "dve", "pool")]
        snap = {a.name: a.last_op for a in comp if a.last_op >= 0}
        for E in comp + [self.actors["sp"]]:
            waits = []
            for a in comp:
                if a is E or a.name not in snap:
                    continue
                idx = snap[a.name]
                if E.seen.get(a.name, -1) >= idx:
                    continue
                E.seen[a.name] = idx
                a.marked.add(idx)
                waits.append((a, idx))
            for b in bufs:
                for (a, _) in b.readers:
                    if not a.is_chan:
                        continue
                    idx2 = a.n - 1
                    if E.seen.get(a.name, -1) >= idx2:
                        continue
                    E.seen[a.name] = idx2
                    a.marked.add(idx2)
                    waits.append((a, idx2))
            i = E.n
            E.n += 1
            self.ops.append(("nop", E, i, None, waits, None))

    def wait_all(self, engname, bufs):
        eng = self.actors[engname]
        waits = self._deps(eng, [], bufs)
        idx = eng.n
        eng.n += 1
        self.ops.append(("nop", eng, idx, None, waits, None))

    def check_no_deadlock(self):
        queues = {}
        for op in self.ops:
            queues.setdefault(op[1].name, []).append(op)
        pos = {k: 0 for k in queues}
        sem = {}
        progress = True
        while progress:
            progress = False
            for k, q in queues.items():
                while pos[k] < len(q):
                    kind, eng, idx, fn, waits, chinfo = q[pos[k]]
                    ok = True
                    for (a, widx) in waits:
                        ep, val = a.rank[widx]
                        if sem.get((a.name, ep), 0) < val:
                            ok = False
                            break
                    if not ok:
                        break
                    if kind == "dma":
                        key = (chinfo[0].name, 0)
                        sem[key] = sem.get(key, 0) + 16
                    elif kind == "op" and idx in eng.marked:
                        key = (eng.name, eng.rank[idx][0])
                        sem[key] = sem.get(key, 0) + 1
                    pos[k] += 1
                    progress = True
        for k, q in queues.items():
            if pos[k] < len(q):
                kind, eng, idx, fn, waits, chinfo = q[pos[k]]
                raise RuntimeError("DEADLOCK: %s stuck at op %d (%s) waits=%s" % (
                    k, idx, fn[:2] if fn else None, [(a.name, w, a.rank[w]) for a, w in waits]))

    def emit(self):
        nsem = 0
        for a in self.actors.values():
            m = sorted(a.marked)
            a.rank = {}
            if a.is_chan:
                a.sems = [self.ctx.enter_context(self.nc.semaphore("c_" + a.name))]
                nsem += 1
                for i in m:
                    a.rank[i] = (0, 16 * (i + 1))
            else:
                nep = max((len(m) + EPOCH - 1) // EPOCH, 1)
                a.sems = [self.ctx.enter_context(self.nc.semaphore("e_%s%d" % (a.name, e)))
                          for e in range(nep)]
                nsem += nep
                for r, i in enumerate(m):
                    a.rank[i] = (r // EPOCH, r % EPOCH + 1)
        self.nsem = nsem
        self.check_no_deadlock()
        for kind, eng, idx, fn, waits, chinfo in self.ops:
            for (a, widx) in waits:
                ep, val = a.rank[widx]
                eng.handle.wait_ge(a.sems[ep], val)
            if kind == "nop":
                continue
            ename, meth, a_, k_ = fn
            inst = getattr(getattr(self.nc, ename), meth)(*a_, **k_)
            if kind == "dma":
                assert idx not in eng.marked, "DMA queue op used as an engine milestone"
                inst.then_inc(chinfo[0].sems[0], 16)
            elif idx in eng.marked:
                inst.then_inc(eng.sems[eng.rank[idx][0]], 1)


class _EngProxy:
    def __init__(self, name):
        self._name = name

    def __getattr__(self, meth):
        name = self._name

        def call(*a, **k):
            return (name, meth, a, k)
        return call


class _NcProxy:
    def __init__(self):
        self.tensor = _EngProxy("tensor")
        self.scalar = _EngProxy("scalar")
        self.vector = _EngProxy("vector")
        self.gpsimd = _EngProxy("gpsimd")
        self.sync = _EngProxy("sync")


class Slots:
    def __init__(self, aps, name):
        self.items = [(ap, Buf("%s%d" % (name, i))) for i, ap in enumerate(aps)]
        self.i = 0

    def next(self):
        it = self.items[self.i % len(self.items)]
        self.i += 1
        return it


class Sub:
    def __init__(self, kind, idx, ntok, tok0, blk):
        self.kind = kind
        self.idx = idx
        self.ntok = ntok
        self.tok0 = tok0
        self.blk = blk
        self.nt = ntok // 128
        self.first = (kind == 'p' and idx == 0)
        self.last = (kind == 'p' and idx == 7)


def build_program():
    nc = bass.Bass("TRN2", target_bir_lowering=False)
    D = {}

    def din(name, shape):
        D[name] = nc.dram_tensor(name, list(shape), F32, kind="ExternalInput").ap()

    def dout(name, shape):
        D[name] = nc.dram_tensor(name, list(shape), F32, kind="ExternalOutput").ap()

    din("xp", (2048, 1024)); din("xs", (128, 1024))
    din("ck", (16, 128, 256)); din("cv", (16, 128, 256))
    din("scb", (16, 30, 1024)); din("scf", (16, 2, 5632))
    din("norm1_g", (1, 1024)); din("w_in", (1024, 5632)); din("sinks", (1, 16))
    din("w_o", (1024, 1024)); din("conv_b_w", (31, 1024)); din("conv_b_bias", (1, 1024))
    din("ln_b_g", (1, 1024)); din("ln_b_b", (1, 1024)); din("w_pb", (1024, 1024))
    din("w_out", (1024, 1024)); din("norm2_g", (1, 1024)); din("w_up", (1024, 5632))
    din("ffn_conv_w", (3, 5632)); din("ffn_conv_b", (1, 5632)); din("w_down", (2816, 1024))
    din("norm_f_g", (1, 1024))
    dout("yp", (2048, 1024)); dout("ys", (128, 1024))
    dout("kwp", (128, 256)); dout("vwp", (128, 256)); dout("cbp", (30, 1024)); dout("cfp", (2, 5632))
    dout("kws", (16, 128, 256)); dout("vws", (16, 128, 256))
    dout("cbs", (16, 30, 1024)); dout("cfs", (16, 2, 5632))

    w_in3 = D["w_in"].rearrange("(kc p) n -> p kc n", p=128)
    w_o3 = D["w_o"].rearrange("(kc p) n -> p kc n", p=128)
    w_pb3 = D["w_pb"].rearrange("(kc p) n -> p kc n", p=128)
    w_out3 = D["w_out"].rearrange("(kc p) n -> p kc n", p=128)
    w_up3 = D["w_up"].rearrange("(kc p) n -> p kc n", p=128)
    w_dn3 = D["w_down"].rearrange("(j p) n -> p j n", p=128)

    ctx = ExitStack()
    P = Prog(nc, ctx)

    def sb(name, shape, dt):
        return ctx.enter_context(nc.sbuf_tensor(name, list(shape), dt))

    ident_bf = sb("ident_bf", (128, 128), BF16); b_idbf = Buf("idbf")
    ident_f = sb("ident_f", (128, 128), F32); b_idf = Buf("idf")
    ones_bf = sb("ones_bf", (128, 128), BF16); b_ones = Buf("ones")
    MASK2 = sb("mask2", (128, 2, 2, 128), BF16); b_mask2 = Buf("mask2")
    MASKS4 = sb("masks4", (128, 4, 128), BF16); b_masks4 = Buf("masks4")
    MASKP = sb("maskp", (128, 8), BF16); b_maskp = Buf("maskp")
    esink = sb("esink", (128, 16), F32); b_esink = Buf("esink")
    gf_bc = sb("gf_bc", (128, 1024), F32); b_gf = Buf("gf")
    PARAM1 = sb("param1", (128, 8, 36), F32); b_par1 = Buf("par1")
    PARAM2 = sb("param2", (128, 44, 4), F32); b_par2 = Buf("par2")
    RS = sb("rstd_tab", (128, 3, 16), F32)
    RS2 = sb("rstd_tmp", (128, 16), F32); b_rs2 = Buf("rs2")
    b_rs = [Buf("rs0"), Buf("rs1"), Buf("rs2")]
    KC = sb("kcarry", (128, 4, 128), BF16); b_kc = Buf("kc")
    VC = sb("vcarry", (128, 4, 65), BF16); b_vc = Buf("vc")
    UC = sb("ucarry", (128, 8, 30), F32); b_uc = Buf("uc")
    CF = sb("fcarry", (128, 44, 2), F32); b_cf = Buf("cf")
    CFS = sb("fcarry_s", (128, 44, 16, 2), F32); b_cfs = Buf("cfs")
    XT = sb("XT", (128, 8, 1152), BF16)
    BIG = sb("BIG", (128, 18432), BF16)
    ARENA = sb("ARENA", (128, 2, 24576), BF16)
    NWCH = 12
    slotbufs = [[Buf("slot%d_%d" % (s_, c_)) for c_ in range(NWCH)] for s_ in range(2)]
    slotbuf = [slotbufs[0], slotbufs[1]]
    wl_count = [0, 0]
    WORKN = 20096
    WORK = sb("WORK", (128, WORKN), BF16)
    b_junk = Buf("junk")

    banks = [ctx.enter_context(nc.psum_tensor("bank%d" % i, [128, 512], F32)) for i in range(8)]
    pb = [Buf("bank%d" % i) for i in range(8)]

    def bank_f(i):
        return banks[i][:]

    def bank_b(i):
        return banks[i][:].bitcast(BF16)

    b_xt = [Buf("XT%d" % i) for i in range(9)]

    def xt_bufs(sub):
        t0 = sub.tok0 // 128
        return b_xt[t0:t0 + sub.nt]

    blk_index = {0: 0, 4096: 1, 8192: 2, 12288: 3, 16384: 4}
    b_A = [Buf("A%d" % i) for i in range(5)]
    b_B = [Buf("B%d" % i) for i in range(5)]
    b_H = [[Buf("H%d_%d" % (i, t)) for t in range(2)] for i in range(5)]

    def big_A(sub):
        n = sub.ntok
        return BIG[:, sub.blk:sub.blk + 8 * n].rearrange("p (c t) -> p c t", t=n)

    def big_B(sub):
        n = sub.ntok
        return BIG[:, sub.blk + 8 * n:sub.blk + 16 * n].rearrange("p (c t) -> p c t", t=n)

    def big_H(sub, ti):
        o = sub.blk + ti * 2048
        return BIG[:, o:o + 2048].bitcast(F32)

    def blkbufs(sub):
        i = blk_index[sub.blk]
        return b_A[i], b_B[i], b_H[i][:sub.nt]

    class Carver:
        def __init__(self):
            self.off = 0

        def take(self, nelem_bf16):
            o = self.off
            self.off += (nelem_bf16 + 15) // 16 * 16
            assert self.off <= WORKN, "WORK overflow %d" % self.off
            return WORK[:, o:o + nelem_bf16]

        def f32(self, n):
            return self.take(2 * n).bitcast(F32)

        def bf(self, n):
            return self.take(n)

    b_work = Buf("work_epoch")

    def x_src(sub, ti):
        if sub.kind == 'p':
            r0 = (sub.idx * 2 + ti) * 128
            return D["xp"][r0:r0 + 128, :]
        return D["xs"][:, :]

    def y_dst(sub, ti):
        if sub.kind == 'p':
            r0 = (sub.idx * 2 + ti) * 128
            return D["yp"][r0:r0 + 128, :]
        return D["ys"][:, :]

    SG0 = [Sub('p', i, 256, 256 * i, 4096 * i) for i in range(4)] + [Sub('s', 0, 128, 1024, 16384)]
    SG1 = [Sub('p', 4 + i, 256, 256 * i, 4096 * i) for i in range(4)]
    SGS = [SG0, SG1]

    def stage_fence():
        P.barrier(out_bufs)

    def slot_view(slot, off, kc, n):
        return ARENA[:, slot, off:off + kc * n].rearrange("p (k n) -> p k n", n=n)

    def wload(slot, dst, src):
        import os
        if os.environ.get("KPRINT"):
            print("WLOAD", slot, dst.shape, dst.offset, dst.ap, src.shape, src.offset, src.ap)
        if "w" in os.environ.get("KSKIP", "").split(","):
            return
        c_ = wl_count[slot] % NWCH
        wl_count[slot] += 1
        P.dma("pool", "wslot%d_%d" % (slot, c_), lambda: nc.gpsimd.dma_start(out=dst, in_=src),
              writes=[slotbufs[slot][c_]], waw=True)

    FG = [(0, 8), (8, 16), (16, 22)]

    def load_stage_weights(stage, slot):
        import os
        part = os.environ.get("KWPART", "")
        if stage == "S1" and part:
            if "q" in part:
                wload(slot, slot_view(slot, 0, 8, 1024), w_in3[:, :, 0:1024])
            if "k" in part:
                kd = ARENA[:, slot, 8192:12288].rearrange("p (k h e) -> p k h e", h=4, e=128)
                for kvh in range(4):
                    ksrc = w_in3[:, :, 1024 + kvh * 64:1088 + kvh * 64]
                    wload(slot, kd[:, :, kvh, 0:64], ksrc)
                    wload(slot, kd[:, :, kvh, 64:128], ksrc)
            if "v" in part:
                wload(slot, slot_view(slot, 12288, 8, 256), w_in3[:, :, 1280:1536])
            if "o" in part:
                wload(slot, slot_view(slot, 14336, 8, 1024), w_o3)
            return
        if stage == "S1":
            wload(slot, slot_view(slot, 0, 8, 1024), w_in3[:, :, 0:1024])
            kd = ARENA[:, slot, 8192:12288].rearrange("p (k h e) -> p k h e", h=4, e=128)
            for kvh in range(4):
                ksrc = w_in3[:, :, 1024 + kvh * 64:1088 + kvh * 64]
                wload(slot, kd[:, :, kvh, 0:64], ksrc)
                wload(slot, kd[:, :, kvh, 64:128], ksrc)
            wload(slot, slot_view(slot, 12288, 8, 256), w_in3[:, :, 1280:1536])
            wload(slot, slot_view(slot, 14336, 8, 1024), w_o3)
        elif stage == "S2":
            ks2 = os.environ.get("KS2", "abc")
            if "a" in ks2:
                wload(slot, slot_view(slot, 0, 8, 1024), w_in3[:, :, 1536:2560])
            if "A" in ks2:
                wload(slot, slot_view(0, 8192, 8, 1024), w_in3[:, :, 1536:2560])
            if "Z" in ks2:
                wload(slot, slot_view(1, 0, 8, 1024), w_in3[:, :, 0:1024])
            if "b" in ks2:
                wload(slot, slot_view(slot, 8192, 8, 1024), w_in3[:, :, 2560:3584])
            if "c" in ks2:
                wload(slot, slot_view(slot, 16384, 8, 1024), w_pb3)
        elif stage == "S3":
            wload(slot, slot_view(slot, 0, 8, 1024), w_in3[:, :, 3584:4608])
            wload(slot, slot_view(slot, 8192, 8, 1024), w_in3[:, :, 4608:5632])
            wload(slot, slot_view(slot, 16384, 8, 1024), w_out3)
        else:
            j0, j1 = FG[int(stage[1])]
            n = j1 - j0
            wload(slot, slot_view(slot, 0, 8, n * 128), w_up3[:, :, j0 * 128:j1 * 128])
            wload(slot, slot_view(slot, 8192, 8, n * 128), w_up3[:, :, 2816 + j0 * 128:2816 + j1 * 128])
            wload(slot, ARENA[:, slot, 16384:16384 + n * 1024].rearrange("p (j m) -> p j m", m=1024),
                  w_dn3[:, j0:j1, :])

    def init_consts():
        g = nc.gpsimd
        P.op("pool", lambda: g.memset(ident_bf[:], 1.0), writes=[b_idbf])
        P.op("pool", lambda: g.affine_select(ident_bf[:], ident_bf[:], [[-1, 128]], ALU.is_equal, 0.0,
                                             base=0, channel_multiplier=1), reads=[b_idbf], writes=[b_idbf])
        P.op("pool", lambda: g.memset(ident_f[:], 1.0), writes=[b_idf])
        P.op("pool", lambda: g.affine_select(ident_f[:], ident_f[:], [[-1, 128]], ALU.is_equal, 0.0,
                                             base=0, channel_multiplier=1), reads=[b_idf], writes=[b_idf])
        P.op("pool", lambda: g.memset(ones_bf[:], 1.0 / 1024.0), writes=[b_ones])
        P.op("pool", lambda: g.memset(MASK2[:], 1.0), writes=[b_mask2])
        for hh in range(2):
            P.op("pool", lambda hh=hh: g.affine_select(MASK2[:, hh, 0, :], MASK2[:, hh, 0, :], [[-1, 128]],
                                                       ALU.is_gt, 0.0, base=0, channel_multiplier=1),
                 reads=[b_mask2], writes=[b_mask2])
            P.op("pool", lambda hh=hh: g.affine_select(MASK2[:, hh, 1, :], MASK2[:, hh, 1, :], [[1, 128]],
                                                       ALU.is_ge, 0.0, base=0, channel_multiplier=-1),
                 reads=[b_mask2], writes=[b_mask2])
        P.op("pool", lambda: g.memset(MASKS4[:], 1.0), writes=[b_masks4])
        for r in range(4):
            v = MASKS4[:, r, :].rearrange("p (b i) -> p b i", i=8)
            P.op("pool", lambda v=v: g.affine_select(v, v, [[8, 16], [1, 8]], ALU.is_ge, 0.0,
                                                     base=0, channel_multiplier=-1),
                 reads=[b_masks4], writes=[b_masks4])
            P.op("pool", lambda v=v: g.affine_select(v, v, [[-8, 16], [0, 8]], ALU.is_ge, 0.0,
                                                     base=0, channel_multiplier=1),
                 reads=[b_masks4], writes=[b_masks4])
        P.op("pool", lambda: g.memset(MASKP[:], 1.0), writes=[b_maskp])
        P.op("pool", lambda: g.affine_select(MASKP[:], MASKP[:], [[-1, 8]], ALU.is_gt, 0.0,
                                             base=0, channel_multiplier=1), reads=[b_maskp], writes=[b_maskp])
        P.op("pool", lambda: g.memset(CF[:], 0.0), writes=[b_cf])
        P.dma("sp", "esink", lambda: nc.sync.dma_start(out=esink[:], in_=D["sinks"].to_broadcast([128, 16])),
              writes=[b_esink])
        P.op("act", lambda: nc.scalar.activation(out=esink[:], in_=esink[:], func=AF.Exp),
             reads=[b_esink], writes=[b_esink])
        P.dma("sp", "gf", lambda: nc.sync.dma_start(out=gf_bc[:], in_=D["norm_f_g"].to_broadcast([128, 1024])),
              writes=[b_gf])
        cv = Carver()
        prow = cv.f32(1024)
        prow2 = cv.f32(5632)
        b_prow = Buf("prow"); b_prow2 = Buf("prow2")
        rows = [("norm1_g", 0, 1), ("conv_b_w", 1, 31), ("conv_b_bias", 32, 1), ("ln_b_g", 33, 1),
                ("ln_b_b", 34, 1), ("norm2_g", 35, 1)]
        for nm, r0, nr in rows:
            P.dma("sp", "prow", lambda nm=nm, r0=r0, nr=nr: nc.sync.dma_start(out=prow[r0:r0 + nr, :], in_=D[nm]),
                  reads=[b_work], writes=[b_prow])
        P.dma("sp", "prow2", lambda: nc.sync.dma_start(out=prow2[0:3, :], in_=D["ffn_conv_w"]),
              reads=[b_work], writes=[b_prow2])
        P.dma("sp", "prow2", lambda: nc.sync.dma_start(out=prow2[3:4, :], in_=D["ffn_conv_b"]),
              reads=[b_work], writes=[b_prow2])
        pv = bank_f(0)[:, 0:288].rearrange("p (c r) -> p c r", r=36)
        for c in range(8):
            P.op("pe", lambda c=c: nc.tensor.transpose(pv[:, c, :], prow[0:36, c * 128:(c + 1) * 128],
                                                       ident_f[0:36, 0:36]),
                 reads=[b_prow, b_idf], writes=[pb[0]])
        P.op("dve", lambda: nc.vector.tensor_copy(out=PARAM1[:], in_=pv), reads=[pb[0]], writes=[b_par1])
        pv2 = bank_f(1)[:, 0:176].rearrange("p (c r) -> p c r", r=4)
        for c in range(44):
            P.op("pe", lambda c=c: nc.tensor.transpose(pv2[:, c, :], prow2[0:4, c * 128:(c + 1) * 128],
                                                       ident_f[0:4, 0:4]),
                 reads=[b_prow2, b_idf], writes=[pb[1]])
        P.op("dve", lambda: nc.vector.tensor_copy(out=PARAM2[:], in_=pv2), reads=[pb[1]], writes=[b_par2])

    b_d2d = Buf("d2d")
    out_bufs = [b_d2d]

    def issue_d2d():
        for src, dst, a, b_, n in (("ck", "kws", 8, 128, 120), ("cv", "vws", 8, 128, 120), ("scb", "cbs", 8, 30, 22)):
            P.dma("sp", "d2d", lambda src=src, dst=dst, a=a, b_=b_, n=n:
                  nc.sync.dma_start(out=D[dst][:, 0:n, :], in_=D[src][:, a:b_, :]), writes=[b_d2d])

    def stage_normT(sg, which):
        stage_fence()
        cv = Carver()
        xts = Slots([cv.f32(1024) for _ in range(2)], "xt")
        xss = Slots([cv.bf(1024) for _ in range(2)], "xs")
        junk = cv.bf(1024)
        grow = 0 if which == 0 else 35
        tiles = [(s, ti) for s in sg for ti in range(s.nt)]
        nt = len(tiles)

        def src_tile(s, ti):
            if which == 0:
                ap, bf = xts.next()
                P.dma("sp", bf.name, lambda ap=ap, s=s, ti=ti: nc.sync.dma_start(out=ap, in_=x_src(s, ti)),
                      reads=[b_work], writes=[bf])
                return ap, bf
            return big_H(s, ti), blkbufs(s)[2][ti]
        for n, (s, ti) in enumerate(tiles):
            ap, bf = src_tile(s, ti)
            P.op("act", lambda ap=ap, n=n: nc.scalar.activation(out=junk, in_=ap, func=AF.Square,
                                                                accum_out=RS[:, which, n:n + 1]),
                 reads=[bf, b_work], writes=[b_junk, b_rs[which]])
        P.op("act", lambda: nc.scalar.activation(out=RS2[:, 0:nt], in_=RS[:, which, 0:nt], func=AF.Sqrt,
                                                 bias=EPS, scale=1.0 / 1024.0),
             reads=[b_rs[which]], writes=[b_rs2])
        P.op("dve", lambda: nc.vector.reciprocal(out=RS[:, which, 0:nt], in_=RS2[:, 0:nt]),
             reads=[b_rs2], writes=[b_rs[which]])
        for n, (s, ti) in enumerate(tiles):
            ap, bf = src_tile(s, ti)
            xs_, bxs = xss.next()
            P.op("act", lambda ap=ap, xs_=xs_, n=n: nc.scalar.activation(
                out=xs_, in_=ap, func=AF.Copy, scale=RS[:, which, n:n + 1]),
                reads=[bf, b_rs[which], b_work], writes=[bxs])
            bk = n % 2
            pT = bank_b(bk).rearrange("p (c t) -> p c t", t=128)
            for c in range(8):
                P.op("pe", lambda c=c, pT=pT, xs_=xs_: nc.tensor.transpose(pT[:, c, :], xs_[:, c * 128:(c + 1) * 128],
                                                                           ident_bf[:]),
                     reads=[bxs, b_idbf], writes=[pb[bk]])
            t0 = s.tok0 + ti * 128
            gb = PARAM1[:, :, grow:grow + 1].to_broadcast([128, 8, 128])
            P.op("dve", lambda pT=pT, t0=t0, gb=gb: nc.vector.tensor_tensor(
                out=XT[:, :, t0:t0 + 128], in0=pT, in1=gb, op=ALU.mult),
                reads=[pb[bk], b_par1], writes=[b_xt[t0 // 128]])

    def stage_S1(sg, slot):
        stage_fence()
        Wq = slot_view(slot, 0, 8, 1024)
        WkD = ARENA[:, slot, 8192:12288].rearrange("p (k h e) -> p k h e", h=4, e=128)
        Wv = slot_view(slot, 12288, 8, 256)
        Wo = slot_view(slot, 14336, 8, 1024)
        bw = slotbuf[slot]
        cv = Carver()
        qT = cv.bf(8 * 256).rearrange("p (c t) -> p c t", t=256); b_qT = Buf("qT")
        kTd = cv.bf(4 * 256).rearrange("p (h t) -> p h t", t=256); b_kTd = Buf("kTd")
        Vaug = cv.bf(2 * 4 * 65).rearrange("p (s h e) -> p s h e", h=4, e=65); b_V = Buf("Vaug")
        pTs = Slots([cv.bf(512).rearrange("p (a b q) -> p a b q", a=2, b=2) for _ in range(4)], "pT")
        attn = Slots([cv.bf(1024) for _ in range(2)], "attn")
        attnT = cv.bf(8 * 256).rearrange("p (c t) -> p c t", t=256); b_attnT = Buf("attnT")
        den = cv.f32(16); b_den = Buf("den")
        rden = cv.f32(16); b_rden = Buf("rden")
        ktok = cv.f32(256); b_ktok = Buf("ktok")
        vtok = cv.f32(256); b_vtok = Buf("vtok")
        kpd = cv.bf(4 * 4 * 128).rearrange("p (r h e) -> p r h e", h=4, e=128); b_kpd = Buf("kpd")
        kpT = cv.bf(4 * 4 * 128).rearrange("p (r h k) -> p r h k", h=4, k=128); b_kpT = Buf("kpT")
        PTpad_raw = cv.bf(4 * 16 * 32)
        PTpad = PTpad_raw.rearrange("p (r h q) -> p r h q", h=16, q=32); b_ptpad = Buf("ptpad")
        Vpa = cv.bf(4 * 4 * 65).rearrange("p (r h e) -> p r h e", h=4, e=65); b_vpa = Buf("vpa")
        PTown_raw = cv.bf(16 * 128)
        PTown = PTown_raw.rearrange("p (h q) -> p h q", q=128); b_ptown = Buf("ptown")
        has_sample = any(s.kind == 's' for s in sg)
        P.op("pool", lambda: nc.gpsimd.memset(Vaug[:], 1.0), reads=[b_work], writes=[b_V])
        if has_sample:
            P.op("pool", lambda: nc.gpsimd.memset(Vpa[:], 1.0), reads=[b_work], writes=[b_vpa])
            P.op("pool", lambda: nc.gpsimd.memset(PTpad[:], 0.0), reads=[b_work], writes=[b_ptpad])
        MM, MS, MO, MT = (0, 1), (2, 3), (4, 5, 6), 7
        mmi = [0]

        def next_mm():
            b = MM[mmi[0] % 2]
            mmi[0] += 1
            return b
        si = [0]

        def normalize_and_transpose(s, ti, at, bat):
            for bi in range(3):
                h0 = bi * 7
                nh = min(7, 16 - h0)
                ov = bank_f(MO[bi])[:, 0:nh * 65].rearrange("p (h e) -> p h e", e=65)
                P.op("dve", lambda ov=ov, h0=h0, nh=nh: nc.vector.tensor_tensor(
                    out=den[:, h0:h0 + nh], in0=ov[:, :, 64], in1=esink[:, h0:h0 + nh], op=ALU.add),
                    reads=[pb[MO[bi]], b_esink, b_work], writes=[b_den])
            P.op("dve", lambda: nc.vector.reciprocal(out=rden, in_=den), reads=[b_den], writes=[b_rden])
            for bi in range(3):
                h0 = bi * 7
                nh = min(7, 16 - h0)
                ov = bank_f(MO[bi])[:, 0:nh * 65].rearrange("p (h e) -> p h e", e=65)
                av = at[:, h0 * 64:(h0 + nh) * 64].rearrange("p (h e) -> p h e", e=64)
                rb = rden[:, h0:h0 + nh].unsqueeze(2).to_broadcast([128, nh, 64])
                P.op("dve", lambda ov=ov, av=av, rb=rb: nc.vector.tensor_tensor(
                    out=av, in0=ov[:, :, 0:64], in1=rb, op=ALU.mult),
                    reads=[pb[MO[bi]], b_rden], writes=[bat])
            pT = bank_b(MT).rearrange("p (c t) -> p c t", t=128)
            for c in range(8):
                P.op("pe", lambda c=c, at=at, pT=pT: nc.tensor.transpose(pT[:, c, :], at[:, c * 128:(c + 1) * 128],
                                                                         ident_bf[:]),
                     reads=[bat, b_idbf], writes=[pb[MT]])
            P.op("act", lambda pT=pT, ti=ti: nc.scalar.copy(out=attnT[:, :, ti * 128:(ti + 1) * 128], in_=pT),
                 reads=[pb[MT]], writes=[b_attnT])

        for s in sg:
            n = s.ntok
            xb = xt_bufs(s)
            A_ap = big_A(s)
            bA, bB, bH = blkbufs(s)
            for half in range(4):
                bk = next_mm()
                for j in range(2):
                    oc = half * 2 + j
                    for kc in range(8):
                        P.op("pe", lambda oc=oc, kc=kc, bk=bk, j=j: nc.tensor.matmul(
                            bank_f(bk)[:, j * 256:j * 256 + n], lhsT=Wq[:, kc, oc * 128:(oc + 1) * 128],
                            rhs=XT[:, kc, s.tok0:s.tok0 + n], start=(kc == 0), stop=(kc == 7)),
                            reads=bw + xb, writes=[pb[bk]])
                src = bank_f(bk).rearrange("p (j t) -> p j t", t=256)[:, :, 0:n]
                P.op("act", lambda src=src, half=half: nc.scalar.copy(out=qT[:, half * 2:half * 2 + 2, 0:n], in_=src),
                     reads=[pb[bk], b_work], writes=[b_qT])
            for half in range(2):
                bk = next_mm()
                for j in range(2):
                    kvh = half * 2 + j
                    for kc in range(8):
                        P.op("pe", lambda kvh=kvh, kc=kc, bk=bk, j=j: nc.tensor.matmul(
                            bank_f(bk)[:, j * 256:j * 256 + n], lhsT=WkD[:, kc, kvh, :],
                            rhs=XT[:, kc, s.tok0:s.tok0 + n], start=(kc == 0), stop=(kc == 7)),
                            reads=bw + xb, writes=[pb[bk]])
                src = bank_f(bk).rearrange("p (j t) -> p j t", t=256)[:, :, 0:n]
                P.op("act", lambda src=src, half=half: nc.scalar.copy(out=kTd[:, half * 2:half * 2 + 2, 0:n], in_=src),
                     reads=[pb[bk], b_work], writes=[b_kTd])
            bk = next_mm()
            for ti in range(s.nt):
                for kc in range(8):
                    P.op("pe", lambda ti=ti, kc=kc, bk=bk: nc.tensor.matmul(
                        bank_f(bk)[:, ti * 256:(ti + 1) * 256],
                        lhsT=XT[:, kc, s.tok0 + ti * 128:s.tok0 + (ti + 1) * 128], rhs=Wv[:, kc, :],
                        start=(kc == 0), stop=(kc == 7)),
                        reads=bw + xb, writes=[pb[bk]])
            vsrc = bank_f(bk)[:, 0:s.nt * 256].rearrange("p (s h e) -> p s h e", h=4, e=64)
            P.op("act", lambda vsrc=vsrc: nc.scalar.copy(out=Vaug[:, 0:s.nt, :, 0:64], in_=vsrc),
                 reads=[pb[bk], b_work], writes=[b_V])
            if os.environ.get("KS1STOP", "") == "proj":
                return
            want_out = s.last or s.kind == 's'
            if want_out:
                ti_o = s.nt - 1
                P.op("dve", lambda ti_o=ti_o, bk=bk: nc.vector.tensor_copy(
                    out=vtok, in_=bank_f(bk)[:, ti_o * 256:(ti_o + 1) * 256]),
                    reads=[pb[bk], b_work, b_V], writes=[b_vtok])
                bk2 = next_mm()
                for kvh in range(4):
                    for kc in range(8):
                        P.op("pe", lambda kc=kc, kvh=kvh, bk2=bk2, ti_o=ti_o: nc.tensor.matmul(
                            bank_f(bk2)[:, kvh * 64:(kvh + 1) * 64],
                            lhsT=XT[:, kc, s.tok0 + ti_o * 128:s.tok0 + (ti_o + 1) * 128],
                            rhs=WkD[:, kc, kvh, 0:64], start=(kc == 0), stop=(kc == 7)),
                            reads=bw + xb, writes=[pb[bk2]])
                P.op("dve", lambda bk2=bk2: nc.vector.tensor_copy(out=ktok, in_=bank_f(bk2)[:, 0:256]),
                     reads=[pb[bk2], b_work], writes=[b_ktok])
                if s.kind == 'p':
                    P.dma("sp", "ktok", lambda: nc.sync.dma_start(out=D["kwp"], in_=ktok), reads=[b_ktok])
                    P.dma("sp", "vtok", lambda: nc.sync.dma_start(out=D["vwp"], in_=vtok), reads=[b_vtok])
                else:
                    for b in range(16):
                        P.dma("sp", "ktok", lambda b=b: nc.sync.dma_start(out=D["kws"][b, 120:128, :],
                                                                          in_=ktok[b * 8:(b + 1) * 8, :]), reads=[b_ktok])
                        P.dma("sp", "vtok", lambda b=b: nc.sync.dma_start(out=D["vws"][b, 120:128, :],
                                                                          in_=vtok[b * 8:(b + 1) * 8, :]), reads=[b_vtok])
                out_bufs.extend([b_ktok, b_vtok])
            if os.environ.get("KS1STOP", "") == "prompt" and s.kind == 's':
                return
            if s.kind == 'p':
                for ti in range(s.nt):
                    noprev = s.first and ti == 0
                    at, bat = attn.next()

                    def scores(g, ti=ti, noprev=noprev):
                        kvh = g
                        for hh in range(2):
                            bS = MS[hh]
                            Sv = bank_f(bS).rearrange("p (j b q) -> p j b q", j=2, b=2)
                            ps_ = slice(hh * 64, hh * 64 + 64)
                            for j in range(2):
                                hp = 2 * g + j
                                rhs = qT[ps_, hp, ti * 128:(ti + 1) * 128]
                                if not noprev:
                                    if ti == 0:
                                        lk, rb_ = KC[ps_, kvh, :], [b_kc]
                                    else:
                                        lk, rb_ = kTd[ps_, kvh, 0:128], [b_kTd]
                                    P.op("pe", lambda Sv=Sv, j=j, lk=lk, rhs=rhs: nc.tensor.matmul(
                                        Sv[:, j, 0, :], lhsT=lk, rhs=rhs, start=True, stop=True),
                                        reads=rb_ + [b_qT], writes=[pb[bS]])
                                P.op("pe", lambda Sv=Sv, j=j, ps_=ps_, kvh=kvh, rhs=rhs: nc.tensor.matmul(
                                    Sv[:, j, 1, :], lhsT=kTd[ps_, kvh, ti * 128:(ti + 1) * 128], rhs=rhs,
                                    start=True, stop=True),
                                    reads=[b_kTd, b_qT], writes=[pb[bS]])

                    def softmax_num(g, noprev=noprev):
                        res = []
                        for hh in range(2):
                            bS = MS[hh]
                            Sv = bank_f(bS).rearrange("p (j b q) -> p j b q", j=2, b=2)
                            pT, bpT = pTs.next()
                            if noprev:
                                P.op("act", lambda pT=pT, Sv=Sv: nc.scalar.activation(
                                    out=pT[:, :, 1, :], in_=Sv[:, :, 1, :], func=AF.Exp, scale=0.125),
                                    reads=[pb[bS], b_work], writes=[bpT])
                                P.op("dve", lambda pT=pT: nc.vector.tensor_tensor(
                                    out=pT[:, :, 1, :], in0=pT[:, :, 1, :], in1=MASK2[:, :, 1, :], op=ALU.mult),
                                    reads=[bpT, b_mask2], writes=[bpT])
                            else:
                                P.op("act", lambda pT=pT, Sv=Sv: nc.scalar.activation(
                                    out=pT, in_=Sv, func=AF.Exp, scale=0.125),
                                    reads=[pb[bS], b_work], writes=[bpT])
                                P.op("dve", lambda pT=pT: nc.vector.tensor_tensor(
                                    out=pT, in0=pT, in1=MASK2[:], op=ALU.mult),
                                    reads=[bpT, b_mask2], writes=[bpT])
                            res.append((pT, bpT))
                        return res

                    def pv(g, pts, ti=ti, noprev=noprev):
                        kvh = g
                        for hh in range(2):
                            pT, bpT = pts[hh]
                            for j in range(2):
                                h = 2 * (2 * g + j) + hh
                                bo = MO[h // 7]
                                oo = bank_f(bo)[:, (h % 7) * 65:(h % 7) * 65 + 65]
                                if not noprev:
                                    if ti == 0:
                                        vp, rb_ = VC[:, kvh, :], [b_vc]
                                    else:
                                        vp, rb_ = Vaug[:, 0, kvh, :], [b_V]
                                    P.op("pe", lambda oo=oo, pT=pT, j=j, vp=vp: nc.tensor.matmul(
                                        oo, lhsT=pT[:, j, 0, :], rhs=vp, start=True, stop=False),
                                        reads=[bpT] + rb_, writes=[pb[bo]])
                                P.op("pe", lambda oo=oo, pT=pT, j=j, kvh=kvh: nc.tensor.matmul(
                                    oo, lhsT=pT[:, j, 1, :], rhs=Vaug[:, ti, kvh, :], start=noprev, stop=True),
                                    reads=[bpT, b_V], writes=[pb[bo]])
                    scores(0)
                    for g in range(4):
                        pts = softmax_num(g)
                        if g + 1 < 4:
                            scores(g + 1)
                        pv(g, pts)
                    normalize_and_transpose(s, ti, at, bat)
                P.op("pool", lambda: nc.gpsimd.tensor_copy(out=KC[:], in_=kTd[:, :, 128:256]),
                     reads=[b_kTd], writes=[b_kc])
                P.op("act", lambda: nc.scalar.copy(out=VC[:].rearrange("p h e -> p (h e)"),
                                                   in_=Vaug[:, 1, :, :].rearrange("p h e -> p (h e)")),
                     reads=[b_V], writes=[b_vc])
            else:
                PTv = PTown_raw.rearrange("p (hp hh q) -> p hp hh q", hh=2, q=128)
                for r8 in range(2):
                    for hh in range(2):
                        bS = MS[hh]
                        Sv = bank_f(bS).rearrange("p (j q) -> p j q", q=128)
                        ps_ = slice(hh * 64, hh * 64 + 64)
                        for j in range(4):
                            hp = 4 * r8 + j
                            kvh = hp // 2
                            P.op("pe", lambda Sv=Sv, j=j, ps_=ps_, kvh=kvh, hp=hp: nc.tensor.matmul(
                                Sv[:, j, :], lhsT=kTd[ps_, kvh, 0:128], rhs=qT[ps_, hp, 0:128], start=True, stop=True),
                                reads=[b_kTd, b_qT], writes=[pb[bS]])
                        dsto = PTv[:, 4 * r8:4 * r8 + 4, hh, :]
                        P.op("act", lambda Sv=Sv, dsto=dsto: nc.scalar.activation(
                            out=dsto, in_=Sv, func=AF.Exp, scale=0.125),
                            reads=[pb[bS], b_work], writes=[b_ptown])
                        P.op("pool", lambda dsto=dsto: nc.gpsimd.tensor_tensor(
                            out=dsto, in0=dsto, in1=MASKS4[:], op=ALU.mult),
                            reads=[b_ptown, b_masks4], writes=[b_ptown])
                at, bat = attn.next()
                PTpv = PTpad_raw.rearrange("p (r hp hh q) -> p r hp hh q", hp=8, hh=2, q=32)
                for bb in range(4):
                    for r in range(4):
                        ksrc = D["ck"][4 * bb + r].rearrange("k (h d) -> k h d", d=64)
                        vsrc2 = D["cv"][4 * bb + r].rearrange("k (h d) -> k h d", d=64)
                        P.dma("pool", "kpd", lambda ksrc=ksrc, r=r: nc.gpsimd.dma_start(out=kpd[:, r, :, 0:64], in_=ksrc),
                              reads=[b_work], writes=[b_kpd])
                        P.dma("pool", "kpd", lambda ksrc=ksrc, r=r: nc.gpsimd.dma_start(out=kpd[:, r, :, 64:128], in_=ksrc),
                              reads=[b_work], writes=[b_kpd])
                        P.dma("pool", "vpa", lambda vsrc2=vsrc2, r=r: nc.gpsimd.dma_start(out=Vpa[:, r, :, 0:64], in_=vsrc2),
                              reads=[b_work], writes=[b_vpa])
                    for half in range(2):
                        pT = bank_b(MT).rearrange("p (c t) -> p c t", t=128)
                        for j in range(8):
                            r, kvh = (half * 8 + j) // 4, (half * 8 + j) % 4
                            P.op("pe", lambda pT=pT, j=j, r=r, kvh=kvh: nc.tensor.transpose(
                                pT[:, j, :], kpd[:, r, kvh, :], ident_bf[:]),
                                reads=[b_kpd, b_idbf], writes=[pb[MT]])
                        dst = kpT[:, half * 2:half * 2 + 2, :, :].rearrange("p r h k -> p (r h) k")
                        P.op("act", lambda pT=pT, dst=dst: nc.scalar.copy(out=dst, in_=pT),
                             reads=[pb[MT]], writes=[b_kpT])
                    for hh in range(2):
                        bS = MS[hh]
                        Sv = bank_f(bS)[:, 0:256].rearrange("p (r hp i) -> p r hp i", r=4, i=8)
                        ps_ = slice(hh * 64, hh * 64 + 64)
                        for r in range(4):
                            b = 4 * bb + r
                            for hp in range(8):
                                kvh = hp // 2
                                P.op("pe", lambda Sv=Sv, r=r, hp=hp, ps_=ps_, kvh=kvh, b=b: nc.tensor.matmul(
                                    Sv[:, r, hp, :], lhsT=kpT[ps_, r, kvh, :], rhs=qT[ps_, hp, b * 8:(b + 1) * 8],
                                    start=True, stop=True),
                                    reads=[b_kpT, b_qT], writes=[pb[bS]])
                        for r in range(4):
                            dstp = PTpv[:, r, :, hh, r * 8:(r + 1) * 8]
                            P.op("act", lambda Sv=Sv, r=r, dstp=dstp: nc.scalar.activation(
                                out=dstp, in_=Sv[:, r, :, :], func=AF.Exp, scale=0.125),
                                reads=[pb[bS]], writes=[b_ptpad])
                    for r in range(4):
                        dstp = PTpad[:, r, :, r * 8:(r + 1) * 8]
                        mb = MASKP[:].unsqueeze(1).to_broadcast([128, 16, 8])
                        P.op("pool", lambda dstp=dstp, mb=mb: nc.gpsimd.tensor_tensor(
                            out=dstp, in0=dstp, in1=mb, op=ALU.mult),
                            reads=[b_ptpad, b_maskp], writes=[b_ptpad])
                    for h in range(16):
                        kvh = h // 4
                        bo = MO[h // 7]
                        oo = bank_f(bo)[32 * bb:32 * bb + 32, (h % 7) * 65:(h % 7) * 65 + 65]
                        for r in range(4):
                            P.op("pe", lambda oo=oo, r=r, h=h, kvh=kvh, bb=bb: nc.tensor.matmul(
                                oo, lhsT=PTpad[:, r, h, :], rhs=Vpa[:, r, kvh, :], start=(r == 0), stop=False,
                                tile_position=(0, 32 * bb)),
                                reads=[b_ptpad, b_vpa], writes=[pb[bo]])
                        P.op("pe", lambda oo=oo, h=h, kvh=kvh, bb=bb: nc.tensor.matmul(
                            oo, lhsT=PTown[:, h, 32 * bb:32 * bb + 32], rhs=Vaug[:, 0, kvh, :], start=False, stop=True,
                            tile_position=(0, 32 * bb)),
                            reads=[b_ptown, b_V], writes=[pb[bo]])
                normalize_and_transpose(s, 0, at, bat)
            if os.environ.get("KS1STOP", "") == "attn":
                return
            for half in range(4):
                bk = next_mm()
                for j in range(2):
                    oc = half * 2 + j
                    for kc in range(8):
                        P.op("pe", lambda oc=oc, kc=kc, bk=bk, j=j: nc.tensor.matmul(
                            bank_f(bk)[:, j * 256:j * 256 + n], lhsT=Wo[:, kc, oc * 128:(oc + 1) * 128],
                            rhs=attnT[:, kc, 0:n], start=(kc == 0), stop=(kc == 7)),
                            reads=bw + [b_attnT], writes=[pb[bk]])
                src = bank_f(bk).rearrange("p (j t) -> p j t", t=256)[:, :, 0:n]
                P.op("dve", lambda src=src, half=half, A_ap=A_ap: nc.vector.tensor_copy(
                    out=A_ap[:, half * 2:half * 2 + 2, :], in_=src),
                    reads=[pb[bk]], writes=[bA] + bH)

    def stage_S2(sg, slot):
        stage_fence()
        Wga = slot_view(slot, 0, 8, 1024)
        Wgb = slot_view(slot, 8192, 8, 1024)
        Wpb = slot_view(slot, 16384, 8, 1024)
        bw = slotbuf[slot]
        state = {"kind": None}

        def alloc(kind):
            cv = Carver()
            nn = 256 if kind == 'p' else 128
            d = {}
            d["UEXTraw"] = cv.f32(8 * 286 if kind == 'p' else 8 * 16 * 38)
            d["ACC"] = cv.f32(8 * nn)
            d["AB16"] = cv.bf(8 * nn)
            d["zT"] = cv.bf(8 * nn)
            d["sgs"] = Slots([cv.f32(nn) for _ in range(2)], "sgt")
            d["mean"] = cv.f32(nn)
            d["m2"] = cv.f32(nn)
            d["rstd"] = cv.f32(nn)
            d["utok"] = cv.f32(1024)
            if kind == 'p':
                d["pacc"] = Slots([cv.f32(256) for _ in range(2)], "pacc")
                d["diag"] = Slots([cv.f32(128) for _ in range(4)], "diag")
            if kind == 's':
                d["stg"] = Slots([cv.f32(1024)], "scbstg")
            d["bufs"] = {k: Buf(k + kind) for k in ("uext", "acc", "ab16", "zT", "mean", "m2", "rstdln", "utok")}
            return d
        NPOOL = 0
        NPE = 10
        MG, MST, MP = (0, 1, 2), 3, (4, 5)
        b_ca = [Buf("convacc%d" % i) for i in range(4)]
        gi = [0]
        pi = [0]
        for s in sg:
            n = s.ntok
            xb = xt_bufs(s)
            bA, bB, bH = blkbufs(s)
            B_ap = big_B(s)
            if state["kind"] != s.kind:
                if state["kind"] is not None:
                    P.barrier(out_bufs)
                state["kind"] = s.kind
                state["d"] = alloc(s.kind)
            d = state["d"]
            UEXTraw, ACC, AB16, zT, sgs = d["UEXTraw"], d["ACC"], d["AB16"], d["zT"], d["sgs"]
            mean, m2, rstd, utok = d["mean"], d["m2"], d["rstd"], d["utok"]
            stg = d.get("stg")
            bb_ = d["bufs"]
            b_u, b_acc, b_ab16, b_zT = bb_["uext"], bb_["acc"], bb_["ab16"], bb_["zT"]
            b_mean, b_m2, b_rstd, b_utok = bb_["mean"], bb_["m2"], bb_["rstdln"], bb_["utok"]
            if s.kind == 'p':
                UE = UEXTraw[:, 0:8 * 286].rearrange("p (c t) -> p c t", t=286)

                def ucur(oc, UE=UE):
                    return UE[:, oc, 30:286]

                def utap(oc, k, UE=UE):
                    return UE[:, oc, k:k + 256]

                def accv(oc):
                    return ACC[:, oc * 256:(oc + 1) * 256]
                if s.first:
                    P.op("pool", lambda UE=UE: nc.gpsimd.memset(UE[:, :, 0:30], 0.0), reads=[b_work], writes=[b_u])
                else:
                    P.op("pool", lambda UE=UE: nc.gpsimd.tensor_copy(out=UE[:, :, 0:30], in_=UC[:]),
                         reads=[b_uc, b_work], writes=[b_u])
            else:
                UE = UEXTraw[:, 0:8 * 16 * 38].rearrange("p (c b t) -> p c b t", b=16, t=38)

                def ucur(oc, UE=UE):
                    return UE[:, oc, :, 30:38]

                def utap(oc, k, UE=UE):
                    return UE[:, oc, :, k:k + 8]

                def accv(oc):
                    return ACC[:, oc * 128:(oc + 1) * 128].rearrange("p (b i) -> p b i", i=8)
                for c4 in range(4):
                    st_, bst = stg.next()
                    src = D["scb"][4 * c4:4 * c4 + 4].rearrange("b r f -> (b r) f")
                    P.dma("sp", bst.name, lambda st_=st_, src=src: nc.sync.dma_start(out=st_[0:120, :], in_=src),
                          reads=[b_work], writes=[bst])
                    for half in range(2):
                        bk = MP[pi[0] % 2]
                        pi[0] += 1
                        pv = bank_f(bk)[:, 0:480].rearrange("p (c t) -> p c t", t=120)
                        for j in range(4):
                            oc = half * 4 + j
                            P.op("pe", lambda pv=pv, j=j, oc=oc, st_=st_: nc.tensor.transpose(
                                pv[:, j, :], st_[0:120, oc * 128:(oc + 1) * 128], ident_f[0:120, 0:120]),
                                reads=[bst, b_idf], writes=[pb[bk]])
                        dst = UE[:, half * 4:half * 4 + 4, 4 * c4:4 * c4 + 4, 0:30]
                        srcv = bank_f(bk)[:, 0:480].rearrange("p (c b t) -> p c b t", b=4, t=30)
                        P.op("act", lambda dst=dst, srcv=srcv: nc.scalar.copy(out=dst, in_=srcv),
                             reads=[pb[bk], b_work], writes=[b_u])
            acc_all = ACC[:, 0:8 * n]
            for oc in range(8):
                bk = MG[gi[0] % 3]
                gi[0] += 1
                for j, Wx in enumerate((Wga, Wgb)):
                    for kc in range(8):
                        P.op("pe", lambda oc=oc, kc=kc, bk=bk, j=j, Wx=Wx: nc.tensor.matmul(
                            bank_f(bk)[:, j * 256:j * 256 + n], lhsT=Wx[:, kc, oc * 128:(oc + 1) * 128],
                            rhs=XT[:, kc, s.tok0:s.tok0 + n], start=(kc == 0), stop=(kc == 7)),
                            reads=bw + xb, writes=[pb[bk]])
                sg_, bsg = sgs.next()
                P.op("act", lambda sg_=sg_, bk=bk: nc.scalar.activation(
                    out=sg_[:, 0:n], in_=bank_f(bk)[:, 256:256 + n], func=AF.Sigmoid),
                    reads=[pb[bk], b_work], writes=[bsg])
                if s.kind == 'p':
                    ga_v, sg_v = bank_f(bk)[:, 0:n], sg_[:, 0:n]
                else:
                    ga_v = bank_f(bk)[:, 0:n].rearrange("p (b i) -> p b i", i=8)
                    sg_v = sg_[:, 0:n].rearrange("p (b i) -> p b i", i=8)
                if s.kind == 'p':
                    P.op("dve", lambda oc=oc, ga_v=ga_v, sg_v=sg_v, ucur=ucur: nc.vector.tensor_tensor(
                        out=ucur(oc), in0=ga_v, in1=sg_v, op=ALU.mult),
                        reads=[pb[bk], bsg], writes=[b_u])
                else:
                    P.op("dve", lambda oc=oc, ga_v=ga_v, sg_v=sg_v: nc.vector.tensor_tensor(
                        out=ACC[:, oc * 128:(oc + 1) * 128].rearrange("p (b i) -> p b i", i=8), in0=ga_v, in1=sg_v,
                        op=ALU.mult),
                        reads=[pb[bk], bsg], writes=[b_acc])
            if s.kind == 's':
                P.op("pool", lambda UE=UE: nc.gpsimd.tensor_copy(
                    out=UE[:, :, :, 30:38], in_=ACC[:, 0:1024].rearrange("p (c b i) -> p c b i", c=8, i=8)),
                    reads=[b_acc], writes=[b_u])
            if s.last or s.kind == 's':
                for half in range(2):
                    bk = MP[pi[0] % 2]
                    pi[0] += 1
                    pv = bank_f(bk).rearrange("p (c t) -> p c t", t=128)
                    for j in range(4):
                        oc = half * 4 + j
                        src = UE[:, oc, 158:286] if s.kind == 'p' else ACC[:, oc * 128:(oc + 1) * 128]
                        P.op("pe", lambda pv=pv, j=j, src=src: nc.tensor.transpose(pv[:, j, :], src, ident_f[:]),
                             reads=[b_u, b_acc, b_idf], writes=[pb[bk]])
                    P.op("act", lambda half=half, bk=bk: nc.scalar.copy(
                        out=utok[:, half * 512:(half + 1) * 512], in_=bank_f(bk)),
                        reads=[pb[bk], b_work], writes=[b_utok])
                if s.kind == 'p':
                    P.dma("sp", "utok", lambda: nc.sync.dma_start(out=D["cbp"], in_=utok[98:128, :]), reads=[b_utok])
                else:
                    for b in range(16):
                        P.dma("sp", "utok", lambda b=b: nc.sync.dma_start(
                            out=D["cbs"][b, 22:30, :], in_=utok[b * 8:(b + 1) * 8, :]), reads=[b_utok])
                out_bufs.append(b_utok)
            for ocp in range(4):
                accs = []
                for oc in (2 * ocp, 2 * ocp + 1):
                    cab = 6 + oc % 2
                    if s.kind == 'p':
                        accP = bank_f(cab)[:, 0:256]
                    else:
                        accP = bank_f(cab)[:, 0:128].rearrange("p (b i) -> p b i", i=8)
                    accs.append((oc, accP, pb[cab]))
                    P.op("act", lambda oc=oc, accP=accP, ucur=ucur: nc.scalar.activation(
                        out=accP, in_=ucur(oc), func=AF.Identity,
                        scale=PARAM1[:, oc, 31:32], bias=PARAM1[:, oc, 32:33]),
                        reads=[b_u, b_par1, b_work], writes=[pb[cab]])
                npool = NPOOL if s.kind == 'p' else 0
                paccs = []
                for oc, accP, bca in accs:
                    if npool:
                        pa, bpa = d["pacc"].next()
                        pt_, bpt = d["ptmp"].next()
                        for k in range(npool):
                            if k == 0:
                                P.op("pool", lambda oc=oc, k=k, pa=pa, utap=utap: nc.gpsimd.tensor_scalar(
                                    out=pa, in0=utap(oc, k), scalar1=PARAM1[:, oc, 1 + k:2 + k], scalar2=None,
                                    op0=ALU.mult), reads=[b_u, b_par1, b_work], writes=[bpa])
                            else:
                                P.op("pool", lambda oc=oc, k=k, pt_=pt_, utap=utap: nc.gpsimd.tensor_scalar(
                                    out=pt_, in0=utap(oc, k), scalar1=PARAM1[:, oc, 1 + k:2 + k], scalar2=None,
                                    op0=ALU.mult), reads=[b_u, b_par1, b_work], writes=[bpt])
                                P.op("pool", lambda pa=pa, pt_=pt_: nc.gpsimd.tensor_tensor(
                                    out=pa, in0=pa, in1=pt_, op=ALU.add), reads=[bpa, bpt], writes=[bpa])
                        paccs.append((pa, bpa))
                npe = NPE if s.kind == 'p' else 0
                pes = []
                for oc, accP, bca in accs:
                    if not npe:
                        break
                    pbk = MP[oc % 2]
                    pep = bank_f(pbk)[:, 0:256]
                    for k in range(npe):
                        dg, bdg = d["diag"].next()
                        P.op("act", lambda dg=dg, oc=oc, k=k: nc.scalar.activation(
                            out=dg, in_=ident_f[:], func=AF.Copy, scale=PARAM1[:, oc, 1 + k:2 + k]),
                            reads=[b_idf, b_par1, b_work], writes=[bdg])
                        P.op("pe", lambda dg=dg, oc=oc, k=k, pep=pep, utap=utap: nc.tensor.matmul(
                            pep, lhsT=dg, rhs=utap(oc, k), start=(k == 0), stop=(k == npe - 1)),
                            reads=[bdg, b_u], writes=[pb[pbk]])
                    pes.append((pep, pbk))
                for k in range(max(npool, npe), 30):
                    for oc, accP, bca in accs:
                        P.op("dve", lambda oc=oc, k=k, accP=accP, utap=utap: nc.vector.scalar_tensor_tensor(
                            out=accP, in0=utap(oc, k), scalar=PARAM1[:, oc, 1 + k:2 + k], in1=accP,
                            op0=ALU.mult, op1=ALU.add),
                            reads=[b_u, bca, b_par1], writes=[bca])
                if npe:
                    for (oc, accP, bca), (pep, pbk) in zip(accs, pes):
                        pa, bpa = d["pacc"].next()
                        P.op("act", lambda pa=pa, pep=pep: nc.scalar.copy(out=pa, in_=pep),
                             reads=[pb[pbk], b_work], writes=[bpa])
                        P.op("dve", lambda accP=accP, pa=pa: nc.vector.tensor_tensor(
                            out=accP, in0=pa, in1=accP, op=ALU.add), reads=[bpa, bca], writes=[bca])
                if npool:
                    for (oc, accP, bca), (pa, bpa) in zip(accs, paccs):
                        P.op("dve", lambda accP=accP, pa=pa: nc.vector.tensor_tensor(
                            out=accP, in0=pa, in1=accP, op=ALU.add), reads=[bpa, bca], writes=[bca])
                for oc, accP, bca in accs:
                    P.op("act", lambda oc=oc, accP=accP, accv=accv: nc.scalar.copy(out=accv(oc), in_=accP),
                         reads=[bca, b_work], writes=[b_acc])
                    ab_v = AB16[:, oc * n:(oc + 1) * n]
                    if s.kind == 's':
                        ab_v = ab_v.rearrange("p (b i) -> p b i", i=8)
                    P.op("act", lambda accP=accP, ab_v=ab_v: nc.scalar.copy(out=ab_v, in_=accP),
                         reads=[bca, b_work], writes=[b_ab16])
            if s.kind == 'p':
                P.op("pool", lambda UE=UE: nc.gpsimd.tensor_copy(out=UC[:], in_=UE[:, :, 256:286]),
                     reads=[b_u], writes=[b_uc])
            for oc in range(8):
                P.op("pe", lambda oc=oc: nc.tensor.matmul(
                    bank_f(MST)[:, 0:n], lhsT=ones_bf[:], rhs=AB16[:, oc * n:(oc + 1) * n],
                    start=(oc == 0), stop=(oc == 7)),
                    reads=[b_ab16, b_ones], writes=[pb[MST]])
            P.op("act", lambda: nc.scalar.copy(out=mean[:, 0:n], in_=bank_f(MST)[:, 0:n]),
                 reads=[pb[MST], b_work], writes=[b_mean])
            P.op("act", lambda acc_all=acc_all: nc.scalar.activation(out=AB16[:, 0:8 * n], in_=acc_all, func=AF.Square),
                 reads=[b_acc], writes=[b_ab16])
            for oc in range(8):
                P.op("pe", lambda oc=oc: nc.tensor.matmul(
                    bank_f(MST)[:, 256:256 + n], lhsT=ones_bf[:], rhs=AB16[:, oc * n:(oc + 1) * n],
                    start=(oc == 0), stop=(oc == 7)),
                    reads=[b_ab16, b_ones], writes=[pb[MST]])
            P.op("dve", lambda: nc.vector.tensor_tensor(out=m2[:, 0:n], in0=mean[:, 0:n], in1=mean[:, 0:n], op=ALU.mult),
                 reads=[b_mean, b_work], writes=[b_m2])
            P.op("dve", lambda: nc.vector.tensor_tensor(out=m2[:, 0:n], in0=bank_f(MST)[:, 256:256 + n], in1=m2[:, 0:n],
                                                        op=ALU.subtract),
                 reads=[pb[MST], b_m2], writes=[b_m2])
            P.op("act", lambda: nc.scalar.activation(out=m2[:, 0:n], in_=m2[:, 0:n], func=AF.Sqrt, bias=EPS, scale=1.0),
                 reads=[b_m2], writes=[b_m2])
            P.op("dve", lambda: nc.vector.reciprocal(out=rstd[:, 0:n], in_=m2[:, 0:n]),
                 reads=[b_m2, b_work], writes=[b_rstd])
            a3 = acc_all.rearrange("p (c t) -> p c t", t=n)
            P.op("dve", lambda a3=a3: nc.vector.tensor_tensor(
                out=a3, in0=a3, in1=mean[:, 0:n].unsqueeze(1).to_broadcast([128, 8, n]), op=ALU.subtract),
                reads=[b_acc, b_mean], writes=[b_acc])
            P.op("dve", lambda a3=a3: nc.vector.tensor_tensor(
                out=a3, in0=a3, in1=rstd[:, 0:n].unsqueeze(1).to_broadcast([128, 8, n]), op=ALU.mult),
                reads=[b_acc, b_rstd], writes=[b_acc])
            for oc in range(8):
                P.op("act", lambda oc=oc: nc.scalar.activation(
                    out=zT[:, oc * n:(oc + 1) * n], in_=ACC[:, oc * n:(oc + 1) * n], func=AF.Silu,
                    scale=PARAM1[:, oc, 33:34], bias=PARAM1[:, oc, 34:35]),
                    reads=[b_acc, b_par1, b_work], writes=[b_zT])
            for half in range(4):
                bk = MP[pi[0] % 2]
                pi[0] += 1
                for j in range(2):
                    oc = half * 2 + j
                    for kc in range(8):
                        P.op("pe", lambda oc=oc, kc=kc, bk=bk, j=j: nc.tensor.matmul(
                            bank_f(bk)[:, j * 256:j * 256 + n], lhsT=Wpb[:, kc, oc * 128:(oc + 1) * 128],
                            rhs=zT[:, kc * n:(kc + 1) * n], start=(kc == 0), stop=(kc == 7)),
                            reads=bw + [b_zT], writes=[pb[bk]])
                src = bank_f(bk).rearrange("p (j t) -> p j t", t=256)[:, :, 0:n]
                P.op("act", lambda src=src, half=half, B_ap=B_ap: nc.scalar.copy(
                    out=B_ap[:, half * 2:half * 2 + 2, :], in_=src),
                    reads=[pb[bk]], writes=[bB] + bH)

    def stage_S3(sg, slot):
        stage_fence()
        Wa = slot_view(slot, 0, 8, 1024)
        Wb = slot_view(slot, 8192, 8, 1024)
        Wout = slot_view(slot, 16384, 8, 1024)
        bw = slotbuf[slot]
        cv = Carver()
        sgs = Slots([cv.f32(512).rearrange("p (j t) -> p j t", t=256) for _ in range(2)], "sg3")
        m1s = Slots([cv.f32(256) for _ in range(2)], "m1")
        m2s = Slots([cv.f32(256) for _ in range(2)], "m2")
        mT = cv.bf(8 * 256); b_mT = Buf("mT")
        xts = Slots([cv.f32(1024) for _ in range(2)], "xt3")
        MG, MH = (0, 1, 2), (3, 4, 5, 6)
        gi = [0]
        hi = [0]
        for s in sg:
            n = s.ntok
            xb = xt_bufs(s)
            bA, bB, bH = blkbufs(s)
            A_ap, B_ap = big_A(s), big_B(s)
            for oc in range(8):
                bk = MG[gi[0] % 3]
                gi[0] += 1
                for j, Wx in enumerate((Wa, Wb)):
                    for kc in range(8):
                        P.op("pe", lambda oc=oc, kc=kc, bk=bk, j=j, Wx=Wx: nc.tensor.matmul(
                            bank_f(bk)[:, j * 256:j * 256 + n], lhsT=Wx[:, kc, oc * 128:(oc + 1) * 128],
                            rhs=XT[:, kc, s.tok0:s.tok0 + n], start=(kc == 0), stop=(kc == 7)),
                            reads=bw + xb, writes=[pb[bk]])
                sg_, bsg = sgs.next()
                src = bank_f(bk).rearrange("p (j t) -> p j t", t=256)[:, :, 0:n]
                P.op("act", lambda sg_=sg_, src=src: nc.scalar.activation(out=sg_[:, :, 0:n], in_=src, func=AF.Sigmoid),
                     reads=[pb[bk], b_work], writes=[bsg])
                m1, bm1 = m1s.next()
                m2_, bm2 = m2s.next()
                P.op("pool", lambda m1=m1, sg_=sg_, oc=oc, A_ap=A_ap: nc.gpsimd.tensor_tensor(
                    out=m1[:, 0:n], in0=sg_[:, 0, 0:n], in1=A_ap[:, oc, :], op=ALU.mult),
                    reads=[bsg, bA, b_work], writes=[bm1])
                P.op("pool", lambda m2_=m2_, sg_=sg_, oc=oc, B_ap=B_ap: nc.gpsimd.tensor_tensor(
                    out=m2_[:, 0:n], in0=sg_[:, 1, 0:n], in1=B_ap[:, oc, :], op=ALU.mult),
                    reads=[bsg, bB, b_work], writes=[bm2])
                P.op("dve", lambda m1=m1, m2_=m2_, oc=oc: nc.vector.tensor_tensor(
                    out=mT[:, oc * n:(oc + 1) * n], in0=m1[:, 0:n], in1=m2_[:, 0:n], op=ALU.add),
                    reads=[bm1, bm2, b_work], writes=[b_mT])
            for ti in range(s.nt):
                xt_, bxt = xts.next()
                P.dma("sp", bxt.name, lambda xt_=xt_, s=s, ti=ti: nc.sync.dma_start(out=xt_, in_=x_src(s, ti)),
                      reads=[b_work], writes=[bxt])
                H = big_H(s, ti)
                for hf in range(2):
                    bk = MH[hi[0] % 4]
                    hi[0] += 1
                    for kc in range(8):
                        P.op("pe", lambda kc=kc, bk=bk, ti=ti, hf=hf: nc.tensor.matmul(
                            bank_f(bk), lhsT=mT[:, kc * n + ti * 128:kc * n + (ti + 1) * 128],
                            rhs=Wout[:, kc, hf * 512:(hf + 1) * 512], start=(kc == 0), stop=(kc == 7)),
                            reads=bw + [b_mT], writes=[pb[bk]])
                    P.op("dve", lambda H=H, bk=bk, hf=hf, xt_=xt_: nc.vector.tensor_tensor(
                        out=H[:, hf * 512:(hf + 1) * 512], in0=bank_f(bk), in1=xt_[:, hf * 512:(hf + 1) * 512],
                        op=ALU.add),
                        reads=[pb[bk], bxt], writes=[bH[ti], bA, bB])

    def stage_F(sg, slot, fg):
        stage_fence()
        j0, j1 = FG[fg]
        nch = j1 - j0
        Wg = slot_view(slot, 0, 8, nch * 128)
        Wv_ = slot_view(slot, 8192, 8, nch * 128)
        Wdn = ARENA[:, slot, 16384:16384 + nch * 1024].rearrange("p (j m) -> p j m", m=1024)
        bw = slotbuf[slot]
        cv = Carver()
        upxs = Slots([cv.f32(2 * 258) for _ in range(4)], "upx")
        cgs = Slots([cv.f32(512).rearrange("p (g t) -> p g t", t=256) for _ in range(4)], "cg")
        bcg_half = {bf_.name: [Buf(bf_.name + "g"), Buf(bf_.name + "v")] for _, bf_ in cgs.items}
        sgts = Slots([cv.f32(256) for _ in range(4)], "sgf")
        actTs = Slots([cv.bf(8 * 256) for _ in range(2)], "actT")
        hstg = cv.f32(2048); b_hstg = Buf("hstg")
        HIST = cv.f32(16 * 32); b_hist = Buf("hist")
        MU, MD, MX = (0, 1, 2, 3), (4, 5, 6), 7
        b_cg = {2: [Buf("cg2g"), Buf("cg2v")], 3: [Buf("cg3g"), Buf("cg3v")]}
        ui = [0]
        di = [0]
        ci = [0]
        pending = [None]
        CF4 = CF[:].rearrange("p (g j) r -> p g j r", g=2)
        CFS4 = CFS[:].rearrange("p (g j) b r -> p g j b r", g=2)
        P2 = PARAM2[:].rearrange("p (g j) r -> p g j r", g=2)
        def make_sub(s):
            n = s.ntok
            xb = xt_bufs(s)
            bA, bB, bH = blkbufs(s)
            actT, bact = actTs.next()
            if s.kind == 's':
                scf2 = D["scf"].rearrange("b r f -> (b r) f")
                hv = hstg.rearrange("p (g f) -> p g f", g=2)
                for g_ in range(2):
                    c0 = g_ * 2816 + j0 * 128
                    P.dma("sp", "hstg", lambda g_=g_, c0=c0: nc.sync.dma_start(
                        out=hv[0:32, g_, 0:nch * 128], in_=scf2[:, c0:c0 + nch * 128]),
                        reads=[b_work], writes=[b_hstg])
                pv = bank_f(MX).rearrange("p (c t) -> p c t", t=32)
                for g_ in range(2):
                    for jj in range(nch):
                        P.op("pe", lambda g_=g_, jj=jj, pv=pv: nc.tensor.transpose(
                            pv[:, g_ * 8 + jj, :], hv[0:32, g_, jj * 128:(jj + 1) * 128], ident_f[0:32, 0:32]),
                            reads=[b_hstg, b_idf], writes=[pb[MX]])
                P.op("act", lambda: nc.scalar.copy(out=HIST, in_=bank_f(MX)), reads=[pb[MX], b_work], writes=[b_hist])
                HV = HIST.rearrange("p (g j b r) -> p g j b r", g=2, j=8, r=2)
            def stageA(jj):
                j = j0 + jj
                bk = MU[ui[0] % 4]
                ui[0] += 1
                for g_, Wx in enumerate((Wg, Wv_)):
                    for kc in range(8):
                        P.op("pe", lambda jj=jj, kc=kc, bk=bk, g_=g_, Wx=Wx: nc.tensor.matmul(
                            bank_f(bk)[:, g_ * 256:g_ * 256 + n], lhsT=Wx[:, kc, jj * 128:(jj + 1) * 128],
                            rhs=XT[:, kc, s.tok0:s.tok0 + n], start=(kc == 0), stop=(kc == 7)),
                            reads=bw + xb, writes=[pb[bk]])
                upx, bupx = upxs.next()
                src = bank_f(bk).rearrange("p (g t) -> p g t", t=256)[:, :, 0:n]
                if s.kind == 'p':
                    U = upx.rearrange("p (g t) -> p g t", t=258)
                    P.op("act", lambda U=U, src=src: nc.scalar.copy(out=U[:, :, 2:258], in_=src),
                         reads=[pb[bk], b_work], writes=[bupx])
                    if s.first:
                        P.op("pool", lambda U=U: nc.gpsimd.memset(U[:, :, 0:2], 0.0), reads=[b_work], writes=[bupx])
                    else:
                        P.op("pool", lambda U=U, j=j: nc.gpsimd.tensor_copy(out=U[:, :, 0:2], in_=CF4[:, :, j, :]),
                             reads=[b_cf, b_work], writes=[bupx])
                    P.op("pool", lambda U=U, j=j: nc.gpsimd.tensor_copy(out=CF4[:, :, j, :], in_=U[:, :, 256:258]),
                         reads=[bupx], writes=[b_cf])

                    def tap(g_, k, U=U):
                        return U[:, g_, k:k + 256]

                else:
                    U = upx[:, 0:320].rearrange("p (g b t) -> p g b t", g=2, t=10)
                    P.op("act", lambda U=U, src=src: nc.scalar.copy(
                        out=U[:, :, :, 2:10], in_=src.rearrange("p g (b i) -> p g b i", i=8)),
                        reads=[pb[bk], b_work], writes=[bupx])
                    P.op("pool", lambda U=U, jj=jj: nc.gpsimd.tensor_copy(out=U[:, :, :, 0:2], in_=HV[:, :, jj, :, :]),
                         reads=[b_hist, b_work], writes=[bupx])
                    P.op("pool", lambda U=U, j=j: nc.gpsimd.tensor_copy(out=CFS4[:, :, j, :, :], in_=U[:, :, :, 8:10]),
                         reads=[bupx], writes=[b_cfs])

                    def tap(g_, k, U=U):
                        return U[:, g_, :, k:k + 8]

                return dict(tap=tap, bupx=bupx, j=j, jj=jj)

            def stageB(st):
                tap, bupx, j, jj = st['tap'], st['bupx'], st['j'], st['jj']
                cg, bcg = cgs.next()
                if s.kind == 'p':
                    def cgP(g_, cg=cg):
                        return cg[:, g_, :]
                else:
                    def cgP(g_, cg=cg):
                        return cg[:, g_, 0:128].rearrange("p (b i) -> p b i", i=8)
                bcgh = bcg_half[bcg.name]
                for g_ in range(2):
                    P.op("act", lambda g_=g_, j=j, tap=tap, cgP=cgP: nc.scalar.activation(
                        out=cgP(g_), in_=tap(g_, 2), func=AF.Identity,
                        scale=P2[:, g_, j, 2:3], bias=P2[:, g_, j, 3:4]),
                        reads=[bupx, b_par2, b_work], writes=[bcgh[g_]])
                for k in (1, 0):
                    for g_ in range(2):
                        P.op("dve", lambda g_=g_, j=j, k=k, tap=tap, cgP=cgP: nc.vector.scalar_tensor_tensor(
                            out=cgP(g_), in0=tap(g_, k), scalar=P2[:, g_, j, k:k + 1], in1=cgP(g_),
                            op0=ALU.mult, op1=ALU.add),
                            reads=[bupx, bcgh[g_], b_par2], writes=[bcgh[g_]])
                st['cg'], st['bcgh'] = cg, bcgh

            def stageC(st):
                cg, bcgh, jj = st['cg'], st['bcgh'], st['jj']
                sgt, bsgt = sgts.next()
                P.op("act", lambda: nc.scalar.activation(
                    out=sgt[:, 0:n], in_=cg[:, 0, 0:n], func=AF.Silu),
                    reads=[bcgh[0], b_work], writes=[bsgt])
                P.op("pool", lambda: nc.gpsimd.tensor_tensor(
                    out=actT[:, jj * n:(jj + 1) * n], in0=sgt[:, 0:n], in1=cg[:, 1, 0:n], op=ALU.mult),
                    reads=[bsgt, bcgh[1], b_work], writes=[bact])
            def down():
                for ti in range(s.nt):
                    H = big_H(s, ti)
                    for hf in range(2):
                        bk = MD[di[0] % 3]
                        di[0] += 1
                        for jj in range(nch):
                            P.op("pe", lambda jj=jj, bk=bk, ti=ti, hf=hf: nc.tensor.matmul(
                                bank_f(bk), lhsT=actT[:, jj * n + ti * 128:jj * n + (ti + 1) * 128],
                                rhs=Wdn[:, jj, hf * 512:(hf + 1) * 512], start=(jj == 0), stop=(jj == nch - 1)),
                                reads=bw + [bact], writes=[pb[bk]])
                        P.op("dve", lambda H=H, bk=bk, hf=hf: nc.vector.tensor_tensor(
                            out=H[:, hf * 512:(hf + 1) * 512], in0=bank_f(bk), in1=H[:, hf * 512:(hf + 1) * 512],
                            op=ALU.add),
                            reads=[pb[bk], bH[ti]], writes=[bH[ti]])
            return stageA, stageB, stageC, down

        subs = [make_sub(s_) for s_ in sg]
        items = [(si_, jj_) for si_ in range(len(sg)) for jj_ in range(nch)]
        sts = {}
        for t in range(len(items) + 2):
            if t < len(items):
                sts[t] = subs[items[t][0]][0](items[t][1])
            if 0 <= t - 1 < len(items):
                subs[items[t - 1][0]][1](sts[t - 1])
            if 0 <= t - 2 < len(items):
                si2, jj2 = items[t - 2]
                subs[si2][2](sts[t - 2])
                if jj2 == nch - 1:
                    subs[si2][3]()

    def stage_final(sg, sgi):
        stage_fence()
        cv = Carver()
        junk = cv.bf(1024)
        tstg = Slots([cv.f32(1408) for _ in range(2)], "tstg")
        tiles = [(s, ti) for s in sg for ti in range(s.nt)]
        nt = len(tiles)
        for n, (s, ti) in enumerate(tiles):
            P.op("act", lambda s=s, ti=ti, n=n: nc.scalar.activation(
                out=junk, in_=big_H(s, ti), func=AF.Square, accum_out=RS[:, 2, n:n + 1]),
                reads=[blkbufs(s)[2][ti], b_work], writes=[b_junk, b_rs[2]])
        P.op("act", lambda: nc.scalar.activation(out=RS2[:, 0:nt], in_=RS[:, 2, 0:nt], func=AF.Sqrt,
                                                 bias=EPS, scale=1.0 / 1024.0),
             reads=[b_rs[2]], writes=[b_rs2])
        P.op("dve", lambda: nc.vector.reciprocal(out=RS[:, 2, 0:nt], in_=RS2[:, 0:nt]),
             reads=[b_rs2], writes=[b_rs[2]])
        for n, (s, ti) in enumerate(tiles):
            H = big_H(s, ti)
            bh = blkbufs(s)[2][ti]
            P.op("dve", lambda H=H, n=n: nc.vector.scalar_tensor_tensor(
                out=H, in0=H, scalar=RS[:, 2, n:n + 1], in1=gf_bc[:], op0=ALU.mult, op1=ALU.mult),
                reads=[bh, b_rs[2], b_gf], writes=[bh])
            P.dma("sp", "yout%d" % blk_index[s.blk], lambda H=H, s=s, ti=ti: nc.sync.dma_start(out=y_dst(s, ti), in_=H),
                  reads=[bh])
            out_bufs.append(bh)
        jobs = []
        if sgi == 0:
            jobs.append(("s", CFS[:].rearrange("p c b r -> p c (b r)"), 32, D["cfs"].rearrange("b r f -> (b r) f"), b_cfs))
        else:
            jobs.append(("p", CF[:], 2, D["cfp"], b_cf))
        bi = [0]
        for kind, srcT, nr, dst, bsrc in jobs:
            for rnd in range(4):
                st_, bst = tstg.next()
                for q in range(3):
                    bk = (bi[0] % 2) * 3 + q
                    nchq = min(4, 11 - q * 4)
                    for jj in range(nchq):
                        c = rnd * 11 + q * 4 + jj
                        P.op("pe", lambda bk=bk, jj=jj, c=c, srcT=srcT, nr=nr: nc.tensor.transpose(
                            bank_f(bk)[0:nr, jj * 128:(jj + 1) * 128], srcT[:, c, :], ident_f[:]),
                            reads=[bsrc, b_idf], writes=[pb[bk]])
                    P.op("act", lambda bk=bk, q=q, nchq=nchq, st_=st_, nr=nr: nc.scalar.copy(
                        out=st_[0:nr, q * 512:q * 512 + nchq * 128], in_=bank_f(bk)[0:nr, 0:nchq * 128]),
                        reads=[pb[bk], b_work], writes=[bst])
                bi[0] += 1
                P.dma("sp", bst.name, lambda st_=st_, nr=nr, dst=dst, rnd=rnd: nc.sync.dma_start(
                    out=dst[:, rnd * 1408:(rnd + 1) * 1408], in_=st_[0:nr, :]), reads=[bst])
                out_bufs.append(bst)

    import os
    nc_real = nc
    nc = _NcProxy()
    stages = []
    for sgi in range(2):
        for st in ("S1", "S2", "S3", "F0", "F1", "F2"):
            stages.append((sgi, st))
    init_consts()
    load_stage_weights(stages[0][1], 0)
    maxst = int(os.environ.get("KSTAGES", "99"))
    substop = os.environ.get("KSUB", "")
    for i, (sgi, st) in enumerate(stages):
        if i >= maxst:
            break
        slot = i % 2
        sg = SGS[sgi]
        if i + 1 < len(stages) and "w2" not in os.environ.get("KSKIP", "") and not (i == 0 and os.environ.get("KORDER", "") == "early"):
            load_stage_weights(stages[i + 1][1], (i + 1) % 2)
        if st == "S1":
            stage_normT(sg, 0)
            if substop == "X":
                break
            stage_S1(sg, slot)
        elif st == "S2":
            if sgi == 0:
                issue_d2d()
            stage_S2(sg, slot)
        elif st == "S3":
            stage_S3(sg, slot)
            stage_normT(sg, 1)
        else:
            stage_F(sg, slot, int(st[1]))
            if st == "F2":
                stage_final(sg, sgi)
    print("TOTAL_OPS", len(P.ops))
    if os.environ.get("KDUMPOP"):
        for k_ in os.environ["KDUMPOP"].split(","):
            o_ = P.ops[int(k_)]
            print("OP", k_, o_[0], o_[1].name, o_[2], o_[3][:2], [(a.name, w) for a, w in o_[4]])
            print("   ARGS", {kk: (vv.shape, vv.offset, vv.ap) if hasattr(vv, "ap") else vv for kk, vv in o_[3][3].items()})
    if P.limit is not None and P.ops:
        print("LAST_OP", P.ops[min(P.limit, len(P.ops)) - 1][1].name, P.ops[min(P.limit, len(P.ops)) - 1][3][:2])
    P.wait_all("sp", out_bufs + slotbufs[0] + slotbufs[1] + [b_gf, b_esink])
    P.emit()
    ctx.close()
    return nc_real


_NC_CACHE = {}


def kernel(**inputs):
    f = lambda a: np.ascontiguousarray(np.asarray(a, dtype=np.float32))
    xp = f(inputs["x_prompt"]); xs = f(inputs["x_sample"])
    ck = f(inputs["cache_k_win"])[0].reshape(128, 128, 256)
    cv = f(inputs["cache_v_win"])[0].reshape(128, 128, 256)
    scb = f(inputs["state_conv_b"])[0]
    scf = f(inputs["state_conv_ffn"])[0]
    shared = {
        "norm1_g": f(inputs["norm1_g"]).reshape(1, 1024), "w_in": f(inputs["w_in"])[0],
        "sinks": f(inputs["sinks"]).reshape(1, 16), "w_o": f(inputs["w_o_attn"])[0],
        "conv_b_w": f(inputs["conv_b_w"])[0], "conv_b_bias": f(inputs["conv_b_bias"]).reshape(1, 1024),
        "ln_b_g": f(inputs["ln_b_g"]).reshape(1, 1024), "ln_b_b": f(inputs["ln_b_b"]).reshape(1, 1024),
        "w_pb": f(inputs["w_pb"])[0], "w_out": f(inputs["w_out"])[0],
        "norm2_g": f(inputs["norm2_g"]).reshape(1, 1024), "w_up": f(inputs["w_up"])[0],
        "ffn_conv_w": f(inputs["ffn_conv_w"])[0], "ffn_conv_b": f(inputs["ffn_conv_b"]).reshape(1, 5632),
        "w_down": f(inputs["w_down"])[0], "norm_f_g": f(inputs["norm_f_g"]).reshape(1, 1024),
    }
    in_maps = []
    for c in range(NCORES):
        m = dict(shared)
        m["xp"] = xp[c]
        m["xs"] = xs[16 * c:16 * c + 16].reshape(128, 1024)
        m["ck"] = ck[16 * c:16 * c + 16]
        m["cv"] = cv[16 * c:16 * c + 16]
        m["scb"] = scb[16 * c:16 * c + 16]
        m["scf"] = scf[16 * c:16 * c + 16]
        in_maps.append(m)
    if "nc" not in _NC_CACHE:
        _NC_CACHE["nc"] = build_program()
    res = run_bass_kernel_spmd(_NC_CACHE["nc"], in_maps, core_ids=list(range(NCORES)))
    R = res.results
    cat = lambda k: np.concatenate([np.asarray(r[k], dtype=np.float32) for r in R], axis=0)
    y_prompt = np.stack([np.asarray(r["yp"], dtype=np.float32) for r in R], axis=0)
    y_sample = cat("ys").reshape(128, 8, 1024)
    k_win_prompt = np.stack([np.asarray(r["kwp"]) for r in R], 0).reshape(1, 8, 128, 4, 64).astype(np.float32)
    v_win_prompt = np.stack([np.asarray(r["vwp"]) for r in R], 0).reshape(1, 8, 128, 4, 64).astype(np.float32)
    conv_b_prompt = np.stack([np.asarray(r["cbp"]) for r in R], 0).reshape(1, 8, 30, 1024).astype(np.float32)
    conv_ffn_prompt = np.stack([np.asarray(r["cfp"]) for r in R], 0).reshape(1, 8, 2, 5632).astype(np.float32)
    k_win_sample = cat("kws").reshape(1, 128, 128, 4, 64)
    v_win_sample = cat("vws").reshape(1, 128, 128, 4, 64)
    conv_b_sample = cat("cbs").reshape(1, 128, 30, 1024)
    conv_ffn_sample = cat("cfs").reshape(1, 128, 2, 5632)
    return (y_prompt, y_sample, k_win_prompt, v_win_prompt, conv_b_prompt, conv_ffn_prompt,
            k_win_sample, v_win_sample, conv_b_sample, conv_ffn_sample)
```

```python
from contextlib import ExitStack
import os
import numpy as np
import concourse.bass as bass
import concourse.mybir as mybir
from concourse.bass_utils import run_bass_kernel_spmd

F32 = mybir.dt.float32
BF16 = mybir.dt.bfloat16
AF = mybir.ActivationFunctionType
ALU = mybir.AluOpType
EPOCH = 4000
NCORES = 8
EPS = 1e-6


class Buf:
    __slots__ = ("name", "last_w", "readers")

    def __init__(self, name):
        self.name = name
        self.last_w = None
        self.readers = []


class Actor:
    def __init__(self, name, handle=None, is_chan=False):
        self.name = name
        self.handle = handle
        self.is_chan = is_chan
        self.n = 0
        self.last_op = -1
        self.marked = set()
        self.sems = []
        self.rank = {}
        self.seen = {}


class Prog:
    def __init__(self, nc, ctx):
        self.nc = nc
        self.ctx = ctx
        self.ops = []
        self.limit = int(os.environ["KOPLIMIT"]) if os.environ.get("KOPLIMIT") else None
        self.actors = {}
        for nm, h in (("pe", nc.tensor), ("act", nc.scalar), ("dve", nc.vector),
                      ("pool", nc.gpsimd), ("sp", nc.sync)):
            self.actors[nm] = Actor(nm, h)

    def chan(self, name):
        if name not in self.actors:
            self.actors[name] = Actor(name, None, is_chan=True)
        return self.actors[name]

    def _deps(self, eng, reads, writes, skip_chan=None):
        deps = {}

        def add(a, idx):
            if deps.get(a.name, (None, -1))[1] < idx:
                deps[a.name] = (a, idx)
        for b in reads:
            if b.last_w is not None:
                add(*b.last_w)
        for b in writes:
            if b.last_w is not None:
                add(*b.last_w)
            for r in b.readers:
                add(*r)
        waits = []
        for nm, (a, idx) in deps.items():
            if a is eng and eng.name == "pe":
                continue
            if a is skip_chan:
                continue
            if eng.seen.get(nm, -1) >= idx:
                continue
            if a.is_chan and idx != a.n - 1:
                raise RuntimeError("unsafe wait on shared DMA channel %s (%d of %d)" % (nm, idx, a.n))
            eng.seen[nm] = idx
            a.marked.add(idx)
            waits.append((a, idx))
        return waits

    def op(self, engname, fn, reads=(), writes=()):
        if self.limit is not None and len(self.ops) >= self.limit:
            return
        eng = self.actors[engname]
        waits = self._deps(eng, reads, writes)
        idx = eng.n
        eng.n += 1
        eng.last_op = idx
        self.ops.append(("op", eng, idx, fn(), waits, None))
        for b in reads:
            b.readers.append((eng, idx))
        for b in writes:
            b.last_w = (eng, idx)
            b.readers = []

    def dma(self, qname, channame, fn, reads=(), writes=(), waw=False):
        if self.limit is not None and len(self.ops) >= self.limit:
            return
        q = self.actors[qname]
        ch = self.chan(channame)
        waits = self._deps(q, reads, writes, skip_chan=None if waw else ch)
        idx = ch.n
        ch.n += 1
        qidx = q.n
        q.n += 1
        self.ops.append(("dma", q, qidx, fn(), waits, (ch, idx)))
        for b in reads:
            b.readers.append((ch, idx))
        for b in writes:
            b.last_w = (ch, idx)
            b.readers = []

    def barrier(self, bufs=()):
        comp = [self.actors[n] for n in ("pe", "act", "dve", "pool")]
        snap = {a.name: a.last_op for a in comp if a.last_op >= 0}
        for E in comp + [self.actors["sp"]]:
            waits = []
            for a in comp:
                if a is E or a.name not in snap:
                    continue
                idx = snap[a.name]
                if E.seen.get(a.name, -1) >= idx:
                    continue
                E.seen[a.name] = idx
                a.marked.add(idx)
                waits.append((a, idx))
            for b in bufs:
                for (a, _) in b.readers:
                    if not a.is_chan:
                        continue
                    idx2 = a.n - 1
                    if E.seen.get(a.name, -1) >= idx2:
                        continue
                    E.seen[a.name] = idx2
                    a.marked.add(idx2)
                    waits.append((a, idx2))
            i = E.n
            E.n += 1
            self.ops.append(("nop", E, i, None, waits, None))

    def wait_all(self, engname, bufs):
        eng = self.actors[engname]
        waits = self._deps(eng, [], bufs)
        idx = eng.n
        eng.n += 1
        self.ops.append(("nop", eng, idx, None, waits, None))

    def check_no_deadlock(self):
        queues = {}
        for op in self.ops:
            queues.setdefault(op[1].name, []).append(op)
        pos = {k: 0 for k in queues}
        sem = {}
        progress = True
        while progress:
            progress = False
            for k, q in queues.items():
                while pos[k] < len(q):
                    kind, eng, idx, fn, waits, chinfo = q[pos[k]]
                    ok = True
                    for (a, widx) in waits:
                        ep, val = a.rank[widx]
                        if sem.get((a.name, ep), 0) < val:
                            ok = False
                            break
                    if not ok:
                        break
                    if kind == "dma":
                        key = (chinfo[0].name, 0)
                        sem[key] = sem.get(key, 0) + 16
                    elif kind == "op" and idx in eng.marked:
                        key = (eng.name, eng.rank[idx][0])
                        sem[key] = sem.get(key, 0) + 1
                    pos[k] += 1
                    progress = True
        for k, q in queues.items():
            if pos[k] < len(q):
                kind, eng, idx, fn, waits, chinfo = q[pos[k]]
                raise RuntimeError("DEADLOCK: %s stuck at op %d (%s) waits=%s" % (
                    k, idx, fn[:2] if fn else None, [(a.name, w, a.rank[w]) for a, w in waits]))

    def emit(self):
        nsem = 0
        for a in self.actors.values():
            m = sorted(a.marked)
            a.rank = {}
            if a.is_chan:
                a.sems = [self.ctx.enter_context(self.nc.semaphore("c_" + a.name))]
                nsem += 1
                for i in m:
                    a.rank[i] = (0, 16 * (i + 1))
            else:
                nep = max((len(m) + EPOCH - 1) // EPOCH, 1)
                a.sems = [self.ctx.enter_context(self.nc.semaphore("e_%s%d" % (a.name, e)))
                          for e in range(nep)]
                nsem += nep
                for r, i in enumerate(m):
                    a.rank[i] = (r // EPOCH, r % EPOCH + 1)
        self.nsem = nsem
        self.check_no_deadlock()
        for kind, eng, idx, fn, waits, chinfo in self.ops:
            for (a, widx) in waits:
                ep, val = a.rank[widx]
                eng.handle.wait_ge(a.sems[ep], val)
            if kind == "nop":
                continue
            ename, meth, a_, k_ = fn
            inst = getattr(getattr(self.nc, ename), meth)(*a_, **k_)
            if kind == "dma":
                assert idx not in eng.marked, "DMA queue op used as an engine milestone"
                inst.then_inc(chinfo[0].sems[0], 16)
            elif idx in eng.marked:
                inst.then_inc(eng.sems[eng.rank[idx][0]], 1)


class _EngProxy:
    def __init__(self, name):
        self._name = name

    def __getattr__(self, meth):
        name = self._name

        def call(*a, **k):
            return (name, meth, a, k)
        return call


class _NcProxy:
    def __init__(self):
        self.tensor = _EngProxy("tensor")
        self.scalar = _EngProxy("scalar")
        self.vector = _EngProxy("vector")
        self.gpsimd = _EngProxy("gpsimd")
        self.sync = _EngProxy("sync")


class Slots:
    def __init__(self, aps, name):
        self.items = [(ap, Buf("%s%d" % (name, i))) for i, ap in enumerate(aps)]
        self.i = 0

    def next(self):
        it = self.items[self.i % len(self.items)]
        self.i += 1
        return it


class Sub:
    def __init__(self, kind, idx, ntok, tok0, blk):
        self.kind = kind
        self.idx = idx
        self.ntok = ntok
        self.tok0 = tok0
        self.blk = blk
        self.nt = ntok // 128
        self.first = (kind == 'p' and idx == 0)
        self.last = (kind == 'p' and idx == 7)


def build_program():
    nc = bass.Bass("TRN2", target_bir_lowering=False)
    D = {}

    def din(name, shape):
        D[name] = nc.dram_tensor(name, list(shape), F32, kind="ExternalInput").ap()

    def dout(name, shape):
        D[name] = nc.dram_tensor(name, list(shape), F32, kind="ExternalOutput").ap()

    din("xp", (2048, 1024)); din("xs", (128, 1024))
    din("ck", (16, 128, 256)); din("cv", (16, 128, 256))
    din("scb", (16, 30, 1024)); din("scf", (16, 2, 5632))
    din("norm1_g", (1, 1024)); din("w_in", (1024, 5632)); din("sinks", (1, 16))
    din("w_o", (1024, 1024)); din("conv_b_w", (31, 1024)); din("conv_b_bias", (1, 1024))
    din("ln_b_g", (1, 1024)); din("ln_b_b", (1, 1024)); din("w_pb", (1024, 1024))
    din("w_out", (1024, 1024)); din("norm2_g", (1, 1024)); din("w_up", (1024, 5632))
    din("ffn_conv_w", (3, 5632)); din("ffn_conv_b", (1, 5632)); din("w_down", (2816, 1024))
    din("norm_f_g", (1, 1024))
    dout("yp", (2048, 1024)); dout("ys", (128, 1024))
    dout("kwp", (128, 256)); dout("vwp", (128, 256)); dout("cbp", (30, 1024)); dout("cfp", (2, 5632))
    dout("kws", (16, 128, 256)); dout("vws", (16, 128, 256))
    dout("cbs", (16, 30, 1024)); dout("cfs", (16, 2, 5632))

    w_in3 = D["w_in"].rearrange("(kc p) n -> p kc n", p=128)
    w_o3 = D["w_o"].rearrange("(kc p) n -> p kc n", p=128)
    w_pb3 = D["w_pb"].rearrange("(kc p) n -> p kc n", p=128)
    w_out3 = D["w_out"].rearrange("(kc p) n -> p kc n", p=128)
    w_up3 = D["w_up"].rearrange("(kc p) n -> p kc n", p=128)
    w_dn3 = D["w_down"].rearrange("(j p) n -> p j n", p=128)

    ctx = ExitStack()
    P = Prog(nc, ctx)

    def sb(name, shape, dt):
        return ctx.enter_context(nc.sbuf_tensor(name, list(shape), dt))

    ident_bf = sb("ident_bf", (128, 128), BF16); b_idbf = Buf("idbf")
    ident_f = sb("ident_f", (128, 128), F32); b_idf = Buf("idf")
    ones_bf = sb("ones_bf", (128, 128), BF16); b_ones = Buf("ones")
    MASK2 = sb("mask2", (128, 2, 2, 128), BF16); b_mask2 = Buf("mask2")
    MASKS4 = sb("masks4", (128, 4, 128), BF16); b_masks4 = Buf("masks4")
    MASKP = sb("maskp", (128, 8), BF16); b_maskp = Buf("maskp")
    esink = sb("esink", (128, 16), F32); b_esink = Buf("esink")
    gf_bc = sb("gf_bc", (128, 1024), F32); b_gf = Buf("gf")
    PARAM1 = sb("param1", (128, 8, 36), F32); b_par1 = Buf("par1")
    PARAM2 = sb("param2", (128, 44, 4), F32); b_par2 = Buf("par2")
    RS = sb("rstd_tab", (128, 3, 16), F32)
    RS2 = sb("rstd_tmp", (128, 16), F32); b_rs2 = Buf("rs2")
    b_rs = [Buf("rs0"), Buf("rs1"), Buf("rs2")]
    KC = sb("kcarry", (128, 4, 128), BF16); b_kc = Buf("kc")
    VC = sb("vcarry", (128, 4, 65), BF16); b_vc = Buf("vc")
    UC = sb("ucarry", (128, 8, 30), F32); b_uc = Buf("uc")
    CF = sb("fcarry", (128, 44, 2), F32); b_cf = Buf("cf")
    CFS = sb("fcarry_s", (128, 44, 16, 2), F32); b_cfs = Buf("cfs")
    XT = sb("XT", (128, 8, 1152), BF16)
    BIG = sb("BIG", (128, 18432), BF16)
    ARENA = sb("ARENA", (128, 2, 24576), BF16)
    NWCH = 12
    slotbufs = [[Buf("slot%d_%d" % (s_, c_)) for c_ in range(NWCH)] for s_ in range(2)]
    slotbuf = [slotbufs[0], slotbufs[1]]
    wl_count = [0, 0]
    WORKN = 20096
    WORK = sb("WORK", (128, WORKN), BF16)
    b_junk = Buf("junk")

    banks = [ctx.enter_context(nc.psum_tensor("bank%d" % i, [128, 512], F32)) for i in range(8)]
    pb = [Buf("bank%d" % i) for i in range(8)]

    def bank_f(i):
        return banks[i][:]

    def bank_b(i):
        return banks[i][:].bitcast(BF16)

    b_xt = [Buf("XT%d" % i) for i in range(9)]

    def xt_bufs(sub):
        t0 = sub.tok0 // 128
        return b_xt[t0:t0 + sub.nt]

    blk_index = {0: 0, 4096: 1, 8192: 2, 12288: 3, 16384: 4}
    b_A = [Buf("A%d" % i) for i in range(5)]
    b_B = [Buf("B%d" % i) for i in range(5)]
    b_H = [[Buf("H%d_%d" % (i, t)) for t in range(2)] for i in range(5)]

    def big_A(sub):
        n = sub.ntok
        return BIG[:, sub.blk:sub.blk + 8 * n].rearrange("p (c t) -> p c t", t=n)

    def big_B(sub):
        n = sub.ntok
        return BIG[:, sub.blk + 8 * n:sub.blk + 16 * n].rearrange("p (c t) -> p c t", t=n)

    def big_H(sub, ti):
        o = sub.blk + ti * 2048
        return BIG[:, o:o + 2048].bitcast(F32)

    def blkbufs(sub):
        i = blk_index[sub.blk]
        return b_A[i], b_B[i], b_H[i][:sub.nt]

    class Carver:
        def __init__(self):
            self.off = 0

        def take(self, nelem_bf16):
            o = self.off
            self.off += (nelem_bf16 + 15) // 16 * 16
            assert self.off <= WORKN, "WORK overflow %d" % self.off
            return WORK[:, o:o + nelem_bf16]

        def f32(self, n):
            return self.take(2 * n).bitcast(F32)

        def bf(self, n):
            return self.take(n)

    b_work = Buf("work_epoch")

    def x_src(sub, ti):
        if sub.kind == 'p':
            r0 = (sub.idx * 2 + ti) * 128
            return D["xp"][r0:r0 + 128, :]
        return D["xs"][:, :]

    def y_dst(sub, ti):
        if sub.kind == 'p':
            r0 = (sub.idx * 2 + ti) * 128
            return D["yp"][r0:r0 + 128, :]
        return D["ys"][:, :]

    SG0 = [Sub('p', i, 256, 256 * i, 4096 * i) for i in range(4)] + [Sub('s', 0, 128, 1024, 16384)]
    SG1 = [Sub('p', 4 + i, 256, 256 * i, 4096 * i) for i in range(4)]
    SGS = [SG0, SG1]

    def stage_fence():
        P.barrier(out_bufs)

    def slot_view(slot, off, kc, n):
        return ARENA[:, slot, off:off + kc * n].rearrange("p (k n) -> p k n", n=n)

    def wload(slot, dst, src):
        import os
        if os.environ.get("KPRINT"):
            print("WLOAD", slot, dst.shape, dst.offset, dst.ap, src.shape, src.offset, src.ap)
        if "w" in os.environ.get("KSKIP", "").split(","):
            return
        c_ = wl_count[slot] % NWCH
        wl_count[slot] += 1
        P.dma("pool", "wslot%d_%d" % (slot, c_), lambda: nc.gpsimd.dma_start(out=dst, in_=src),
              writes=[slotbufs[slot][c_]], waw=True)

    FG = [(0, 8), (8, 16), (16, 22)]

    def load_stage_weights(stage, slot):
        import os
        part = os.environ.get("KWPART", "")
        if stage == "S1" and part:
            if "q" in part:
                wload(slot, slot_view(slot, 0, 8, 1024), w_in3[:, :, 0:1024])
            if "k" in part:
                kd = ARENA[:, slot, 8192:12288].rearrange("p (k h e) -> p k h e", h=4, e=128)
                for kvh in range(4):
                    ksrc = w_in3[:, :, 1024 + kvh * 64:1088 + kvh * 64]
                    wload(slot, kd[:, :, kvh, 0:64], ksrc)
                    wload(slot, kd[:, :, kvh, 64:128], ksrc)
            if "v" in part:
                wload(slot, slot_view(slot, 12288, 8, 256), w_in3[:, :, 1280:1536])
            if "o" in part:
                wload(slot, slot_view(slot, 14336, 8, 1024), w_o3)
            return
        if stage == "S1":
            wload(slot, slot_view(slot, 0, 8, 1024), w_in3[:, :, 0:1024])
            kd = ARENA[:, slot, 8192:12288].rearrange("p (k h e) -> p k h e", h=4, e=128)
            for kvh in range(4):
                ksrc = w_in3[:, :, 1024 + kvh * 64:1088 + kvh * 64]
                wload(slot, kd[:, :, kvh, 0:64], ksrc)
                wload(slot, kd[:, :, kvh, 64:128], ksrc)
            wload(slot, slot_view(slot, 12288, 8, 256), w_in3[:, :, 1280:1536])
            wload(slot, slot_view(slot, 14336, 8, 1024), w_o3)
        elif stage == "S2":
            ks2 = os.environ.get("KS2", "abc")
            if "a" in ks2:
                wload(slot, slot_view(slot, 0, 8, 1024), w_in3[:, :, 1536:2560])
            if "A" in ks2:
                wload(slot, slot_view(0, 8192, 8, 1024), w_in3[:, :, 1536:2560])
            if "Z" in ks2:
                wload(slot, slot_view(1, 0, 8, 1024), w_in3[:, :, 0:1024])
            if "b" in ks2:
                wload(slot, slot_view(slot, 8192, 8, 1024), w_in3[:, :, 2560:3584])
            if "c" in ks2:
                wload(slot, slot_view(slot, 16384, 8, 1024), w_pb3)
        elif stage == "S3":
            wload(slot, slot_view(slot, 0, 8, 1024), w_in3[:, :, 3584:4608])
            wload(slot, slot_view(slot, 8192, 8, 1024), w_in3[:, :, 4608:5632])
            wload(slot, slot_view(slot, 16384, 8, 1024), w_out3)
        else:
            j0, j1 = FG[int(stage[1])]
            n = j1 - j0
            wload(slot, slot_view(slot, 0, 8, n * 128), w_up3[:, :, j0 * 128:j1 * 128])
            wload(slot, slot_view(slot, 8192, 8, n * 128), w_up3[:, :, 2816 + j0 * 128:2816 + j1 * 128])
            wload(slot, ARENA[:, slot, 16384:16384 + n * 1024].rearrange("p (j m) -> p j m", m=1024),
                  w_dn3[:, j0:j1, :])

    def init_consts():
        g = nc.gpsimd
        P.op("pool", lambda: g.memset(ident_bf[:], 1.0), writes=[b_idbf])
        P.op("pool", lambda: g.affine_select(ident_bf[:], ident_bf[:], [[-1, 128]], ALU.is_equal, 0.0,
                                             base=0, channel_multiplier=1), reads=[b_idbf], writes=[b_idbf])
        P.op("pool", lambda: g.memset(ident_f[:], 1.0), writes=[b_idf])
        P.op("pool", lambda: g.affine_select(ident_f[:], ident_f[:], [[-1, 128]], ALU.is_equal, 0.0,
                                             base=0, channel_multiplier=1), reads=[b_idf], writes=[b_idf])
        P.op("pool", lambda: g.memset(ones_bf[:], 1.0 / 1024.0), writes=[b_ones])
        P.op("pool", lambda: g.memset(MASK2[:], 1.0), writes=[b_mask2])
        for hh in range(2):
            P.op("pool", lambda hh=hh: g.affine_select(MASK2[:, hh, 0, :], MASK2[:, hh, 0, :], [[-1, 128]],
                                                       ALU.is_gt, 0.0, base=0, channel_multiplier=1),
                 reads=[b_mask2], writes=[b_mask2])
            P.op("pool", lambda hh=hh: g.affine_select(MASK2[:, hh, 1, :], MASK2[:, hh, 1, :], [[1, 128]],
                                                       ALU.is_ge, 0.0, base=0, channel_multiplier=-1),
                 reads=[b_mask2], writes=[b_mask2])
        P.op("pool", lambda: g.memset(MASKS4[:], 1.0), writes=[b_masks4])
        for r in range(4):
            v = MASKS4[:, r, :].rearrange("p (b i) -> p b i", i=8)
            P.op("pool", lambda v=v: g.affine_select(v, v, [[8, 16], [1, 8]], ALU.is_ge, 0.0,
                                                     base=0, channel_multiplier=-1),
                 reads=[b_masks4], writes=[b_masks4])
            P.op("pool", lambda v=v: g.affine_select(v, v, [[-8, 16], [0, 8]], ALU.is_ge, 0.0,
                                                     base=0, channel_multiplier=1),
                 reads=[b_masks4], writes=[b_masks4])
        P.op("pool", lambda: g.memset(MASKP[:], 1.0), writes=[b_maskp])
        P.op("pool", lambda: g.affine_select(MASKP[:], MASKP[:], [[-1, 8]], ALU.is_gt, 0.0,
                                             base=0, channel_multiplier=1), reads=[b_maskp], writes=[b_maskp])
        P.op("pool", lambda: g.memset(CF[:], 0.0), writes=[b_cf])
        P.dma("sp", "esink", lambda: nc.sync.dma_start(out=esink[:], in_=D["sinks"].to_broadcast([128, 16])),
              writes=[b_esink])
        P.op("act", lambda: nc.scalar.activation(out=esink[:], in_=esink[:], func=AF.Exp),
             reads=[b_esink], writes=[b_esink])
        P.dma("sp", "gf", lambda: nc.sync.dma_start(out=gf_bc[:], in_=D["norm_f_g"].to_broadcast([128, 1024])),
              writes=[b_gf])
        cv = Carver()
        prow = cv.f32(1024)
        prow2 = cv.f32(5632)
        b_prow = Buf("prow"); b_prow2 = Buf("prow2")
        rows = [("norm1_g", 0, 1), ("conv_b_w", 1, 31), ("conv_b_bias", 32, 1), ("ln_b_g", 33, 1),
                ("ln_b_b", 34, 1), ("norm2_g", 35, 1)]
        for nm, r0, nr in rows:
            P.dma("sp", "prow", lambda nm=nm, r0=r0, nr=nr: nc.sync.dma_start(out=prow[r0:r0 + nr, :], in_=D[nm]),
                  reads=[b_work], writes=[b_prow])
        P.dma("sp", "prow2", lambda: nc.sync.dma_start(out=prow2[0:3, :], in_=D["ffn_conv_w"]),
              reads=[b_work], writes=[b_prow2])
        P.dma("sp", "prow2", lambda: nc.sync.dma_start(out=prow2[3:4, :], in_=D["ffn_conv_b"]),
              reads=[b_work], writes=[b_prow2])
        pv = bank_f(0)[:, 0:288].rearrange("p (c r) -> p c r", r=36)
        for c in range(8):
            P.op("pe", lambda c=c: nc.tensor.transpose(pv[:, c, :], prow[0:36, c * 128:(c + 1) * 128],
                                                       ident_f[0:36, 0:36]),
                 reads=[b_prow, b_idf], writes=[pb[0]])
        P.op("dve", lambda: nc.vector.tensor_copy(out=PARAM1[:], in_=pv), reads=[pb[0]], writes=[b_par1])
        pv2 = bank_f(1)[:, 0:176].rearrange("p (c r) -> p c r", r=4)
        for c in range(44):
            P.op("pe", lambda c=c: nc.tensor.transpose(pv2[:, c, :], prow2[0:4, c * 128:(c + 1) * 128],
                                                       ident_f[0:4, 0:4]),
                 reads=[b_prow2, b_idf], writes=[pb[1]])
        P.op("dve", lambda: nc.vector.tensor_copy(out=PARAM2[:], in_=pv2), reads=[pb[1]], writes=[b_par2])

    b_d2d = Buf("d2d")
    out_bufs = [b_d2d]

    def issue_d2d():
        for src, dst, a, b_, n in (("ck", "kws", 8, 128, 120), ("cv", "vws", 8, 128, 120), ("scb", "cbs", 8, 30, 22)):
            P.dma("sp", "d2d", lambda src=src, dst=dst, a=a, b_=b_, n=n:
                  nc.sync.dma_start(out=D[dst][:, 0:n, :], in_=D[src][:, a:b_, :]), writes=[b_d2d])

    def stage_normT(sg, which):
        stage_fence()
        cv = Carver()
        xts = Slots([cv.f32(1024) for _ in range(2)], "xt")
        xss = Slots([cv.bf(1024) for _ in range(2)], "xs")
        junk = cv.bf(1024)
        grow = 0 if which == 0 else 35
        tiles = [(s, ti) for s in sg for ti in range(s.nt)]
        nt = len(tiles)

        def src_tile(s, ti):
            if which == 0:
                ap, bf = xts.next()
                P.dma("sp", bf.name, lambda ap=ap, s=s, ti=ti: nc.sync.dma_start(out=ap, in_=x_src(s, ti)),
                      reads=[b_work], writes=[bf])
                return ap, bf
            return big_H(s, ti), blkbufs(s)[2][ti]
        for n, (s, ti) in enumerate(tiles):
            ap, bf = src_tile(s, ti)
            P.op("act", lambda ap=ap, n=n: nc.scalar.activation(out=junk, in_=ap, func=AF.Square,
                                                                accum_out=RS[:, which, n:n + 1]),
                 reads=[bf, b_work], writes=[b_junk, b_rs[which]])
        P.op("act", lambda: nc.scalar.activation(out=RS2[:, 0:nt], in_=RS[:, which, 0:nt], func=AF.Sqrt,
                                                 bias=EPS, scale=1.0 / 1024.0),
             reads=[b_rs[which]], writes=[b_rs2])
        P.op("dve", lambda: nc.vector.reciprocal(out=RS[:, which, 0:nt], in_=RS2[:, 0:nt]),
             reads=[b_rs2], writes=[b_rs[which]])
        for n, (s, ti) in enumerate(tiles):
            ap, bf = src_tile(s, ti)
            xs_, bxs = xss.next()
            P.op("act", lambda ap=ap, xs_=xs_, n=n: nc.scalar.activation(
                out=xs_, in_=ap, func=AF.Copy, scale=RS[:, which, n:n + 1]),
                reads=[bf, b_rs[which], b_work], writes=[bxs])
            bk = n % 2
            pT = bank_b(bk).rearrange("p (c t) -> p c t", t=128)
            for c in range(8):
                P.op("pe", lambda c=c, pT=pT, xs_=xs_: nc.tensor.transpose(pT[:, c, :], xs_[:, c * 128:(c + 1) * 128],
                                                                           ident_bf[:]),
                     reads=[bxs, b_idbf], writes=[pb[bk]])
            t0 = s.tok0 + ti * 128
            gb = PARAM1[:, :, grow:grow + 1].to_broadcast([128, 8, 128])
            P.op("dve", lambda pT=pT, t0=t0, gb=gb: nc.vector.tensor_tensor(
                out=XT[:, :, t0:t0 + 128], in0=pT, in1=gb, op=ALU.mult),
                reads=[pb[bk], b_par1], writes=[b_xt[t0 // 128]])

    def stage_S1(sg, slot):
        stage_fence()
        Wq = slot_view(slot, 0, 8, 1024)
        WkD = ARENA[:, slot, 8192:12288].rearrange("p (k h e) -> p k h e", h=4, e=128)
        Wv = slot_view(slot, 12288, 8, 256)
        Wo = slot_view(slot, 14336, 8, 1024)
        bw = slotbuf[slot]
        cv = Carver()
        qT = cv.bf(8 * 256).rearrange("p (c t) -> p c t", t=256); b_qT = Buf("qT")
        kTd = cv.bf(4 * 256).rearrange("p (h t) -> p h t", t=256); b_kTd = Buf("kTd")
        Vaug = cv.bf(2 * 4 * 65).rearrange("p (s h e) -> p s h e", h=4, e=65); b_V = Buf("Vaug")
        pTs = Slots([cv.bf(512).rearrange("p (a b q) -> p a b q", a=2, b=2) for _ in range(4)], "pT")
        attn = Slots([cv.bf(1024) for _ in range(2)], "attn")
        attnT = cv.bf(8 * 256).rearrange("p (c t) -> p c t", t=256); b_attnT = Buf("attnT")
        den = cv.f32(16); b_den = Buf("den")
        rden = cv.f32(16); b_rden = Buf("rden")
        ktok = cv.f32(256); b_ktok = Buf("ktok")
        vtok = cv.f32(256); b_vtok = Buf("vtok")
        kpd = cv.bf(4 * 4 * 128).rearrange("p (r h e) -> p r h e", h=4, e=128); b_kpd = Buf("kpd")
        kpT = cv.bf(4 * 4 * 128).rearrange("p (r h k) -> p r h k", h=4, k=128); b_kpT = Buf("kpT")
        PTpad_raw = cv.bf(4 * 16 * 32)
        PTpad = PTpad_raw.rearrange("p (r h q) -> p r h q", h=16, q=32); b_ptpad = Buf("ptpad")
        Vpa = cv.bf(4 * 4 * 65).rearrange("p (r h e) -> p r h e", h=4, e=65); b_vpa = Buf("vpa")
        PTown_raw = cv.bf(16 * 128)
        PTown = PTown_raw.rearrange("p (h q) -> p h q", q=128); b_ptown = Buf("ptown")
        has_sample = any(s.kind == 's' for s in sg)
        P.op("pool", lambda: nc.gpsimd.memset(Vaug[:], 1.0), reads=[b_work], writes=[b_V])
        if has_sample:
            P.op("pool", lambda: nc.gpsimd.memset(Vpa[:], 1.0), reads=[b_work], writes=[b_vpa])
            P.op("pool", lambda: nc.gpsimd.memset(PTpad[:], 0.0), reads=[b_work], writes=[b_ptpad])
        MM, MS, MO, MT = (0, 1), (2, 3), (4, 5, 6), 7
        mmi = [0]

        def next_mm():
            b = MM[mmi[0] % 2]
            mmi[0] += 1
            return b
        si = [0]

        def normalize_and_transpose(s, ti, at, bat):
            for bi in range(3):
                h0 = bi * 7
                nh = min(7, 16 - h0)
                ov = bank_f(MO[bi])[:, 0:nh * 65].rearrange("p (h e) -> p h e", e=65)
                P.op("dve", lambda ov=ov, h0=h0, nh=nh: nc.vector.tensor_tensor(
                    out=den[:, h0:h0 + nh], in0=ov[:, :, 64], in1=esink[:, h0:h0 + nh], op=ALU.add),
                    reads=[pb[MO[bi]], b_esink, b_work], writes=[b_den])
            P.op("dve", lambda: nc.vector.reciprocal(out=rden, in_=den), reads=[b_den], writes=[b_rden])
            for bi in range(3):
                h0 = bi * 7
                nh = min(7, 16 - h0)
                ov = bank_f(MO[bi])[:, 0:nh * 65].rearrange("p (h e) -> p h e", e=65)
                av = at[:, h0 * 64:(h0 + nh) * 64].rearrange("p (h e) -> p h e", e=64)
                rb = rden[:, h0:h0 + nh].unsqueeze(2).to_broadcast([128, nh, 64])
                P.op("dve", lambda ov=ov, av=av, rb=rb: nc.vector.tensor_tensor(
                    out=av, in0=ov[:, :, 0:64], in1=rb, op=ALU.mult),
                    reads=[pb[MO[bi]], b_rden], writes=[bat])
            pT = bank_b(MT).rearrange("p (c t) -> p c t", t=128)
            for c in range(8):
                P.op("pe", lambda c=c, at=at, pT=pT: nc.tensor.transpose(pT[:, c, :], at[:, c * 128:(c + 1) * 128],
                                                                         ident_bf[:]),
                     reads=[bat, b_idbf], writes=[pb[MT]])
            P.op("act", lambda pT=pT, ti=ti: nc.scalar.copy(out=attnT[:, :, ti * 128:(ti + 1) * 128], in_=pT),
                 reads=[pb[MT]], writes=[b_attnT])

        for s in sg:
            n = s.ntok
            xb = xt_bufs(s)
            A_ap = big_A(s)
            bA, bB, bH = blkbufs(s)
            for half in range(4):
                bk = next_mm()
                for j in range(2):
                    oc = half * 2 + j
                    for kc in range(8):
                        P.op("pe", lambda oc=oc, kc=kc, bk=bk, j=j: nc.tensor.matmul(
                            bank_f(bk)[:, j * 256:j * 256 + n], lhsT=Wq[:, kc, oc * 128:(oc + 1) * 128],
                            rhs=XT[:, kc, s.tok0:s.tok0 + n], start=(kc == 0), stop=(kc == 7)),
                            reads=bw + xb, writes=[pb[bk]])
                src = bank_f(bk).rearrange("p (j t) -> p j t", t=256)[:, :, 0:n]
                P.op("act", lambda src=src, half=half: nc.scalar.copy(out=qT[:, half * 2:half * 2 + 2, 0:n], in_=src),
                     reads=[pb[bk], b_work], writes=[b_qT])
            for half in range(2):
                bk = next_mm()
                for j in range(2):
                    kvh = half * 2 + j
                    for kc in range(8):
                        P.op("pe", lambda kvh=kvh, kc=kc, bk=bk, j=j: nc.tensor.matmul(
                            bank_f(bk)[:, j * 256:j * 256 + n], lhsT=WkD[:, kc, kvh, :],
                            rhs=XT[:, kc, s.tok0:s.tok0 + n], start=(kc == 0), stop=(kc == 7)),
                            reads=bw + xb, writes=[pb[bk]])
                src = bank_f(bk).rearrange("p (j t) -> p j t", t=256)[:, :, 0:n]
                P.op("act", lambda src=src, half=half: nc.scalar.copy(out=kTd[:, half * 2:half * 2 + 2, 0:n], in_=src),
                     reads=[pb[bk], b_work], writes=[b_kTd])
            bk = next_mm()
            for ti in range(s.nt):
                for kc in range(8):
                    P.op("pe", lambda ti=ti, kc=kc, bk=bk: nc.tensor.matmul(
                        bank_f(bk)[:, ti * 256:(ti + 1) * 256],
                        lhsT=XT[:, kc, s.tok0 + ti * 128:s.tok0 + (ti + 1) * 128], rhs=Wv[:, kc, :],
                        start=(kc == 0), stop=(kc == 7)),
                        reads=bw + xb, writes=[pb[bk]])
            vsrc = bank_f(bk)[:, 0:s.nt * 256].rearrange("p (s h e) -> p s h e", h=4, e=64)
            P.op("act", lambda vsrc=vsrc: nc.scalar.copy(out=Vaug[:, 0:s.nt, :, 0:64], in_=vsrc),
                 reads=[pb[bk], b_work], writes=[b_V])
            if os.environ.get("KS1STOP", "") == "proj":
                return
            want_out = s.last or s.kind == 's'
            if want_out:
                ti_o = s.nt - 1
                P.op("dve", lambda ti_o=ti_o, bk=bk: nc.vector.tensor_copy(
                    out=vtok, in_=bank_f(bk)[:, ti_o * 256:(ti_o + 1) * 256]),
                    reads=[pb[bk], b_work, b_V], writes=[b_vtok])
                bk2 = next_mm()
                for kvh in range(4):
                    for kc in range(8):
                        P.op("pe", lambda kc=kc, kvh=kvh, bk2=bk2, ti_o=ti_o: nc.tensor.matmul(
                            bank_f(bk2)[:, kvh * 64:(kvh + 1) * 64],
                            lhsT=XT[:, kc, s.tok0 + ti_o * 128:s.tok0 + (ti_o + 1) * 128],
                            rhs=WkD[:, kc, kvh, 0:64], start=(kc == 0), stop=(kc == 7)),
                            reads=bw + xb, writes=[pb[bk2]])
                P.op("dve", lambda bk2=bk2: nc.vector.tensor_copy(out=ktok, in_=bank_f(bk2)[:, 0:256]),
                     reads=[pb[bk2], b_work], writes=[b_ktok])
                if s.kind == 'p':
                    P.dma("sp", "ktok", lambda: nc.sync.dma_start(out=D["kwp"], in_=ktok), reads=[b_ktok])
                    P.dma("sp", "vtok", lambda: nc.sync.dma_start(out=D["vwp"], in_=vtok), reads=[b_vtok])
                else:
                    for b in range(16):
                        P.dma("sp", "ktok", lambda b=b: nc.sync.dma_start(out=D["kws"][b, 120:128, :],
                                                                          in_=ktok[b * 8:(b + 1) * 8, :]), reads=[b_ktok])
                        P.dma("sp", "vtok", lambda b=b: nc.sync.dma_start(out=D["vws"][b, 120:128, :],
                                                                          in_=vtok[b * 8:(b + 1) * 8, :]), reads=[b_vtok])
                out_bufs.extend([b_ktok, b_vtok])
            if os.environ.get("KS1STOP", "") == "prompt" and s.kind == 's':
                return
            if s.kind == 'p':
                for ti in range(s.nt):
                    noprev = s.first and ti == 0
                    at, bat = attn.next()

                    def scores(g, ti=ti, noprev=noprev):
                        kvh = g
                        for hh in range(2):
                            bS = MS[hh]
                            Sv = bank_f(bS).rearrange("p (j b q) -> p j b q", j=2, b=2)
                            ps_ = slice(hh * 64, hh * 64 + 64)
                            for j in range(2):
                                hp = 2 * g + j
                                rhs = qT[ps_, hp, ti * 128:(ti + 1) * 128]
                                if not noprev:
                                    if ti == 0:
                                        lk, rb_ = KC[ps_, kvh, :], [b_kc]
                                    else:
                                        lk, rb_ = kTd[ps_, kvh, 0:128], [b_kTd]
                                    P.op("pe", lambda Sv=Sv, j=j, lk=lk, rhs=rhs: nc.tensor.matmul(
                                        Sv[:, j, 0, :], lhsT=lk, rhs=rhs, start=True, stop=True),
                                        reads=rb_ + [b_qT], writes=[pb[bS]])
                                P.op("pe", lambda Sv=Sv, j=j, ps_=ps_, kvh=kvh, rhs=rhs: nc.tensor.matmul(
                                    Sv[:, j, 1, :], lhsT=kTd[ps_, kvh, ti * 128:(ti + 1) * 128], rhs=rhs,
                                    start=True, stop=True),
                                    reads=[b_kTd, b_qT], writes=[pb[bS]])

                    def softmax_num(g, noprev=noprev):
                        res = []
                        for hh in range(2):
                            bS = MS[hh]
                            Sv = bank_f(bS).rearrange("p (j b q) -> p j b q", j=2, b=2)
                            pT, bpT = pTs.next()
                            if noprev:
                                P.op("act", lambda pT=pT, Sv=Sv: nc.scalar.activation(
                                    out=pT[:, :, 1, :], in_=Sv[:, :, 1, :], func=AF.Exp, scale=0.125),
                                    reads=[pb[bS], b_work], writes=[bpT])
                                P.op("dve", lambda pT=pT: nc.vector.tensor_tensor(
                                    out=pT[:, :, 1, :], in0=pT[:, :, 1, :], in1=MASK2[:, :, 1, :], op=ALU.mult),
                                    reads=[bpT, b_mask2], writes=[bpT])
                            else:
                                P.op("act", lambda pT=pT, Sv=Sv: nc.scalar.activation(
                                    out=pT, in_=Sv, func=AF.Exp, scale=0.125),
                                    reads=[pb[bS], b_work], writes=[bpT])
                                P.op("dve", lambda pT=pT: nc.vector.tensor_tensor(
                                    out=pT, in0=pT, in1=MASK2[:], op=ALU.mult),
                                    reads=[bpT, b_mask2], writes=[bpT])
                            res.append((pT, bpT))
                        return res

                    def pv(g, pts, ti=ti, noprev=noprev):
                        kvh = g
                        for hh in range(2):
                            pT, bpT = pts[hh]
                            for j in range(2):
                                h = 2 * (2 * g + j) + hh
                                bo = MO[h // 7]
                                oo = bank_f(bo)[:, (h % 7) * 65:(h % 7) * 65 + 65]
                                if not noprev:
                                    if ti == 0:
                                        vp, rb_ = VC[:, kvh, :], [b_vc]
                                    else:
                                        vp, rb_ = Vaug[:, 0, kvh, :], [b_V]
                                    P.op("pe", lambda oo=oo, pT=pT, j=j, vp=vp: nc.tensor.matmul(
                                        oo, lhsT=pT[:, j, 0, :], rhs=vp, start=True, stop=False),
                                        reads=[bpT] + rb_, writes=[pb[bo]])
                                P.op("pe", lambda oo=oo, pT=pT, j=j, kvh=kvh: nc.tensor.matmul(
                                    oo, lhsT=pT[:, j, 1, :], rhs=Vaug[:, ti, kvh, :], start=noprev, stop=True),
                                    reads=[bpT, b_V], writes=[pb[bo]])
                    scores(0)
                    for g in range(4):
                        pts = softmax_num(g)
                        if g + 1 < 4:
                            scores(g + 1)
                        pv(g, pts)
                    normalize_and_transpose(s, ti, at, bat)
                P.op("pool", lambda: nc.gpsimd.tensor_copy(out=KC[:], in_=kTd[:, :, 128:256]),
                     reads=[b_kTd], writes=[b_kc])
                P.op("act", lambda: nc.scalar.copy(out=VC[:].rearrange("p h e -> p (h e)"),
                                                   in_=Vaug[:, 1, :, :].rearrange("p h e -> p (h e)")),
                     reads=[b_V], writes=[b_vc])
            else:
                PTv = PTown_raw.rearrange("p (hp hh q) -> p hp hh q", hh=2, q=128)
                for r8 in range(2):
                    for hh in range(2):
                        bS = MS[hh]
                        Sv = bank_f(bS).rearrange("p (j q) -> p j q", q=128)
                        ps_ = slice(hh * 64, hh * 64 + 64)
                        for j in range(4):
                            hp = 4 * r8 + j
                            kvh = hp // 2
                            P.op("pe", lambda Sv=Sv, j=j, ps_=ps_, kvh=kvh, hp=hp: nc.tensor.matmul(
                                Sv[:, j, :], lhsT=kTd[ps_, kvh, 0:128], rhs=qT[ps_, hp, 0:128], start=True, stop=True),
                                reads=[b_kTd, b_qT], writes=[pb[bS]])
                        dsto = PTv[:, 4 * r8:4 * r8 + 4, hh, :]
                        P.op("act", lambda Sv=Sv, dsto=dsto: nc.scalar.activation(
                            out=dsto, in_=Sv, func=AF.Exp, scale=0.125),
                            reads=[pb[bS], b_work], writes=[b_ptown])
                        P.op("pool", lambda dsto=dsto: nc.gpsimd.tensor_tensor(
                            out=dsto, in0=dsto, in1=MASKS4[:], op=ALU.mult),
                            reads=[b_ptown, b_masks4], writes=[b_ptown])
                at, bat = attn.next()
                PTpv = PTpad_raw.rearrange("p (r hp hh q) -> p r hp hh q", hp=8, hh=2, q=32)
                for bb in range(4):
                    for r in range(4):
                        ksrc = D["ck"][4 * bb + r].rearrange("k (h d) -> k h d", d=64)
                        vsrc2 = D["cv"][4 * bb + r].rearrange("k (h d) -> k h d", d=64)
                        P.dma("pool", "kpd", lambda ksrc=ksrc, r=r: nc.gpsimd.dma_start(out=kpd[:, r, :, 0:64], in_=ksrc),
                              reads=[b_work], writes=[b_kpd])
                        P.dma("pool", "kpd", lambda ksrc=ksrc, r=r: nc.gpsimd.dma_start(out=kpd[:, r, :, 64:128], in_=ksrc),
                              reads=[b_work], writes=[b_kpd])
                        P.dma("pool", "vpa", lambda vsrc2=vsrc2, r=r: nc.gpsimd.dma_start(out=Vpa[:, r, :, 0:64], in_=vsrc2),
                              reads=[b_work], writes=[b_vpa])
                    for half in range(2):
                        pT = bank_b(MT).rearrange("p (c t) -> p c t", t=128)
                        for j in range(8):
                            r, kvh = (half * 8 + j) // 4, (half * 8 + j) % 4
                            P.op("pe", lambda pT=pT, j=j, r=r, kvh=kvh: nc.tensor.transpose(
                                pT[:, j, :], kpd[:, r, kvh, :], ident_bf[:]),
                                reads=[b_kpd, b_idbf], writes=[pb[MT]])
                        dst = kpT[:, half * 2:half * 2 + 2, :, :].rearrange("p r h k -> p (r h) k")
                        P.op("act", lambda pT=pT, dst=dst: nc.scalar.copy(out=dst, in_=pT),
                             reads=[pb[MT]], writes=[b_kpT])
                    for hh in range(2):
                        bS = MS[hh]
                        Sv = bank_f(bS)[:, 0:256].rearrange("p (r hp i) -> p r hp i", r=4, i=8)
                        ps_ = slice(hh * 64, hh * 64 + 64)
                        for r in range(4):
                            b = 4 * bb + r
                            for hp in range(8):
                                kvh = hp // 2
                                P.op("pe", lambda Sv=Sv, r=r, hp=hp, ps_=ps_, kvh=kvh, b=b: nc.tensor.matmul(
                                    Sv[:, r, hp, :], lhsT=kpT[ps_, r, kvh, :], rhs=qT[ps_, hp, b * 8:(b + 1) * 8],
                                    start=True, stop=True),
                                    reads=[b_kpT, b_qT], writes=[pb[bS]])
                        for r in range(4):
                            dstp = PTpv[:, r, :, hh, r * 8:(r + 1) * 8]
                            P.op("act", lambda Sv=Sv, r=r, dstp=dstp: nc.scalar.activation(
                                out=dstp, in_=Sv[:, r, :, :], func=AF.Exp, scale=0.125),
                                reads=[pb[bS]], writes=[b_ptpad])
                    for r in range(4):
                        dstp = PTpad[:, r, :, r * 8:(r + 1) * 8]
                        mb = MASKP[:].unsqueeze(1).to_broadcast([128, 16, 8])
                        P.op("pool", lambda dstp=dstp, mb=mb: nc.gpsimd.tensor_tensor(
                            out=dstp, in0=dstp, in1=mb, op=ALU.mult),
                            reads=[b_ptpad, b_maskp], writes=[b_ptpad])
                    for h in range(16):
                        kvh = h // 4
                        bo = MO[h // 7]
                        oo = bank_f(bo)[32 * bb:32 * bb + 32, (h % 7) * 65:(h % 7) * 65 + 65]
                        for r in range(4):
                            P.op("pe", lambda oo=oo, r=r, h=h, kvh=kvh, bb=bb: nc.tensor.matmul(
                                oo, lhsT=PTpad[:, r, h, :], rhs=Vpa[:, r, kvh, :], start=(r == 0), stop=False,
                                tile_position=(0, 32 * bb)),
                                reads=[b_ptpad, b_vpa], writes=[pb[bo]])
                        P.op("pe", lambda oo=oo, h=h, kvh=kvh, bb=bb: nc.tensor.matmul(
                            oo, lhsT=PTown[:, h, 32 * bb:32 * bb + 32], rhs=Vaug[:, 0, kvh, :], start=False, stop=True,
                            tile_position=(0, 32 * bb)),
                            reads=[b_ptown, b_V], writes=[pb[bo]])
                normalize_and_transpose(s, 0, at, bat)
            if os.environ.get("KS1STOP", "") == "attn":
                return
            for half in range(4):
                bk = next_mm()
                for j in range(2):
                    oc = half * 2 + j
                    for kc in range(8):
                        P.op("pe", lambda oc=oc, kc=kc, bk=bk, j=j: nc.tensor.matmul(
                            bank_f(bk)[:, j * 256:j * 256 + n], lhsT=Wo[:, kc, oc * 128:(oc + 1) * 128],
                            rhs=attnT[:, kc, 0:n], start=(kc == 0), stop=(kc == 7)),
                            reads=bw + [b_attnT], writes=[pb[bk]])
                src = bank_f(bk).rearrange("p (j t) -> p j t", t=256)[:, :, 0:n]
                P.op("dve", lambda src=src, half=half, A_ap=A_ap: nc.vector.tensor_copy(
                    out=A_ap[:, half * 2:half * 2 + 2, :], in_=src),
                    reads=[pb[bk]], writes=[bA] + bH)

    def stage_S2(sg, slot):
        stage_fence()
        Wga = slot_view(slot, 0, 8, 1024)
        Wgb = slot_view(slot, 8192, 8, 1024)
        Wpb = slot_view(slot, 16384, 8, 1024)
        bw = slotbuf[slot]
        state = {"kind": None}

        def alloc(kind):
            cv = Carver()
            nn = 256 if kind == 'p' else 128
            d = {}
            d["UEXTraw"] = cv.f32(8 * 286 if kind == 'p' else 8 * 16 * 38)
            d["ACC"] = cv.f32(8 * nn)
            d["AB16"] = cv.bf(8 * nn)
            d["zT"] = cv.bf(8 * nn)
            d["sgs"] = Slots([cv.f32(nn) for _ in range(2)], "sgt")
            d["mean"] = cv.f32(nn)
            d["m2"] = cv.f32(nn)
            d["rstd"] = cv.f32(nn)
            d["utok"] = cv.f32(1024)
            if kind == 'p':
                d["pacc"] = Slots([cv.f32(256) for _ in range(2)], "pacc")
                d["diag"] = Slots([cv.f32(128) for _ in range(4)], "diag")
            if kind == 's':
                d["stg"] = Slots([cv.f32(1024)], "scbstg")
            d["bufs"] = {k: Buf(k + kind) for k in ("uext", "acc", "ab16", "zT", "mean", "m2", "rstdln", "utok")}
            return d
        NPOOL = 0
        NPE = 11
        MG, MST, MP = (0, 1, 2), 3, (4, 5)
        b_ca = [Buf("convacc%d" % i) for i in range(4)]
        gi = [0]
        pi = [0]
        for s in sg:
            n = s.ntok
            xb = xt_bufs(s)
            bA, bB, bH = blkbufs(s)
            B_ap = big_B(s)
            if state["kind"] != s.kind:
                if state["kind"] is not None:
                    P.barrier(out_bufs)
                state["kind"] = s.kind
                state["d"] = alloc(s.kind)
            d = state["d"]
            UEXTraw, ACC, AB16, zT, sgs = d["UEXTraw"], d["ACC"], d["AB16"], d["zT"], d["sgs"]
            mean, m2, rstd, utok = d["mean"], d["m2"], d["rstd"], d["utok"]
            stg = d.get("stg")
            bb_ = d["bufs"]
            b_u, b_acc, b_ab16, b_zT = bb_["uext"], bb_["acc"], bb_["ab16"], bb_["zT"]
            b_mean, b_m2, b_rstd, b_utok = bb_["mean"], bb_["m2"], bb_["rstdln"], bb_["utok"]
            if s.kind == 'p':
                UE = UEXTraw[:, 0:8 * 286].rearrange("p (c t) -> p c t", t=286)

                def ucur(oc, UE=UE):
                    return UE[:, oc, 30:286]

                def utap(oc, k, UE=UE):
                    return UE[:, oc, k:k + 256]

                def accv(oc):
                    return ACC[:, oc * 256:(oc + 1) * 256]
                if s.first:
                    P.op("pool", lambda UE=UE: nc.gpsimd.memset(UE[:, :, 0:30], 0.0), reads=[b_work], writes=[b_u])
                else:
                    P.op("pool", lambda UE=UE: nc.gpsimd.tensor_copy(out=UE[:, :, 0:30], in_=UC[:]),
                         reads=[b_uc, b_work], writes=[b_u])
            else:
                UE = UEXTraw[:, 0:8 * 16 * 38].rearrange("p (c b t) -> p c b t", b=16, t=38)

                def ucur(oc, UE=UE):
                    return UE[:, oc, :, 30:38]

                def utap(oc, k, UE=UE):
                    return UE[:, oc, :, k:k + 8]

                def accv(oc):
                    return ACC[:, oc * 128:(oc + 1) * 128].rearrange("p (b i) -> p b i", i=8)
                for c4 in range(4):
                    st_, bst = stg.next()
                    src = D["scb"][4 * c4:4 * c4 + 4].rearrange("b r f -> (b r) f")
                    P.dma("sp", bst.name, lambda st_=st_, src=src: nc.sync.dma_start(out=st_[0:120, :], in_=src),
                          reads=[b_work], writes=[bst])
                    for half in range(2):
                        bk = MP[pi[0] % 2]
                        pi[0] += 1
                        pv = bank_f(bk)[:, 0:480].rearrange("p (c t) -> p c t", t=120)
                        for j in range(4):
                            oc = half * 4 + j
                            P.op("pe", lambda pv=pv, j=j, oc=oc, st_=st_: nc.tensor.transpose(
                                pv[:, j, :], st_[0:120, oc * 128:(oc + 1) * 128], ident_f[0:120, 0:120]),
                                reads=[bst, b_idf], writes=[pb[bk]])
                        dst = UE[:, half * 4:half * 4 + 4, 4 * c4:4 * c4 + 4, 0:30]
                        srcv = bank_f(bk)[:, 0:480].rearrange("p (c b t) -> p c b t", b=4, t=30)
                        P.op("act", lambda dst=dst, srcv=srcv: nc.scalar.copy(out=dst, in_=srcv),
                             reads=[pb[bk], b_work], writes=[b_u])
            acc_all = ACC[:, 0:8 * n]
            for oc in range(8):
                bk = MG[gi[0] % 3]
                gi[0] += 1
                for j, Wx in enumerate((Wga, Wgb)):
                    for kc in range(8):
                        P.op("pe", lambda oc=oc, kc=kc, bk=bk, j=j, Wx=Wx: nc.tensor.matmul(
                            bank_f(bk)[:, j * 256:j * 256 + n], lhsT=Wx[:, kc, oc * 128:(oc + 1) * 128],
                            rhs=XT[:, kc, s.tok0:s.tok0 + n], start=(kc == 0), stop=(kc == 7)),
                            reads=bw + xb, writes=[pb[bk]])
                sg_, bsg = sgs.next()
                P.op("act", lambda sg_=sg_, bk=bk: nc.scalar.activation(
                    out=sg_[:, 0:n], in_=bank_f(bk)[:, 256:256 + n], func=AF.Sigmoid),
                    reads=[pb[bk], b_work], writes=[bsg])
                if s.kind == 'p':
                    ga_v, sg_v = bank_f(bk)[:, 0:n], sg_[:, 0:n]
                else:
                    ga_v = bank_f(bk)[:, 0:n].rearrange("p (b i) -> p b i", i=8)
                    sg_v = sg_[:, 0:n].rearrange("p (b i) -> p b i", i=8)
                if s.kind == 'p':
                    P.op("dve", lambda oc=oc, ga_v=ga_v, sg_v=sg_v, ucur=ucur: nc.vector.tensor_tensor(
                        out=ucur(oc), in0=ga_v, in1=sg_v, op=ALU.mult),
                        reads=[pb[bk], bsg], writes=[b_u])
                else:
                    P.op("dve", lambda oc=oc, ga_v=ga_v, sg_v=sg_v: nc.vector.tensor_tensor(
                        out=ACC[:, oc * 128:(oc + 1) * 128].rearrange("p (b i) -> p b i", i=8), in0=ga_v, in1=sg_v,
                        op=ALU.mult),
                        reads=[pb[bk], bsg], writes=[b_acc])
            if s.kind == 's':
                P.op("pool", lambda UE=UE: nc.gpsimd.tensor_copy(
                    out=UE[:, :, :, 30:38], in_=ACC[:, 0:1024].rearrange("p (c b i) -> p c b i", c=8, i=8)),
                    reads=[b_acc], writes=[b_u])
            if s.last or s.kind == 's':
                for half in range(2):
                    bk = MP[pi[0] % 2]
                    pi[0] += 1
                    pv = bank_f(bk).rearrange("p (c t) -> p c t", t=128)
                    for j in range(4):
                        oc = half * 4 + j
                        src = UE[:, oc, 158:286] if s.kind == 'p' else ACC[:, oc * 128:(oc + 1) * 128]
                        P.op("pe", lambda pv=pv, j=j, src=src: nc.tensor.transpose(pv[:, j, :], src, ident_f[:]),
                             reads=[b_u, b_acc, b_idf], writes=[pb[bk]])
                    P.op("act", lambda half=half, bk=bk: nc.scalar.copy(
                        out=utok[:, half * 512:(half + 1) * 512], in_=bank_f(bk)),
                        reads=[pb[bk], b_work], writes=[b_utok])
                if s.kind == 'p':
                    P.dma("sp", "utok", lambda: nc.sync.dma_start(out=D["cbp"], in_=utok[98:128, :]), reads=[b_utok])
                else:
                    for b in range(16):
                        P.dma("sp", "utok", lambda b=b: nc.sync.dma_start(
                            out=D["cbs"][b, 22:30, :], in_=utok[b * 8:(b + 1) * 8, :]), reads=[b_utok])
                out_bufs.append(b_utok)
            for ocp in range(4):
                accs = []
                for oc in (2 * ocp, 2 * ocp + 1):
                    cab = 6 + oc % 2
                    if s.kind == 'p':
                        accP = bank_f(cab)[:, 0:256]
                    else:
                        accP = bank_f(cab)[:, 0:128].rearrange("p (b i) -> p b i", i=8)
                    accs.append((oc, accP, pb[cab]))
                    P.op("act", lambda oc=oc, accP=accP, ucur=ucur: nc.scalar.activation(
                        out=accP, in_=ucur(oc), func=AF.Identity,
                        scale=PARAM1[:, oc, 31:32], bias=PARAM1[:, oc, 32:33]),
                        reads=[b_u, b_par1, b_work], writes=[pb[cab]])
                npool = NPOOL if s.kind == 'p' else 0
                paccs = []
                for oc, accP, bca in accs:
                    if npool:
                        pa, bpa = d["pacc"].next()
                        pt_, bpt = d["ptmp"].next()
                        for k in range(npool):
                            if k == 0:
                                P.op("pool", lambda oc=oc, k=k, pa=pa, utap=utap: nc.gpsimd.tensor_scalar(
                                    out=pa, in0=utap(oc, k), scalar1=PARAM1[:, oc, 1 + k:2 + k], scalar2=None,
                                    op0=ALU.mult), reads=[b_u, b_par1, b_work], writes=[bpa])
                            else:
                                P.op("pool", lambda oc=oc, k=k, pt_=pt_, utap=utap: nc.gpsimd.tensor_scalar(
                                    out=pt_, in0=utap(oc, k), scalar1=PARAM1[:, oc, 1 + k:2 + k], scalar2=None,
                                    op0=ALU.mult), reads=[b_u, b_par1, b_work], writes=[bpt])
                                P.op("pool", lambda pa=pa, pt_=pt_: nc.gpsimd.tensor_tensor(
                                    out=pa, in0=pa, in1=pt_, op=ALU.add), reads=[bpa, bpt], writes=[bpa])
                        paccs.append((pa, bpa))
                npe = NPE if s.kind == 'p' else 0
                pes = []
                for oc, accP, bca in accs:
                    if not npe:
                        break
                    pbk = MP[oc % 2]
                    pep = bank_f(pbk)[:, 0:256]
                    for k in range(npe):
                        dg, bdg = d["diag"].next()
                        P.op("act", lambda dg=dg, oc=oc, k=k: nc.scalar.activation(
                            out=dg, in_=ident_f[:], func=AF.Copy, scale=PARAM1[:, oc, 1 + k:2 + k]),
                            reads=[b_idf, b_par1, b_work], writes=[bdg])
                        P.op("pe", lambda dg=dg, oc=oc, k=k, pep=pep, utap=utap: nc.tensor.matmul(
                            pep, lhsT=dg, rhs=utap(oc, k), start=(k == 0), stop=(k == npe - 1)),
                            reads=[bdg, b_u], writes=[pb[pbk]])
                    pes.append((pep, pbk))
                for k in range(max(npool, npe), 30):
                    for oc, accP, bca in accs:
                        P.op("dve", lambda oc=oc, k=k, accP=accP, utap=utap: nc.vector.scalar_tensor_tensor(
                            out=accP, in0=utap(oc, k), scalar=PARAM1[:, oc, 1 + k:2 + k], in1=accP,
                            op0=ALU.mult, op1=ALU.add),
                            reads=[b_u, bca, b_par1], writes=[bca])
                if npe:
                    for (oc, accP, bca), (pep, pbk) in zip(accs, pes):
                        pa, bpa = d["pacc"].next()
                        P.op("act", lambda pa=pa, pep=pep: nc.scalar.copy(out=pa, in_=pep),
                             reads=[pb[pbk], b_work], writes=[bpa])
                        P.op("dve", lambda accP=accP, pa=pa: nc.vector.tensor_tensor(
                            out=accP, in0=pa, in1=accP, op=ALU.add), reads=[bpa, bca], writes=[bca])
                if npool:
                    for (oc, accP, bca), (pa, bpa) in zip(accs, paccs):
                        P.op("dve", lambda accP=accP, pa=pa: nc.vector.tensor_tensor(
                            out=accP, in0=pa, in1=accP, op=ALU.add), reads=[bpa, bca], writes=[bca])
                for oc, accP, bca in accs:
                    P.op("act", lambda oc=oc, accP=accP, accv=accv: nc.scalar.copy(out=accv(oc), in_=accP),
                         reads=[bca, b_work], writes=[b_acc])
                    ab_v = AB16[:, oc * n:(oc + 1) * n]
                    if s.kind == 's':
                        ab_v = ab_v.rearrange("p (b i) -> p b i", i=8)
                    P.op("act", lambda accP=accP, ab_v=ab_v: nc.scalar.copy(out=ab_v, in_=accP),
                         reads=[bca, b_work], writes=[b_ab16])
            if s.kind == 'p':
                P.op("pool", lambda UE=UE: nc.gpsimd.tensor_copy(out=UC[:], in_=UE[:, :, 256:286]),
                     reads=[b_u], writes=[b_uc])
            for oc in range(8):
                P.op("pe", lambda oc=oc: nc.tensor.matmul(
                    bank_f(MST)[:, 0:n], lhsT=ones_bf[:], rhs=AB16[:, oc * n:(oc + 1) * n],
                    start=(oc == 0), stop=(oc == 7)),
                    reads=[b_ab16, b_ones], writes=[pb[MST]])
            P.op("act", lambda: nc.scalar.copy(out=mean[:, 0:n], in_=bank_f(MST)[:, 0:n]),
                 reads=[pb[MST], b_work], writes=[b_mean])
            P.op("act", lambda acc_all=acc_all: nc.scalar.activation(out=AB16[:, 0:8 * n], in_=acc_all, func=AF.Square),
                 reads=[b_acc], writes=[b_ab16])
            for oc in range(8):
                P.op("pe", lambda oc=oc: nc.tensor.matmul(
                    bank_f(MST)[:, 256:256 + n], lhsT=ones_bf[:], rhs=AB16[:, oc * n:(oc + 1) * n],
                    start=(oc == 0), stop=(oc == 7)),
                    reads=[b_ab16, b_ones], writes=[pb[MST]])
            P.op("dve", lambda: nc.vector.tensor_tensor(out=m2[:, 0:n], in0=mean[:, 0:n], in1=mean[:, 0:n], op=ALU.mult),
                 reads=[b_mean, b_work], writes=[b_m2])
            P.op("dve", lambda: nc.vector.tensor_tensor(out=m2[:, 0:n], in0=bank_f(MST)[:, 256:256 + n], in1=m2[:, 0:n],
                                                        op=ALU.subtract),
                 reads=[pb[MST], b_m2], writes=[b_m2])
            P.op("act", lambda: nc.scalar.activation(out=m2[:, 0:n], in_=m2[:, 0:n], func=AF.Sqrt, bias=EPS, scale=1.0),
                 reads=[b_m2], writes=[b_m2])
            P.op("dve", lambda: nc.vector.reciprocal(out=rstd[:, 0:n], in_=m2[:, 0:n]),
                 reads=[b_m2, b_work], writes=[b_rstd])
            a3 = acc_all.rearrange("p (c t) -> p c t", t=n)
            P.op("dve", lambda a3=a3: nc.vector.tensor_tensor(
                out=a3, in0=a3, in1=mean[:, 0:n].unsqueeze(1).to_broadcast([128, 8, n]), op=ALU.subtract),
                reads=[b_acc, b_mean], writes=[b_acc])
            P.op("dve", lambda a3=a3: nc.vector.tensor_tensor(
                out=a3, in0=a3, in1=rstd[:, 0:n].unsqueeze(1).to_broadcast([128, 8, n]), op=ALU.mult),
                reads=[b_acc, b_rstd], writes=[b_acc])
            for oc in range(8):
                P.op("act", lambda oc=oc: nc.scalar.activation(
                    out=zT[:, oc * n:(oc + 1) * n], in_=ACC[:, oc * n:(oc + 1) * n], func=AF.Silu,
                    scale=PARAM1[:, oc, 33:34], bias=PARAM1[:, oc, 34:35]),
                    reads=[b_acc, b_par1, b_work], writes=[b_zT])
            for half in range(4):
                bk = MP[pi[0] % 2]
                pi[0] += 1
                for j in range(2):
                    oc = half * 2 + j
                    for kc in range(8):
                        P.op("pe", lambda oc=oc, kc=kc, bk=bk, j=j: nc.tensor.matmul(
                            bank_f(bk)[:, j * 256:j * 256 + n], lhsT=Wpb[:, kc, oc * 128:(oc + 1) * 128],
                            rhs=zT[:, kc * n:(kc + 1) * n], start=(kc == 0), stop=(kc == 7)),
                            reads=bw + [b_zT], writes=[pb[bk]])
                src = bank_f(bk).rearrange("p (j t) -> p j t", t=256)[:, :, 0:n]
                P.op("act", lambda src=src, half=half, B_ap=B_ap: nc.scalar.copy(
                    out=B_ap[:, half * 2:half * 2 + 2, :], in_=src),
                    reads=[pb[bk]], writes=[bB] + bH)

    def stage_S3(sg, slot):
        stage_fence()
        Wa = slot_view(slot, 0, 8, 1024)
        Wb = slot_view(slot, 8192, 8, 1024)
        Wout = slot_view(slot, 16384, 8, 1024)
        bw = slotbuf[slot]
        cv = Carver()
        sgs = Slots([cv.f32(512).rearrange("p (j t) -> p j t", t=256) for _ in range(2)], "sg3")
        m1s = Slots([cv.f32(256) for _ in range(2)], "m1")
        m2s = Slots([cv.f32(256) for _ in range(2)], "m2")
        mT = cv.bf(8 * 256); b_mT = Buf("mT")
        xts = Slots([cv.f32(1024) for _ in range(2)], "xt3")
        MG, MH = (0, 1, 2), (3, 4, 5, 6)
        gi = [0]
        hi = [0]
        for s in sg:
            n = s.ntok
            xb = xt_bufs(s)
            bA, bB, bH = blkbufs(s)
            A_ap, B_ap = big_A(s), big_B(s)
            for oc in range(8):
                bk = MG[gi[0] % 3]
                gi[0] += 1
                for j, Wx in enumerate((Wa, Wb)):
                    for kc in range(8):
                        P.op("pe", lambda oc=oc, kc=kc, bk=bk, j=j, Wx=Wx: nc.tensor.matmul(
                            bank_f(bk)[:, j * 256:j * 256 + n], lhsT=Wx[:, kc, oc * 128:(oc + 1) * 128],
                            rhs=XT[:, kc, s.tok0:s.tok0 + n], start=(kc == 0), stop=(kc == 7)),
                            reads=bw + xb, writes=[pb[bk]])
                sg_, bsg = sgs.next()
                src = bank_f(bk).rearrange("p (j t) -> p j t", t=256)[:, :, 0:n]
                P.op("act", lambda sg_=sg_, src=src: nc.scalar.activation(out=sg_[:, :, 0:n], in_=src, func=AF.Sigmoid),
                     reads=[pb[bk], b_work], writes=[bsg])
                m1, bm1 = m1s.next()
                m2_, bm2 = m2s.next()
                P.op("pool", lambda m1=m1, sg_=sg_, oc=oc, A_ap=A_ap: nc.gpsimd.tensor_tensor(
                    out=m1[:, 0:n], in0=sg_[:, 0, 0:n], in1=A_ap[:, oc, :], op=ALU.mult),
                    reads=[bsg, bA, b_work], writes=[bm1])
                P.op("pool", lambda m2_=m2_, sg_=sg_, oc=oc, B_ap=B_ap: nc.gpsimd.tensor_tensor(
                    out=m2_[:, 0:n], in0=sg_[:, 1, 0:n], in1=B_ap[:, oc, :], op=ALU.mult),
                    reads=[bsg, bB, b_work], writes=[bm2])
                P.op("dve", lambda m1=m1, m2_=m2_, oc=oc: nc.vector.tensor_tensor(
                    out=mT[:, oc * n:(oc + 1) * n], in0=m1[:, 0:n], in1=m2_[:, 0:n], op=ALU.add),
                    reads=[bm1, bm2, b_work], writes=[b_mT])
            for ti in range(s.nt):
                xt_, bxt = xts.next()
                P.dma("sp", bxt.name, lambda xt_=xt_, s=s, ti=ti: nc.sync.dma_start(out=xt_, in_=x_src(s, ti)),
                      reads=[b_work], writes=[bxt])
                H = big_H(s, ti)
                for hf in range(2):
                    bk = MH[hi[0] % 4]
                    hi[0] += 1
                    for kc in range(8):
                        P.op("pe", lambda kc=kc, bk=bk, ti=ti, hf=hf: nc.tensor.matmul(
                            bank_f(bk), lhsT=mT[:, kc * n + ti * 128:kc * n + (ti + 1) * 128],
                            rhs=Wout[:, kc, hf * 512:(hf + 1) * 512], start=(kc == 0), stop=(kc == 7)),
                            reads=bw + [b_mT], writes=[pb[bk]])
                    P.op("dve", lambda H=H, bk=bk, hf=hf, xt_=xt_: nc.vector.tensor_tensor(
                        out=H[:, hf * 512:(hf + 1) * 512], in0=bank_f(bk), in1=xt_[:, hf * 512:(hf + 1) * 512],
                        op=ALU.add),
                        reads=[pb[bk], bxt], writes=[bH[ti], bA, bB])

    def stage_F(sg, slot, fg):
        stage_fence()
        j0, j1 = FG[fg]
        nch = j1 - j0
        Wg = slot_view(slot, 0, 8, nch * 128)
        Wv_ = slot_view(slot, 8192, 8, nch * 128)
        Wdn = ARENA[:, slot, 16384:16384 + nch * 1024].rearrange("p (j m) -> p j m", m=1024)
        bw = slotbuf[slot]
        cv = Carver()
        upxs = Slots([cv.f32(2 * 258) for _ in range(4)], "upx")
        cgs = Slots([cv.f32(512).rearrange("p (g t) -> p g t", t=256) for _ in range(4)], "cg")
        bcg_half = {bf_.name: [Buf(bf_.name + "g"), Buf(bf_.name + "v")] for _, bf_ in cgs.items}
        sgts = Slots([cv.f32(256) for _ in range(4)], "sgf")
        actTs = Slots([cv.bf(8 * 256) for _ in range(2)], "actT")
        hstg = cv.f32(2048); b_hstg = Buf("hstg")
        HIST = cv.f32(16 * 32); b_hist = Buf("hist")
        MU, MD, MX = (0, 1, 2, 3), (4, 5, 6), 7
        b_cg = {2: [Buf("cg2g"), Buf("cg2v")], 3: [Buf("cg3g"), Buf("cg3v")]}
        ui = [0]
        di = [0]
        ci = [0]
        pending = [None]
        CF4 = CF[:].rearrange("p (g j) r -> p g j r", g=2)
        CFS4 = CFS[:].rearrange("p (g j) b r -> p g j b r", g=2)
        P2 = PARAM2[:].rearrange("p (g j) r -> p g j r", g=2)
        def make_sub(s):
            n = s.ntok
            xb = xt_bufs(s)
            bA, bB, bH = blkbufs(s)
            actT, bact = actTs.next()
            if s.kind == 's':
                scf2 = D["scf"].rearrange("b r f -> (b r) f")
                hv = hstg.rearrange("p (g f) -> p g f", g=2)
                for g_ in range(2):
                    c0 = g_ * 2816 + j0 * 128
                    P.dma("sp", "hstg", lambda g_=g_, c0=c0: nc.sync.dma_start(
                        out=hv[0:32, g_, 0:nch * 128], in_=scf2[:, c0:c0 + nch * 128]),
                        reads=[b_work], writes=[b_hstg])
                pv = bank_f(MX).rearrange("p (c t) -> p c t", t=32)
                for g_ in range(2):
                    for jj in range(nch):
                        P.op("pe", lambda g_=g_, jj=jj, pv=pv: nc.tensor.transpose(
                            pv[:, g_ * 8 + jj, :], hv[0:32, g_, jj * 128:(jj + 1) * 128], ident_f[0:32, 0:32]),
                            reads=[b_hstg, b_idf], writes=[pb[MX]])
                P.op("act", lambda: nc.scalar.copy(out=HIST, in_=bank_f(MX)), reads=[pb[MX], b_work], writes=[b_hist])
                HV = HIST.rearrange("p (g j b r) -> p g j b r", g=2, j=8, r=2)
            def stageA(jj):
                j = j0 + jj
                bk = MU[ui[0] % 4]
                ui[0] += 1
                for g_, Wx in enumerate((Wg, Wv_)):
                    for kc in range(8):
                        P.op("pe", lambda jj=jj, kc=kc, bk=bk, g_=g_, Wx=Wx: nc.tensor.matmul(
                            bank_f(bk)[:, g_ * 256:g_ * 256 + n], lhsT=Wx[:, kc, jj * 128:(jj + 1) * 128],
                            rhs=XT[:, kc, s.tok0:s.tok0 + n], start=(kc == 0), stop=(kc == 7)),
                            reads=bw + xb, writes=[pb[bk]])
                upx, bupx = upxs.next()
                src = bank_f(bk).rearrange("p (g t) -> p g t", t=256)[:, :, 0:n]
                if s.kind == 'p':
                    U = upx.rearrange("p (g t) -> p g t", t=258)
                    P.op("act", lambda U=U, src=src: nc.scalar.copy(out=U[:, :, 2:258], in_=src),
                         reads=[pb[bk], b_work], writes=[bupx])
                    if s.first:
                        P.op("pool", lambda U=U: nc.gpsimd.memset(U[:, :, 0:2], 0.0), reads=[b_work], writes=[bupx])
                    else:
                        P.op("pool", lambda U=U, j=j: nc.gpsimd.tensor_copy(out=U[:, :, 0:2], in_=CF4[:, :, j, :]),
                             reads=[b_cf, b_work], writes=[bupx])
                    P.op("pool", lambda U=U, j=j: nc.gpsimd.tensor_copy(out=CF4[:, :, j, :], in_=U[:, :, 256:258]),
                         reads=[bupx], writes=[b_cf])

                    def tap(g_, k, U=U):
                        return U[:, g_, k:k + 256]

                else:
                    U = upx[:, 0:320].rearrange("p (g b t) -> p g b t", g=2, t=10)
                    P.op("act", lambda U=U, src=src: nc.scalar.copy(
                        out=U[:, :, :, 2:10], in_=src.rearrange("p g (b i) -> p g b i", i=8)),
                        reads=[pb[bk], b_work], writes=[bupx])
                    P.op("pool", lambda U=U, jj=jj: nc.gpsimd.tensor_copy(out=U[:, :, :, 0:2], in_=HV[:, :, jj, :, :]),
                         reads=[b_hist, b_work], writes=[bupx])
                    P.op("pool", lambda U=U, j=j: nc.gpsimd.tensor_copy(out=CFS4[:, :, j, :, :], in_=U[:, :, :, 8:10]),
                         reads=[bupx], writes=[b_cfs])

                    def tap(g_, k, U=U):
                        return U[:, g_, :, k:k + 8]

                return dict(tap=tap, bupx=bupx, j=j, jj=jj)

            def stageB(st):
                tap, bupx, j, jj = st['tap'], st['bupx'], st['j'], st['jj']
                cg, bcg = cgs.next()
                if s.kind == 'p':
                    def cgP(g_, cg=cg):
                        return cg[:, g_, :]
                else:
                    def cgP(g_, cg=cg):
                        return cg[:, g_, 0:128].rearrange("p (b i) -> p b i", i=8)
                bcgh = bcg_half[bcg.name]
                for g_ in range(2):
                    P.op("act", lambda g_=g_, j=j, tap=tap, cgP=cgP: nc.scalar.activation(
                        out=cgP(g_), in_=tap(g_, 2), func=AF.Identity,
                        scale=P2[:, g_, j, 2:3], bias=P2[:, g_, j, 3:4]),
                        reads=[bupx, b_par2, b_work], writes=[bcgh[g_]])
                for k in (1, 0):
                    for g_ in range(2):
                        P.op("dve", lambda g_=g_, j=j, k=k, tap=tap, cgP=cgP: nc.vector.scalar_tensor_tensor(
                            out=cgP(g_), in0=tap(g_, k), scalar=P2[:, g_, j, k:k + 1], in1=cgP(g_),
                            op0=ALU.mult, op1=ALU.add),
                            reads=[bupx, bcgh[g_], b_par2], writes=[bcgh[g_]])
                st['cg'], st['bcgh'] = cg, bcgh

            def stageC(st):
                cg, bcgh, jj = st['cg'], st['bcgh'], st['jj']
                sgt, bsgt = sgts.next()
                P.op("act", lambda: nc.scalar.activation(
                    out=sgt[:, 0:n], in_=cg[:, 0, 0:n], func=AF.Silu),
                    reads=[bcgh[0], b_work], writes=[bsgt])
                P.op("pool", lambda: nc.gpsimd.tensor_tensor(
                    out=actT[:, jj * n:(jj + 1) * n], in0=sgt[:, 0:n], in1=cg[:, 1, 0:n], op=ALU.mult),
                    reads=[bsgt, bcgh[1], b_work], writes=[bact])
            def down():
                for ti in range(s.nt):
                    H = big_H(s, ti)
                    for hf in range(2):
                        bk = MD[di[0] % 3]
                        di[0] += 1
                        for jj in range(nch):
                            P.op("pe", lambda jj=jj, bk=bk, ti=ti, hf=hf: nc.tensor.matmul(
                                bank_f(bk), lhsT=actT[:, jj * n + ti * 128:jj * n + (ti + 1) * 128],
                                rhs=Wdn[:, jj, hf * 512:(hf + 1) * 512], start=(jj == 0), stop=(jj == nch - 1)),
                                reads=bw + [bact], writes=[pb[bk]])
                        P.op("dve", lambda H=H, bk=bk, hf=hf: nc.vector.tensor_tensor(
                            out=H[:, hf * 512:(hf + 1) * 512], in0=bank_f(bk), in1=H[:, hf * 512:(hf + 1) * 512],
                            op=ALU.add),
                            reads=[pb[bk], bH[ti]], writes=[bH[ti]])
            return stageA, stageB, stageC, down

        subs = [make_sub(s_) for s_ in sg]
        items = [(si_, jj_) for si_ in range(len(sg)) for jj_ in range(nch)]
        sts = {}
        for t in range(len(items) + 2):
            if t < len(items):
                sts[t] = subs[items[t][0]][0](items[t][1])
            if 0 <= t - 1 < len(items):
                subs[items[t - 1][0]][1](sts[t - 1])
            if 0 <= t - 2 < len(items):
                si2, jj2 = items[t - 2]
                subs[si2][2](sts[t - 2])
                if jj2 == nch - 1:
                    subs[si2][3]()

    def stage_final(sg, sgi):
        stage_fence()
        cv = Carver()
        junk = cv.bf(1024)
        tstg = Slots([cv.f32(1408) for _ in range(2)], "tstg")
        tiles = [(s, ti) for s in sg for ti in range(s.nt)]
        nt = len(tiles)
        for n, (s, ti) in enumerate(tiles):
            P.op("act", lambda s=s, ti=ti, n=n: nc.scalar.activation(
                out=junk, in_=big_H(s, ti), func=AF.Square, accum_out=RS[:, 2, n:n + 1]),
                reads=[blkbufs(s)[2][ti], b_work], writes=[b_junk, b_rs[2]])
        P.op("act", lambda: nc.scalar.activation(out=RS2[:, 0:nt], in_=RS[:, 2, 0:nt], func=AF.Sqrt,
                                                 bias=EPS, scale=1.0 / 1024.0),
             reads=[b_rs[2]], writes=[b_rs2])
        P.op("dve", lambda: nc.vector.reciprocal(out=RS[:, 2, 0:nt], in_=RS2[:, 0:nt]),
             reads=[b_rs2], writes=[b_rs[2]])
        for n, (s, ti) in enumerate(tiles):
            H = big_H(s, ti)
            bh = blkbufs(s)[2][ti]
            P.op("dve", lambda H=H, n=n: nc.vector.scalar_tensor_tensor(
                out=H, in0=H, scalar=RS[:, 2, n:n + 1], in1=gf_bc[:], op0=ALU.mult, op1=ALU.mult),
                reads=[bh, b_rs[2], b_gf], writes=[bh])
            P.dma("sp", "yout%d" % blk_index[s.blk], lambda H=H, s=s, ti=ti: nc.sync.dma_start(out=y_dst(s, ti), in_=H),
                  reads=[bh])
            out_bufs.append(bh)
        jobs = []
        if sgi == 0:
            jobs.append(("s", CFS[:].rearrange("p c b r -> p c (b r)"), 32, D["cfs"].rearrange("b r f -> (b r) f"), b_cfs))
        else:
            jobs.append(("p", CF[:], 2, D["cfp"], b_cf))
        bi = [0]
        for kind, srcT, nr, dst, bsrc in jobs:
            for rnd in range(4):
                st_, bst = tstg.next()
                for q in range(3):
                    bk = (bi[0] % 2) * 3 + q
                    nchq = min(4, 11 - q * 4)
                    for jj in range(nchq):
                        c = rnd * 11 + q * 4 + jj
                        P.op("pe", lambda bk=bk, jj=jj, c=c, srcT=srcT, nr=nr: nc.tensor.transpose(
                            bank_f(bk)[0:nr, jj * 128:(jj + 1) * 128], srcT[:, c, :], ident_f[:]),
                            reads=[bsrc, b_idf], writes=[pb[bk]])
                    P.op("act", lambda bk=bk, q=q, nchq=nchq, st_=st_, nr=nr: nc.scalar.copy(
                        out=st_[0:nr, q * 512:q * 512 + nchq * 128], in_=bank_f(bk)[0:nr, 0:nchq * 128]),
                        reads=[pb[bk], b_work], writes=[bst])
                bi[0] += 1
                P.dma("sp", bst.name, lambda st_=st_, nr=nr, dst=dst, rnd=rnd: nc.sync.dma_start(
                    out=dst[:, rnd * 1408:(rnd + 1) * 1408], in_=st_[0:nr, :]), reads=[bst])
                out_bufs.append(bst)

    import os
    nc_real = nc
    nc = _NcProxy()
    stages = []
    for sgi in range(2):
        for st in ("S1", "S2", "S3", "F0", "F1", "F2"):
            stages.append((sgi, st))
    init_consts()
    load_stage_weights(stages[0][1], 0)
    maxst = int(os.environ.get("KSTAGES", "99"))
    substop = os.environ.get("KSUB", "")
    for i, (sgi, st) in enumerate(stages):
        if i >= maxst:
            break
        slot = i % 2
        sg = SGS[sgi]
        if i + 1 < len(stages) and "w2" not in os.environ.get("KSKIP", "") and not (i == 0 and os.environ.get("KORDER", "") == "early"):
            load_stage_weights(stages[i + 1][1], (i + 1) % 2)
        if st == "S1":
            stage_normT(sg, 0)
            if substop == "X":
                break
            stage_S1(sg, slot)
        elif st == "S2":
            if sgi == 0:
                issue_d2d()
            stage_S2(sg, slot)
        elif st == "S3":
            stage_S3(sg, slot)
            stage_normT(sg, 1)
        else:
            stage_F(sg, slot, int(st[1]))
            if st == "F2":
                stage_final(sg, sgi)
    print("TOTAL_OPS", len(P.ops))
    if os.environ.get("KDUMPOP"):
        for k_ in os.environ["KDUMPOP"].split(","):
            o_ = P.ops[int(k_)]
            print("OP", k_, o_[0], o_[1].name, o_[2], o_[3][:2], [(a.name, w) for a, w in o_[4]])
            print("   ARGS", {kk: (vv.shape, vv.offset, vv.ap) if hasattr(vv, "ap") else vv for kk, vv in o_[3][3].items()})
    if P.limit is not None and P.ops:
        print("LAST_OP", P.ops[min(P.limit, len(P.ops)) - 1][1].name, P.ops[min(P.limit, len(P.ops)) - 1][3][:2])
    P.wait_all("sp", out_bufs + slotbufs[0] + slotbufs[1] + [b_gf, b_esink])
    P.emit()
    ctx.close()
    return nc_real


_NC_CACHE = {}


def kernel(**inputs):
    f = lambda a: np.ascontiguousarray(np.asarray(a, dtype=np.float32))
    xp = f(inputs["x_prompt"]); xs = f(inputs["x_sample"])
    ck = f(inputs["cache_k_win"])[0].reshape(128, 128, 256)
    cv = f(inputs["cache_v_win"])[0].reshape(128, 128, 256)
    scb = f(inputs["state_conv_b"])[0]
    scf = f(inputs["state_conv_ffn"])[0]
    shared = {
        "norm1_g": f(inputs["norm1_g"]).reshape(1, 1024), "w_in": f(inputs["w_in"])[0],
        "sinks": f(inputs["sinks"]).reshape(1, 16), "w_o": f(inputs["w_o_attn"])[0],
        "conv_b_w": f(inputs["conv_b_w"])[0], "conv_b_bias": f(inputs["conv_b_bias"]).reshape(1, 1024),
        "ln_b_g": f(inputs["ln_b_g"]).reshape(1, 1024), "ln_b_b": f(inputs["ln_b_b"]).reshape(1, 1024),
        "w_pb": f(inputs["w_pb"])[0], "w_out": f(inputs["w_out"])[0],
        "norm2_g": f(inputs["norm2_g"]).reshape(1, 1024), "w_up": f(inputs["w_up"])[0],
        "ffn_conv_w": f(inputs["ffn_conv_w"])[0], "ffn_conv_b": f(inputs["ffn_conv_b"]).reshape(1, 5632),
        "w_down": f(inputs["w_down"])[0], "norm_f_g": f(inputs["norm_f_g"]).reshape(1, 1024),
    }
    in_maps = []
    for c in range(NCORES):
        m = dict(shared)
        m["xp"] = xp[c]
        m["xs"] = xs[16 * c:16 * c + 16].reshape(128, 1024)
        m["ck"] = ck[16 * c:16 * c + 16]
        m["cv"] = cv[16 * c:16 * c + 16]
        m["scb"] = scb[16 * c:16 * c + 16]
        m["scf"] = scf[16 * c:16 * c + 16]
        in_maps.append(m)
    if "nc" not in _NC_CACHE:
        _NC_CACHE["nc"] = build_program()
    res = run_bass_kernel_spmd(_NC_CACHE["nc"], in_maps, core_ids=list(range(NCORES)))
    R = res.results
    cat = lambda k: np.concatenate([np.asarray(r[k], dtype=np.float32) for r in R], axis=0)
    y_prompt = np.stack([np.asarray(r["yp"], dtype=np.float32) for r in R], axis=0)
    y_sample = cat("ys").reshape(128, 8, 1024)
    k_win_prompt = np.stack([np.asarray(r["kwp"]) for r in R], 0).reshape(1, 8, 128, 4, 64).astype(np.float32)
    v_win_prompt = np.stack([np.asarray(r["vwp"]) for r in R], 0).reshape(1, 8, 128, 4, 64).astype(np.float32)
    conv_b_prompt = np.stack([np.asarray(r["cbp"]) for r in R], 0).reshape(1, 8, 30, 1024).astype(np.float32)
    conv_ffn_prompt = np.stack([np.asarray(r["cfp"]) for r in R], 0).reshape(1, 8, 2, 5632).astype(np.float32)
    k_win_sample = cat("kws").reshape(1, 128, 128, 4, 64)
    v_win_sample = cat("vws").reshape(1, 128, 128, 4, 64)
    conv_b_sample = cat("cbs").reshape(1, 128, 30, 1024)
    conv_ffn_sample = cat("cfs").reshape(1, 128, 2, 5632)
    return (y_prompt, y_sample, k_win_prompt, v_win_prompt, conv_b_prompt, conv_ffn_prompt,
            k_win_sample, v_win_sample, conv_b_sample, conv_ffn_sample)
```

```python
from contextlib import ExitStack
import os
import numpy as np
import concourse.bass as bass
import concourse.mybir as mybir
from concourse.bass_utils import run_bass_kernel_spmd

F32 = mybir.dt.float32
BF16 = mybir.dt.bfloat16
AF = mybir.ActivationFunctionType
ALU = mybir.AluOpType
EPOCH = 4000
NCORES = 8
EPS = 1e-6


class Buf:
    __slots__ = ("name", "last_w", "readers")

    def __init__(self, name):
        self.name = name
        self.last_w = None
        self.readers = []


class Actor:
    def __init__(self, name, handle=None, is_chan=False):
        self.name = name
        self.handle = handle
        self.is_chan = is_chan
        self.n = 0
        self.last_op = -1
        self.marked = set()
        self.sems = []
        self.rank = {}
        self.seen = {}


class Prog:
    def __init__(self, nc, ctx):
        self.nc = nc
        self.ctx = ctx
        self.ops = []
        self.limit = int(os.environ["KOPLIMIT"]) if os.environ.get("KOPLIMIT") else None
        self.actors = {}
        for nm, h in (("pe", nc.tensor), ("act", nc.scalar), ("dve", nc.vector),
                      ("pool", nc.gpsimd), ("sp", nc.sync)):
            self.actors[nm] = Actor(nm, h)

    def chan(self, name):
        if name not in self.actors:
            self.actors[name] = Actor(name, None, is_chan=True)
        return self.actors[name]

    def _deps(self, eng, reads, writes, skip_chan=None):
        deps = {}

        def add(a, idx):
            if deps.get(a.name, (None, -1))[1] < idx:
                deps[a.name] = (a, idx)
        for b in reads:
            if b.last_w is not None:
                add(*b.last_w)
        for b in writes:
            if b.last_w is not None:
                add(*b.last_w)
            for r in b.readers:
                add(*r)
        waits = []
        for nm, (a, idx) in deps.items():
            if a is eng and eng.name == "pe":
                continue
            if a is skip_chan:
                continue
            if eng.seen.get(nm, -1) >= idx:
                continue
            if a.is_chan and idx != a.n - 1:
                raise RuntimeError("unsafe wait on shared DMA channel %s (%d of %d)" % (nm, idx, a.n))
            eng.seen[nm] = idx
            a.marked.add(idx)
            waits.append((a, idx))
        return waits

    def op(self, engname, fn, reads=(), writes=()):
        if self.limit is not None and len(self.ops) >= self.limit:
            return
        eng = self.actors[engname]
        waits = self._deps(eng, reads, writes)
        idx = eng.n
        eng.n += 1
        eng.last_op = idx
        self.ops.append(("op", eng, idx, fn(), waits, None))
        for b in reads:
            b.readers.append((eng, idx))
        for b in writes:
            b.last_w = (eng, idx)
            b.readers = []

    def dma(self, qname, channame, fn, reads=(), writes=(), waw=False):
        if self.limit is not None and len(self.ops) >= self.limit:
            return
        q = self.actors[qname]
        ch = self.chan(channame)
        waits = self._deps(q, reads, writes, skip_chan=None if waw else ch)
        idx = ch.n
        ch.n += 1
        qidx = q.n
        q.n += 1
        self.ops.append(("dma", q, qidx, fn(), waits, (ch, idx)))
        for b in reads:
            b.readers.append((ch, idx))
        for b in writes:
            b.last_w = (ch, idx)
            b.readers = []

    def barrier(self, bufs=()):
        comp = [self.actors[n] for n in ("pe", "act", "dve", "pool")]
        snap = {a.name: a.last_op for a in comp if a.last_op >= 0}
        for E in comp + [self.actors["sp"]]:
            waits = []
            for a in comp:
                if a is E or a.name not in snap:
                    continue
                idx = snap[a.name]
                if E.seen.get(a.name, -1) >= idx:
                    continue
                E.seen[a.name] = idx
                a.marked.add(idx)
                waits.append((a, idx))
            for b in bufs:
                for (a, _) in b.readers:
                    if not a.is_chan:
                        continue
                    idx2 = a.n - 1
                    if E.seen.get(a.name, -1) >= idx2:
                        continue
                    E.seen[a.name] = idx2
                    a.marked.add(idx2)
                    waits.append((a, idx2))
            i = E.n
            E.n += 1
            self.ops.append(("nop", E, i, None, waits, None))

    def wait_all(self, engname, bufs):
        eng = self.actors[engname]
        waits = self._deps(eng, [], bufs)
        idx = eng.n
        eng.n += 1
        self.ops.append(("nop", eng, idx, None, waits, None))

    def check_no_deadlock(self):
        queues = {}
        for op in self.ops:
            queues.setdefault(op[1].name, []).append(op)
        pos = {k: 0 for k in queues}
        sem = {}
        progress = True
        while progress:
            progress = False
            for k, q in queues.items():
                while pos[k] < len(q):
                    kind, eng, idx, fn, waits, chinfo = q[pos[k]]
                    ok = True
                    for (a, widx) in waits:
                        ep, val = a.rank[widx]
                        if sem.get((a.name, ep), 0) < val:
                            ok = False
                            break
                    if not ok:
                        break
                    if kind == "dma":
                        key = (chinfo[0].name, 0)
                        sem[key] = sem.get(key, 0) + 16
                    elif kind == "op" and idx in eng.marked:
                        key = (eng.name, eng.rank[idx][0])
                        sem[key] = sem.get(key, 0) + 1
                    pos[k] += 1
                    progress = True
        for k, q in queues.items():
            if pos[k] < len(q):
                kind, eng, idx, fn, waits, chinfo = q[pos[k]]
                raise RuntimeError("DEADLOCK: %s stuck at op %d (%s) waits=%s" % (
                    k, idx, fn[:2] if fn else None, [(a.name, w, a.rank[w]) for a, w in waits]))

    def emit(self):
        nsem = 0
        for a in self.actors.values():
            m = sorted(a.marked)
            a.rank = {}
            if a.is_chan:
                a.sems = [self.ctx.enter_context(self.nc.semaphore("c_" + a.name))]
                nsem += 1
                for i in m:
                    a.rank[i] = (0, 16 * (i + 1))
            else:
                nep = max((len(m) + EPOCH - 1) // EPOCH, 1)
                a.sems = [self.ctx.enter_context(self.nc.semaphore("e_%s%d" % (a.name, e)))
                          for e in range(nep)]
                nsem += nep
                for r, i in enumerate(m):
                    a.rank[i] = (r // EPOCH, r % EPOCH + 1)
        self.nsem = nsem
        self.check_no_deadlock()
        for kind, eng, idx, fn, waits, chinfo in self.ops:
            for (a, widx) in waits:
                ep, val = a.rank[widx]
                eng.handle.wait_ge(a.sems[ep], val)
            if kind == "nop":
                continue
            ename, meth, a_, k_ = fn
            inst = getattr(getattr(self.nc, ename), meth)(*a_, **k_)
            if kind == "dma":
                assert idx not in eng.marked, "DMA queue op used as an engine milestone"
                inst.then_inc(chinfo[0].sems[0], 16)
            elif idx in eng.marked:
                inst.then_inc(eng.sems[eng.rank[idx][0]], 1)


class _EngProxy:
    def __init__(self, name):
        self._name = name

    def __getattr__(self, meth):
        name = self._name

        def call(*a, **k):
            return (name, meth, a, k)
        return call


class _NcProxy:
    def __init__(self):
        self.tensor = _EngProxy("tensor")
        self.scalar = _EngProxy("scalar")
        self.vector = _EngProxy("vector")
        self.gpsimd = _EngProxy("gpsimd")
        self.sync = _EngProxy("sync")


class Slots:
    def __init__(self, aps, name):
        self.items = [(ap, Buf("%s%d" % (name, i))) for i, ap in enumerate(aps)]
        self.i = 0

    def next(self):
        it = self.items[self.i % len(self.items)]
        self.i += 1
        return it


class Sub:
    def __init__(self, kind, idx, ntok, tok0, blk):
        self.kind = kind
        self.idx = idx
        self.ntok = ntok
        self.tok0 = tok0
        self.blk = blk
        self.nt = ntok // 128
        self.first = (kind == 'p' and idx == 0)
        self.last = (kind == 'p' and idx == 7)


def build_program():
    nc = bass.Bass("TRN2", target_bir_lowering=False)
    D = {}

    def din(name, shape):
        D[name] = nc.dram_tensor(name, list(shape), F32, kind="ExternalInput").ap()

    def dout(name, shape):
        D[name] = nc.dram_tensor(name, list(shape), F32, kind="ExternalOutput").ap()

    din("xp", (2048, 1024)); din("xs", (128, 1024))
    din("ck", (16, 128, 256)); din("cv", (16, 128, 256))
    din("scb", (16, 30, 1024)); din("scf", (16, 2, 5632))
    din("norm1_g", (1, 1024)); din("w_in", (1024, 5632)); din("sinks", (1, 16))
    din("w_o", (1024, 1024)); din("conv_b_w", (31, 1024)); din("conv_b_bias", (1, 1024))
    din("ln_b_g", (1, 1024)); din("ln_b_b", (1, 1024)); din("w_pb", (1024, 1024))
    din("w_out", (1024, 1024)); din("norm2_g", (1, 1024)); din("w_up", (1024, 5632))
    din("ffn_conv_w", (3, 5632)); din("ffn_conv_b", (1, 5632)); din("w_down", (2816, 1024))
    din("norm_f_g", (1, 1024))
    dout("yp", (2048, 1024)); dout("ys", (128, 1024))
    dout("kwp", (128, 256)); dout("vwp", (128, 256)); dout("cbp", (30, 1024)); dout("cfp", (2, 5632))
    dout("kws", (16, 128, 256)); dout("vws", (16, 128, 256))
    dout("cbs", (16, 30, 1024)); dout("cfs", (16, 2, 5632))

    w_in3 = D["w_in"].rearrange("(kc p) n -> p kc n", p=128)
    w_o3 = D["w_o"].rearrange("(kc p) n -> p kc n", p=128)
    w_pb3 = D["w_pb"].rearrange("(kc p) n -> p kc n", p=128)
    w_out3 = D["w_out"].rearrange("(kc p) n -> p kc n", p=128)
    w_up3 = D["w_up"].rearrange("(kc p) n -> p kc n", p=128)
    w_dn3 = D["w_down"].rearrange("(j p) n -> p j n", p=128)

    ctx = ExitStack()
    P = Prog(nc, ctx)

    def sb(name, shape, dt):
        return ctx.enter_context(nc.sbuf_tensor(name, list(shape), dt))

    ident_bf = sb("ident_bf", (128, 128), BF16); b_idbf = Buf("idbf")
    ident_f = sb("ident_f", (128, 128), F32); b_idf = Buf("idf")
    ones_bf = sb("ones_bf", (128, 128), BF16); b_ones = Buf("ones")
    MASK2 = sb("mask2", (128, 2, 2, 128), BF16); b_mask2 = Buf("mask2")
    MASKS4 = sb("masks4", (128, 4, 128), BF16); b_masks4 = Buf("masks4")
    MASKP = sb("maskp", (128, 8), BF16); b_maskp = Buf("maskp")
    esink = sb("esink", (128, 16), F32); b_esink = Buf("esink")
    gf_bc = sb("gf_bc", (128, 1024), F32); b_gf = Buf("gf")
    PARAM1 = sb("param1", (128, 8, 36), F32); b_par1 = Buf("par1")
    PARAM2 = sb("param2", (128, 44, 4), F32); b_par2 = Buf("par2")
    RS = sb("rstd_tab", (128, 3, 16), F32)
    RS2 = sb("rstd_tmp", (128, 16), F32); b_rs2 = Buf("rs2")
    b_rs = [Buf("rs0"), Buf("rs1"), Buf("rs2")]
    KC = sb("kcarry", (128, 4, 128), BF16); b_kc = Buf("kc")
    VC = sb("vcarry", (128, 4, 65), BF16); b_vc = Buf("vc")
    UC = sb("ucarry", (128, 8, 30), F32); b_uc = Buf("uc")
    CF = sb("fcarry", (128, 44, 2), F32); b_cf = Buf("cf")
    CFS = sb("fcarry_s", (128, 44, 16, 2), F32); b_cfs = Buf("cfs")
    XT = sb("XT", (128, 8, 1152), BF16)
    BIG = sb("BIG", (128, 18432), BF16)
    ARENA = sb("ARENA", (128, 2, 24576), BF16)
    NWCH = 12
    slotbufs = [[Buf("slot%d_%d" % (s_, c_)) for c_ in range(NWCH)] for s_ in range(2)]
    slotbuf = [slotbufs[0], slotbufs[1]]
    wl_count = [0, 0]
    WORKN = 20096
    WORK = sb("WORK", (128, WORKN), BF16)
    b_junk = Buf("junk")

    banks = [ctx.enter_context(nc.psum_tensor("bank%d" % i, [128, 512], F32)) for i in range(8)]
    pb = [Buf("bank%d" % i) for i in range(8)]

    def bank_f(i):
        return banks[i][:]

    def bank_b(i):
        return banks[i][:].bitcast(BF16)

    b_xt = [Buf("XT%d" % i) for i in range(9)]

    def xt_bufs(sub):
        t0 = sub.tok0 // 128
        return b_xt[t0:t0 + sub.nt]

    blk_index = {0: 0, 4096: 1, 8192: 2, 12288: 3, 16384: 4}
    b_A = [Buf("A%d" % i) for i in range(5)]
    b_B = [Buf("B%d" % i) for i in range(5)]
    b_H = [[Buf("H%d_%d" % (i, t)) for t in range(2)] for i in range(5)]

    def big_A(sub):
        n = sub.ntok
        return BIG[:, sub.blk:sub.blk + 8 * n].rearrange("p (c t) -> p c t", t=n)

    def big_B(sub):
        n = sub.ntok
        return BIG[:, sub.blk + 8 * n:sub.blk + 16 * n].rearrange("p (c t) -> p c t", t=n)

    def big_H(sub, ti):
        o = sub.blk + ti * 2048
        return BIG[:, o:o + 2048].bitcast(F32)

    def blkbufs(sub):
        i = blk_index[sub.blk]
        return b_A[i], b_B[i], b_H[i][:sub.nt]

    class Carver:
        def __init__(self):
            self.off = 0

        def take(self, nelem_bf16):
            o = self.off
            self.off += (nelem_bf16 + 15) // 16 * 16
            assert self.off <= WORKN, "WORK overflow %d" % self.off
            return WORK[:, o:o + nelem_bf16]

        def f32(self, n):
            return self.take(2 * n).bitcast(F32)

        def bf(self, n):
            return self.take(n)

    b_work = Buf("work_epoch")

    def x_src(sub, ti):
        if sub.kind == 'p':
            r0 = (sub.idx * 2 + ti) * 128
            return D["xp"][r0:r0 + 128, :]
        return D["xs"][:, :]

    def y_dst(sub, ti):
        if sub.kind == 'p':
            r0 = (sub.idx * 2 + ti) * 128
            return D["yp"][r0:r0 + 128, :]
        return D["ys"][:, :]

    SG0 = [Sub('p', i, 256, 256 * i, 4096 * i) for i in range(4)] + [Sub('s', 0, 128, 1024, 16384)]
    SG1 = [Sub('p', 4 + i, 256, 256 * i, 4096 * i) for i in range(4)]
    SGS = [SG0, SG1]

    def stage_fence():
        P.barrier(out_bufs)

    def slot_view(slot, off, kc, n):
        return ARENA[:, slot, off:off + kc * n].rearrange("p (k n) -> p k n", n=n)

    def wload(slot, dst, src):
        import os
        if os.environ.get("KPRINT"):
            print("WLOAD", slot, dst.shape, dst.offset, dst.ap, src.shape, src.offset, src.ap)
        if "w" in os.environ.get("KSKIP", "").split(","):
            return
        c_ = wl_count[slot] % NWCH
        wl_count[slot] += 1
        P.dma("pool", "wslot%d_%d" % (slot, c_), lambda: nc.gpsimd.dma_start(out=dst, in_=src),
              writes=[slotbufs[slot][c_]], waw=True)

    FG = [(0, 8), (8, 16), (16, 22)]

    def load_stage_weights(stage, slot):
        import os
        part = os.environ.get("KWPART", "")
        if stage == "S1" and part:
            if "q" in part:
                wload(slot, slot_view(slot, 0, 8, 1024), w_in3[:, :, 0:1024])
            if "k" in part:
                kd = ARENA[:, slot, 8192:12288].rearrange("p (k h e) -> p k h e", h=4, e=128)
                for kvh in range(4):
                    ksrc = w_in3[:, :, 1024 + kvh * 64:1088 + kvh * 64]
                    wload(slot, kd[:, :, kvh, 0:64], ksrc)
                    wload(slot, kd[:, :, kvh, 64:128], ksrc)
            if "v" in part:
                wload(slot, slot_view(slot, 12288, 8, 256), w_in3[:, :, 1280:1536])
            if "o" in part:
                wload(slot, slot_view(slot, 14336, 8, 1024), w_o3)
            return
        if stage == "S1":
            wload(slot, slot_view(slot, 0, 8, 1024), w_in3[:, :, 0:1024])
            kd = ARENA[:, slot, 8192:12288].rearrange("p (k h e) -> p k h e", h=4, e=128)
            for kvh in range(4):
                ksrc = w_in3[:, :, 1024 + kvh * 64:1088 + kvh * 64]
                wload(slot, kd[:, :, kvh, 0:64], ksrc)
                wload(slot, kd[:, :, kvh, 64:128], ksrc)
            wload(slot, slot_view(slot, 12288, 8, 256), w_in3[:, :, 1280:1536])
            wload(slot, slot_view(slot, 14336, 8, 1024), w_o3)
        elif stage == "S2":
            ks2 = os.environ.get("KS2", "abc")
            if "a" in ks2:
                wload(slot, slot_view(slot, 0, 8, 1024), w_in3[:, :, 1536:2560])
            if "A" in ks2:
                wload(slot, slot_view(0, 8192, 8, 1024), w_in3[:, :, 1536:2560])
            if "Z" in ks2:
                wload(slot, slot_view(1, 0, 8, 1024), w_in3[:, :, 0:1024])
            if "b" in ks2:
                wload(slot, slot_view(slot, 8192, 8, 1024), w_in3[:, :, 2560:3584])
            if "c" in ks2:
                wload(slot, slot_view(slot, 16384, 8, 1024), w_pb3)
        elif stage == "S3":
            wload(slot, slot_view(slot, 0, 8, 1024), w_in3[:, :, 3584:4608])
            wload(slot, slot_view(slot, 8192, 8, 1024), w_in3[:, :, 4608:5632])
            wload(slot, slot_view(slot, 16384, 8, 1024), w_out3)
        else:
            j0, j1 = FG[int(stage[1])]
            n = j1 - j0
            wload(slot, slot_view(slot, 0, 8, n * 128), w_up3[:, :, j0 * 128:j1 * 128])
            wload(slot, slot_view(slot, 8192, 8, n * 128), w_up3[:, :, 2816 + j0 * 128:2816 + j1 * 128])
            wload(slot, ARENA[:, slot, 16384:16384 + n * 1024].rearrange("p (j m) -> p j m", m=1024),
                  w_dn3[:, j0:j1, :])

    def init_consts():
        g = nc.gpsimd
        P.op("pool", lambda: g.memset(ident_bf[:], 1.0), writes=[b_idbf])
        P.op("pool", lambda: g.affine_select(ident_bf[:], ident_bf[:], [[-1, 128]], ALU.is_equal, 0.0,
                                             base=0, channel_multiplier=1), reads=[b_idbf], writes=[b_idbf])
        P.op("pool", lambda: g.memset(ident_f[:], 1.0), writes=[b_idf])
        P.op("pool", lambda: g.affine_select(ident_f[:], ident_f[:], [[-1, 128]], ALU.is_equal, 0.0,
                                             base=0, channel_multiplier=1), reads=[b_idf], writes=[b_idf])
        P.op("pool", lambda: g.memset(ones_bf[:], 1.0 / 1024.0), writes=[b_ones])
        P.op("pool", lambda: g.memset(MASK2[:], 1.0), writes=[b_mask2])
        for hh in range(2):
            P.op("pool", lambda hh=hh: g.affine_select(MASK2[:, hh, 0, :], MASK2[:, hh, 0, :], [[-1, 128]],
                                                       ALU.is_gt, 0.0, base=0, channel_multiplier=1),
                 reads=[b_mask2], writes=[b_mask2])
            P.op("pool", lambda hh=hh: g.affine_select(MASK2[:, hh, 1, :], MASK2[:, hh, 1, :], [[1, 128]],
                                                       ALU.is_ge, 0.0, base=0, channel_multiplier=-1),
                 reads=[b_mask2], writes=[b_mask2])
        P.op("pool", lambda: g.memset(MASKS4[:], 1.0), writes=[b_masks4])
        for r in range(4):
            v = MASKS4[:, r, :].rearrange("p (b i) -> p b i", i=8)
            P.op("pool", lambda v=v: g.affine_select(v, v, [[8, 16], [1, 8]], ALU.is_ge, 0.0,
                                                     base=0, channel_multiplier=-1),
                 reads=[b_masks4], writes=[b_masks4])
            P.op("pool", lambda v=v: g.affine_select(v, v, [[-8, 16], [0, 8]], ALU.is_ge, 0.0,
                                                     base=0, channel_multiplier=1),
                 reads=[b_masks4], writes=[b_masks4])
        P.op("pool", lambda: g.memset(MASKP[:], 1.0), writes=[b_maskp])
        P.op("pool", lambda: g.affine_select(MASKP[:], MASKP[:], [[-1, 8]], ALU.is_gt, 0.0,
                                             base=0, channel_multiplier=1), reads=[b_maskp], writes=[b_maskp])
        P.op("pool", lambda: g.memset(CF[:], 0.0), writes=[b_cf])
        P.dma("sp", "esink", lambda: nc.sync.dma_start(out=esink[:], in_=D["sinks"].to_broadcast([128, 16])),
              writes=[b_esink])
        P.op("act", lambda: nc.scalar.activation(out=esink[:], in_=esink[:], func=AF.Exp),
             reads=[b_esink], writes=[b_esink])
        P.dma("sp", "gf", lambda: nc.sync.dma_start(out=gf_bc[:], in_=D["norm_f_g"].to_broadcast([128, 1024])),
              writes=[b_gf])
        cv = Carver()
        prow = cv.f32(1024)
        prow2 = cv.f32(5632)
        b_prow = Buf("prow"); b_prow2 = Buf("prow2")
        rows = [("norm1_g", 0, 1), ("conv_b_w", 1, 31), ("conv_b_bias", 32, 1), ("ln_b_g", 33, 1),
                ("ln_b_b", 34, 1), ("norm2_g", 35, 1)]
        for nm, r0, nr in rows:
            P.dma("sp", "prow", lambda nm=nm, r0=r0, nr=nr: nc.sync.dma_start(out=prow[r0:r0 + nr, :], in_=D[nm]),
                  reads=[b_work], writes=[b_prow])
        P.dma("sp", "prow2", lambda: nc.sync.dma_start(out=prow2[0:3, :], in_=D["ffn_conv_w"]),
              reads=[b_work], writes=[b_prow2])
        P.dma("sp", "prow2", lambda: nc.sync.dma_start(out=prow2[3:4, :], in_=D["ffn_conv_b"]),
              reads=[b_work], writes=[b_prow2])
        pv = bank_f(0)[:, 0:288].rearrange("p (c r) -> p c r", r=36)
        for c in range(8):
            P.op("pe", lambda c=c: nc.tensor.transpose(pv[:, c, :], prow[0:36, c * 128:(c + 1) * 128],
                                                       ident_f[0:36, 0:36]),
                 reads=[b_prow, b_idf], writes=[pb[0]])
        P.op("dve", lambda: nc.vector.tensor_copy(out=PARAM1[:], in_=pv), reads=[pb[0]], writes=[b_par1])
        pv2 = bank_f(1)[:, 0:176].rearrange("p (c r) -> p c r", r=4)
        for c in range(44):
            P.op("pe", lambda c=c: nc.tensor.transpose(pv2[:, c, :], prow2[0:4, c * 128:(c + 1) * 128],
                                                       ident_f[0:4, 0:4]),
                 reads=[b_prow2, b_idf], writes=[pb[1]])
        P.op("dve", lambda: nc.vector.tensor_copy(out=PARAM2[:], in_=pv2), reads=[pb[1]], writes=[b_par2])

    b_d2d = Buf("d2d")
    out_bufs = [b_d2d]

    def issue_d2d():
        for src, dst, a, b_, n in (("ck", "kws", 8, 128, 120), ("cv", "vws", 8, 128, 120), ("scb", "cbs", 8, 30, 22)):
            P.dma("sp", "d2d", lambda src=src, dst=dst, a=a, b_=b_, n=n:
                  nc.sync.dma_start(out=D[dst][:, 0:n, :], in_=D[src][:, a:b_, :]), writes=[b_d2d])

    def stage_normT(sg, which):
        stage_fence()
        cv = Carver()
        xts = Slots([cv.f32(1024) for _ in range(2)], "xt")
        xss = Slots([cv.bf(1024) for _ in range(2)], "xs")
        junk = cv.bf(1024)
        grow = 0 if which == 0 else 35
        tiles = [(s, ti) for s in sg for ti in range(s.nt)]
        nt = len(tiles)

        def src_tile(s, ti):
            if which == 0:
                ap, bf = xts.next()
                P.dma("sp", bf.name, lambda ap=ap, s=s, ti=ti: nc.sync.dma_start(out=ap, in_=x_src(s, ti)),
                      reads=[b_work], writes=[bf])
                return ap, bf
            return big_H(s, ti), blkbufs(s)[2][ti]
        for n, (s, ti) in enumerate(tiles):
            ap, bf = src_tile(s, ti)
            P.op("act", lambda ap=ap, n=n: nc.scalar.activation(out=junk, in_=ap, func=AF.Square,
                                                                accum_out=RS[:, which, n:n + 1]),
                 reads=[bf, b_work], writes=[b_junk, b_rs[which]])
        P.op("act", lambda: nc.scalar.activation(out=RS2[:, 0:nt], in_=RS[:, which, 0:nt], func=AF.Sqrt,
                                                 bias=EPS, scale=1.0 / 1024.0),
             reads=[b_rs[which]], writes=[b_rs2])
        P.op("dve", lambda: nc.vector.reciprocal(out=RS[:, which, 0:nt], in_=RS2[:, 0:nt]),
             reads=[b_rs2], writes=[b_rs[which]])
        for n, (s, ti) in enumerate(tiles):
            ap, bf = src_tile(s, ti)
            xs_, bxs = xss.next()
            P.op("act", lambda ap=ap, xs_=xs_, n=n: nc.scalar.activation(
                out=xs_, in_=ap, func=AF.Copy, scale=RS[:, which, n:n + 1]),
                reads=[bf, b_rs[which], b_work], writes=[bxs])
            bk = n % 2
            pT = bank_b(bk).rearrange("p (c t) -> p c t", t=128)
            for c in range(8):
                P.op("pe", lambda c=c, pT=pT, xs_=xs_: nc.tensor.transpose(pT[:, c, :], xs_[:, c * 128:(c + 1) * 128],
                                                                           ident_bf[:]),
                     reads=[bxs, b_idbf], writes=[pb[bk]])
            t0 = s.tok0 + ti * 128
            gb = PARAM1[:, :, grow:grow + 1].to_broadcast([128, 8, 128])
            P.op("dve", lambda pT=pT, t0=t0, gb=gb: nc.vector.tensor_tensor(
                out=XT[:, :, t0:t0 + 128], in0=pT, in1=gb, op=ALU.mult),
                reads=[pb[bk], b_par1], writes=[b_xt[t0 // 128]])

    def stage_S1(sg, slot):
        stage_fence()
        Wq = slot_view(slot, 0, 8, 1024)
        WkD = ARENA[:, slot, 8192:12288].rearrange("p (k h e) -> p k h e", h=4, e=128)
        Wv = slot_view(slot, 12288, 8, 256)
        Wo = slot_view(slot, 14336, 8, 1024)
        bw = slotbuf[slot]
        cv = Carver()
        qT = cv.bf(8 * 256).rearrange("p (c t) -> p c t", t=256); b_qT = Buf("qT")
        kTd = cv.bf(4 * 256).rearrange("p (h t) -> p h t", t=256); b_kTd = Buf("kTd")
        Vaug = cv.bf(2 * 4 * 65).rearrange("p (s h e) -> p s h e", h=4, e=65); b_V = Buf("Vaug")
        pTs = Slots([cv.bf(512).rearrange("p (a b q) -> p a b q", a=2, b=2) for _ in range(4)], "pT")
        attn = Slots([cv.bf(1024) for _ in range(2)], "attn")
        attnT = cv.bf(8 * 256).rearrange("p (c t) -> p c t", t=256); b_attnT = Buf("attnT")
        den = cv.f32(16); b_den = Buf("den")
        rden = cv.f32(16); b_rden = Buf("rden")
        ktok = cv.f32(256); b_ktok = Buf("ktok")
        vtok = cv.f32(256); b_vtok = Buf("vtok")
        kpd = cv.bf(4 * 4 * 128).rearrange("p (r h e) -> p r h e", h=4, e=128); b_kpd = Buf("kpd")
        kpT = cv.bf(4 * 4 * 128).rearrange("p (r h k) -> p r h k", h=4, k=128); b_kpT = Buf("kpT")
        PTpad_raw = cv.bf(4 * 16 * 32)
        PTpad = PTpad_raw.rearrange("p (r h q) -> p r h q", h=16, q=32); b_ptpad = Buf("ptpad")
        Vpa = cv.bf(4 * 4 * 65).rearrange("p (r h e) -> p r h e", h=4, e=65); b_vpa = Buf("vpa")
        PTown_raw = cv.bf(16 * 128)
        PTown = PTown_raw.rearrange("p (h q) -> p h q", q=128); b_ptown = Buf("ptown")
        has_sample = any(s.kind == 's' for s in sg)
        P.op("pool", lambda: nc.gpsimd.memset(Vaug[:], 1.0), reads=[b_work], writes=[b_V])
        if has_sample:
            P.op("pool", lambda: nc.gpsimd.memset(Vpa[:], 1.0), reads=[b_work], writes=[b_vpa])
            P.op("pool", lambda: nc.gpsimd.memset(PTpad[:], 0.0), reads=[b_work], writes=[b_ptpad])
        MM, MS, MO, MT = (0, 1), (2, 3), (4, 5, 6), 7
        mmi = [0]

        def next_mm():
            b = MM[mmi[0] % 2]
            mmi[0] += 1
            return b
        si = [0]

        def normalize_and_transpose(s, ti, at, bat):
            for bi in range(3):
                h0 = bi * 7
                nh = min(7, 16 - h0)
                ov = bank_f(MO[bi])[:, 0:nh * 65].rearrange("p (h e) -> p h e", e=65)
                P.op("dve", lambda ov=ov, h0=h0, nh=nh: nc.vector.tensor_tensor(
                    out=den[:, h0:h0 + nh], in0=ov[:, :, 64], in1=esink[:, h0:h0 + nh], op=ALU.add),
                    reads=[pb[MO[bi]], b_esink, b_work], writes=[b_den])
            P.op("dve", lambda: nc.vector.reciprocal(out=rden, in_=den), reads=[b_den], writes=[b_rden])
            for bi in range(3):
                h0 = bi * 7
                nh = min(7, 16 - h0)
                ov = bank_f(MO[bi])[:, 0:nh * 65].rearrange("p (h e) -> p h e", e=65)
                av = at[:, h0 * 64:(h0 + nh) * 64].rearrange("p (h e) -> p h e", e=64)
                rb = rden[:, h0:h0 + nh].unsqueeze(2).to_broadcast([128, nh, 64])
                P.op("dve", lambda ov=ov, av=av, rb=rb: nc.vector.tensor_tensor(
                    out=av, in0=ov[:, :, 0:64], in1=rb, op=ALU.mult),
                    reads=[pb[MO[bi]], b_rden], writes=[bat])
            pT = bank_b(MT).rearrange("p (c t) -> p c t", t=128)
            for c in range(8):
                P.op("pe", lambda c=c, at=at, pT=pT: nc.tensor.transpose(pT[:, c, :], at[:, c * 128:(c + 1) * 128],
                                                                         ident_bf[:]),
                     reads=[bat, b_idbf], writes=[pb[MT]])
            P.op("act", lambda pT=pT, ti=ti: nc.scalar.copy(out=attnT[:, :, ti * 128:(ti + 1) * 128], in_=pT),
                 reads=[pb[MT]], writes=[b_attnT])

        for s in sg:
            n = s.ntok
            xb = xt_bufs(s)
            A_ap = big_A(s)
            bA, bB, bH = blkbufs(s)
            for half in range(4):
                bk = next_mm()
                for j in range(2):
                    oc = half * 2 + j
                    for kc in range(8):
                        P.op("pe", lambda oc=oc, kc=kc, bk=bk, j=j: nc.tensor.matmul(
                            bank_f(bk)[:, j * 256:j * 256 + n], lhsT=Wq[:, kc, oc * 128:(oc + 1) * 128],
                            rhs=XT[:, kc, s.tok0:s.tok0 + n], start=(kc == 0), stop=(kc == 7)),
                            reads=bw + xb, writes=[pb[bk]])
                src = bank_f(bk).rearrange("p (j t) -> p j t", t=256)[:, :, 0:n]
                P.op("act", lambda src=src, half=half: nc.scalar.copy(out=qT[:, half * 2:half * 2 + 2, 0:n], in_=src),
                     reads=[pb[bk], b_work], writes=[b_qT])
            for half in range(2):
                bk = next_mm()
                for j in range(2):
                    kvh = half * 2 + j
                    for kc in range(8):
                        P.op("pe", lambda kvh=kvh, kc=kc, bk=bk, j=j: nc.tensor.matmul(
                            bank_f(bk)[:, j * 256:j * 256 + n], lhsT=WkD[:, kc, kvh, :],
                            rhs=XT[:, kc, s.tok0:s.tok0 + n], start=(kc == 0), stop=(kc == 7)),
                            reads=bw + xb, writes=[pb[bk]])
                src = bank_f(bk).rearrange("p (j t) -> p j t", t=256)[:, :, 0:n]
                P.op("act", lambda src=src, half=half: nc.scalar.copy(out=kTd[:, half * 2:half * 2 + 2, 0:n], in_=src),
                     reads=[pb[bk], b_work], writes=[b_kTd])
            bk = next_mm()
            for ti in range(s.nt):
                for kc in range(8):
                    P.op("pe", lambda ti=ti, kc=kc, bk=bk: nc.tensor.matmul(
                        bank_f(bk)[:, ti * 256:(ti + 1) * 256],
                        lhsT=XT[:, kc, s.tok0 + ti * 128:s.tok0 + (ti + 1) * 128], rhs=Wv[:, kc, :],
                        start=(kc == 0), stop=(kc == 7)),
                        reads=bw + xb, writes=[pb[bk]])
            vsrc = bank_f(bk)[:, 0:s.nt * 256].rearrange("p (s h e) -> p s h e", h=4, e=64)
            P.op("act", lambda vsrc=vsrc: nc.scalar.copy(out=Vaug[:, 0:s.nt, :, 0:64], in_=vsrc),
                 reads=[pb[bk], b_work], writes=[b_V])
            if os.environ.get("KS1STOP", "") == "proj":
                return
            want_out = s.last or s.kind == 's'
            if want_out:
                ti_o = s.nt - 1
                P.op("dve", lambda ti_o=ti_o, bk=bk: nc.vector.tensor_copy(
                    out=vtok, in_=bank_f(bk)[:, ti_o * 256:(ti_o + 1) * 256]),
                    reads=[pb[bk], b_work, b_V], writes=[b_vtok])
                bk2 = next_mm()
                for kvh in range(4):
                    for kc in range(8):
                        P.op("pe", lambda kc=kc, kvh=kvh, bk2=bk2, ti_o=ti_o: nc.tensor.matmul(
                            bank_f(bk2)[:, kvh * 64:(kvh + 1) * 64],
                            lhsT=XT[:, kc, s.tok0 + ti_o * 128:s.tok0 + (ti_o + 1) * 128],
                            rhs=WkD[:, kc, kvh, 0:64], start=(kc == 0), stop=(kc == 7)),
                            reads=bw + xb, writes=[pb[bk2]])
                P.op("dve", lambda bk2=bk2: nc.vector.tensor_copy(out=ktok, in_=bank_f(bk2)[:, 0:256]),
                     reads=[pb[bk2], b_work], writes=[b_ktok])
                if s.kind == 'p':
                    P.dma("sp", "ktok", lambda: nc.sync.dma_start(out=D["kwp"], in_=ktok), reads=[b_ktok])
                    P.dma("sp", "vtok", lambda: nc.sync.dma_start(out=D["vwp"], in_=vtok), reads=[b_vtok])
                else:
                    for b in range(16):
                        P.dma("sp", "ktok", lambda b=b: nc.sync.dma_start(out=D["kws"][b, 120:128, :],
                                                                          in_=ktok[b * 8:(b + 1) * 8, :]), reads=[b_ktok])
                        P.dma("sp", "vtok", lambda b=b: nc.sync.dma_start(out=D["vws"][b, 120:128, :],
                                                                          in_=vtok[b * 8:(b + 1) * 8, :]), reads=[b_vtok])
                out_bufs.extend([b_ktok, b_vtok])
            if os.environ.get("KS1STOP", "") == "prompt" and s.kind == 's':
                return
            if s.kind == 'p':
                for ti in range(s.nt):
                    noprev = s.first and ti == 0
                    at, bat = attn.next()

                    def scores(g, ti=ti, noprev=noprev):
                        kvh = g
                        for hh in range(2):
                            bS = MS[hh]
                            Sv = bank_f(bS).rearrange("p (j b q) -> p j b q", j=2, b=2)
                            ps_ = slice(hh * 64, hh * 64 + 64)
                            for j in range(2):
                                hp = 2 * g + j
                                rhs = qT[ps_, hp, ti * 128:(ti + 1) * 128]
                                if not noprev:
                                    if ti == 0:
                                        lk, rb_ = KC[ps_, kvh, :], [b_kc]
                                    else:
                                        lk, rb_ = kTd[ps_, kvh, 0:128], [b_kTd]
                                    P.op("pe", lambda Sv=Sv, j=j, lk=lk, rhs=rhs: nc.tensor.matmul(
                                        Sv[:, j, 0, :], lhsT=lk, rhs=rhs, start=True, stop=True),
                                        reads=rb_ + [b_qT], writes=[pb[bS]])
                                P.op("pe", lambda Sv=Sv, j=j, ps_=ps_, kvh=kvh, rhs=rhs: nc.tensor.matmul(
                                    Sv[:, j, 1, :], lhsT=kTd[ps_, kvh, ti * 128:(ti + 1) * 128], rhs=rhs,
                                    start=True, stop=True),
                                    reads=[b_kTd, b_qT], writes=[pb[bS]])

                    def softmax_num(g, noprev=noprev):
                        res = []
                        for hh in range(2):
                            bS = MS[hh]
                            Sv = bank_f(bS).rearrange("p (j b q) -> p j b q", j=2, b=2)
                            pT, bpT = pTs.next()
                            if noprev:
                                P.op("act", lambda pT=pT, Sv=Sv: nc.scalar.activation(
                                    out=pT[:, :, 1, :], in_=Sv[:, :, 1, :], func=AF.Exp, scale=0.125),
                                    reads=[pb[bS], b_work], writes=[bpT])
                                P.op("dve", lambda pT=pT: nc.vector.tensor_tensor(
                                    out=pT[:, :, 1, :], in0=pT[:, :, 1, :], in1=MASK2[:, :, 1, :], op=ALU.mult),
                                    reads=[bpT, b_mask2], writes=[bpT])
                            else:
                                P.op("act", lambda pT=pT, Sv=Sv: nc.scalar.activation(
                                    out=pT, in_=Sv, func=AF.Exp, scale=0.125),
                                    reads=[pb[bS], b_work], writes=[bpT])
                                P.op("dve", lambda pT=pT: nc.vector.tensor_tensor(
                                    out=pT, in0=pT, in1=MASK2[:], op=ALU.mult),
                                    reads=[bpT, b_mask2], writes=[bpT])
                            res.append((pT, bpT))
                        return res

                    def pv(g, pts, ti=ti, noprev=noprev):
                        kvh = g
                        for hh in range(2):
                            pT, bpT = pts[hh]
                            for j in range(2):
                                h = 2 * (2 * g + j) + hh
                                bo = MO[h // 7]
                                oo = bank_f(bo)[:, (h % 7) * 65:(h % 7) * 65 + 65]
                                if not noprev:
                                    if ti == 0:
                                        vp, rb_ = VC[:, kvh, :], [b_vc]
                                    else:
                                        vp, rb_ = Vaug[:, 0, kvh, :], [b_V]
                                    P.op("pe", lambda oo=oo, pT=pT, j=j, vp=vp: nc.tensor.matmul(
                                        oo, lhsT=pT[:, j, 0, :], rhs=vp, start=True, stop=False),
                                        reads=[bpT] + rb_, writes=[pb[bo]])
                                P.op("pe", lambda oo=oo, pT=pT, j=j, kvh=kvh: nc.tensor.matmul(
                                    oo, lhsT=pT[:, j, 1, :], rhs=Vaug[:, ti, kvh, :], start=noprev, stop=True),
                                    reads=[bpT, b_V], writes=[pb[bo]])
                    scores(0)
                    for g in range(4):
                        pts = softmax_num(g)
                        if g + 1 < 4:
                            scores(g + 1)
                        pv(g, pts)
                    normalize_and_transpose(s, ti, at, bat)
                P.op("pool", lambda: nc.gpsimd.tensor_copy(out=KC[:], in_=kTd[:, :, 128:256]),
                     reads=[b_kTd], writes=[b_kc])
                P.op("act", lambda: nc.scalar.copy(out=VC[:].rearrange("p h e -> p (h e)"),
                                                   in_=Vaug[:, 1, :, :].rearrange("p h e -> p (h e)")),
                     reads=[b_V], writes=[b_vc])
            else:
                PTv = PTown_raw.rearrange("p (hp hh q) -> p hp hh q", hh=2, q=128)
                for r8 in range(2):
                    for hh in range(2):
                        bS = MS[hh]
                        Sv = bank_f(bS).rearrange("p (j q) -> p j q", q=128)
                        ps_ = slice(hh * 64, hh * 64 + 64)
                        for j in range(4):
                            hp = 4 * r8 + j
                            kvh = hp // 2
                            P.op("pe", lambda Sv=Sv, j=j, ps_=ps_, kvh=kvh, hp=hp: nc.tensor.matmul(
                                Sv[:, j, :], lhsT=kTd[ps_, kvh, 0:128], rhs=qT[ps_, hp, 0:128], start=True, stop=True),
                                reads=[b_kTd, b_qT], writes=[pb[bS]])
                        dsto = PTv[:, 4 * r8:4 * r8 + 4, hh, :]
                        P.op("act", lambda Sv=Sv, dsto=dsto: nc.scalar.activation(
                            out=dsto, in_=Sv, func=AF.Exp, scale=0.125),
                            reads=[pb[bS], b_work], writes=[b_ptown])
                        P.op("pool", lambda dsto=dsto: nc.gpsimd.tensor_tensor(
                            out=dsto, in0=dsto, in1=MASKS4[:], op=ALU.mult),
                            reads=[b_ptown, b_masks4], writes=[b_ptown])
                at, bat = attn.next()
                PTpv = PTpad_raw.rearrange("p (r hp hh q) -> p r hp hh q", hp=8, hh=2, q=32)
                for bb in range(4):
                    for r in range(4):
                        ksrc = D["ck"][4 * bb + r].rearrange("k (h d) -> k h d", d=64)
                        vsrc2 = D["cv"][4 * bb + r].rearrange("k (h d) -> k h d", d=64)
                        P.dma("pool", "kpd", lambda ksrc=ksrc, r=r: nc.gpsimd.dma_start(out=kpd[:, r, :, 0:64], in_=ksrc),
                              reads=[b_work], writes=[b_kpd])
                        P.dma("pool", "kpd", lambda ksrc=ksrc, r=r: nc.gpsimd.dma_start(out=kpd[:, r, :, 64:128], in_=ksrc),
                              reads=[b_work], writes=[b_kpd])
                        P.dma("pool", "vpa", lambda vsrc2=vsrc2, r=r: nc.gpsimd.dma_start(out=Vpa[:, r, :, 0:64], in_=vsrc2),
                              reads=[b_work], writes=[b_vpa])
                    for half in range(2):
                        pT = bank_b(MT).rearrange("p (c t) -> p c t", t=128)
                        for j in range(8):
                            r, kvh = (half * 8 + j) // 4, (half * 8 + j) % 4
                            P.op("pe", lambda pT=pT, j=j, r=r, kvh=kvh: nc.tensor.transpose(
                                pT[:, j, :], kpd[:, r, kvh, :], ident_bf[:]),
                                reads=[b_kpd, b_idbf], writes=[pb[MT]])
                        dst = kpT[:, half * 2:half * 2 + 2, :, :].rearrange("p r h k -> p (r h) k")
                        P.op("act", lambda pT=pT, dst=dst: nc.scalar.copy(out=dst, in_=pT),
                             reads=[pb[MT]], writes=[b_kpT])
                    for hh in range(2):
                        bS = MS[hh]
                        Sv = bank_f(bS)[:, 0:256].rearrange("p (r hp i) -> p r hp i", r=4, i=8)
                        ps_ = slice(hh * 64, hh * 64 + 64)
                        for r in range(4):
                            b = 4 * bb + r
                            for hp in range(8):
                                kvh = hp // 2
                                P.op("pe", lambda Sv=Sv, r=r, hp=hp, ps_=ps_, kvh=kvh, b=b: nc.tensor.matmul(
                                    Sv[:, r, hp, :], lhsT=kpT[ps_, r, kvh, :], rhs=qT[ps_, hp, b * 8:(b + 1) * 8],
                                    start=True, stop=True),
                                    reads=[b_kpT, b_qT], writes=[pb[bS]])
                        for r in range(4):
                            dstp = PTpv[:, r, :, hh, r * 8:(r + 1) * 8]
                            P.op("act", lambda Sv=Sv, r=r, dstp=dstp: nc.scalar.activation(
                                out=dstp, in_=Sv[:, r, :, :], func=AF.Exp, scale=0.125),
                                reads=[pb[bS]], writes=[b_ptpad])
                    for r in range(4):
                        dstp = PTpad[:, r, :, r * 8:(r + 1) * 8]
                        mb = MASKP[:].unsqueeze(1).to_broadcast([128, 16, 8])
                        P.op("pool", lambda dstp=dstp, mb=mb: nc.gpsimd.tensor_tensor(
                            out=dstp, in0=dstp, in1=mb, op=ALU.mult),
                            reads=[b_ptpad, b_maskp], writes=[b_ptpad])
                    for h in range(16):
                        kvh = h // 4
                        bo = MO[h // 7]
                        oo = bank_f(bo)[32 * bb:32 * bb + 32, (h % 7) * 65:(h % 7) * 65 + 65]
                        for r in range(4):
                            P.op("pe", lambda oo=oo, r=r, h=h, kvh=kvh, bb=bb: nc.tensor.matmul(
                                oo, lhsT=PTpad[:, r, h, :], rhs=Vpa[:, r, kvh, :], start=(r == 0), stop=False,
                                tile_position=(0, 32 * bb)),
                                reads=[b_ptpad, b_vpa], writes=[pb[bo]])
                        P.op("pe", lambda oo=oo, h=h, kvh=kvh, bb=bb: nc.tensor.matmul(
                            oo, lhsT=PTown[:, h, 32 * bb:32 * bb + 32], rhs=Vaug[:, 0, kvh, :], start=False, stop=True,
                            tile_position=(0, 32 * bb)),
                            reads=[b_ptown, b_V], writes=[pb[bo]])
                normalize_and_transpose(s, 0, at, bat)
            if os.environ.get("KS1STOP", "") == "attn":
                return
            for half in range(4):
                bk = next_mm()
                for j in range(2):
                    oc = half * 2 + j
                    for kc in range(8):
                        P.op("pe", lambda oc=oc, kc=kc, bk=bk, j=j: nc.tensor.matmul(
                            bank_f(bk)[:, j * 256:j * 256 + n], lhsT=Wo[:, kc, oc * 128:(oc + 1) * 128],
                            rhs=attnT[:, kc, 0:n], start=(kc == 0), stop=(kc == 7)),
                            reads=bw + [b_attnT], writes=[pb[bk]])
                src = bank_f(bk).rearrange("p (j t) -> p j t", t=256)[:, :, 0:n]
                P.op("dve", lambda src=src, half=half, A_ap=A_ap: nc.vector.tensor_copy(
                    out=A_ap[:, half * 2:half * 2 + 2, :], in_=src),
                    reads=[pb[bk]], writes=[bA] + bH)

    def stage_S2(sg, slot):
        stage_fence()
        Wga = slot_view(slot, 0, 8, 1024)
        Wgb = slot_view(slot, 8192, 8, 1024)
        Wpb = slot_view(slot, 16384, 8, 1024)
        bw = slotbuf[slot]
        state = {"kind": None}

        def alloc(kind):
            cv = Carver()
            nn = 256 if kind == 'p' else 128
            d = {}
            d["UEXTraw"] = cv.f32(8 * 286 if kind == 'p' else 8 * 16 * 38)
            d["ACC"] = cv.f32(8 * nn)
            d["AB16"] = cv.bf(8 * nn)
            d["zT"] = cv.bf(8 * nn)
            d["sgs"] = Slots([cv.f32(nn) for _ in range(2)], "sgt")
            d["mean"] = cv.f32(nn)
            d["m2"] = cv.f32(nn)
            d["rstd"] = cv.f32(nn)
            d["utok"] = cv.f32(1024)
            if kind == 'p':
                d["pacc"] = Slots([cv.f32(256) for _ in range(2)], "pacc")
                d["diag"] = Slots([cv.f32(128) for _ in range(4)], "diag")
            if kind == 's':
                d["stg"] = Slots([cv.f32(1024)], "scbstg")
            d["bufs"] = {k: Buf(k + kind) for k in ("uext", "acc", "ab16", "zT", "mean", "m2", "rstdln", "utok")}
            return d
        NPOOL = 0
        NPE = 12
        MG, MST, MP = (0, 1, 2), 3, (4, 5)
        b_ca = [Buf("convacc%d" % i) for i in range(4)]
        gi = [0]
        pi = [0]
        for s in sg:
            n = s.ntok
            xb = xt_bufs(s)
            bA, bB, bH = blkbufs(s)
            B_ap = big_B(s)
            if state["kind"] != s.kind:
                if state["kind"] is not None:
                    P.barrier(out_bufs)
                state["kind"] = s.kind
                state["d"] = alloc(s.kind)
            d = state["d"]
            UEXTraw, ACC, AB16, zT, sgs = d["UEXTraw"], d["ACC"], d["AB16"], d["zT"], d["sgs"]
            mean, m2, rstd, utok = d["mean"], d["m2"], d["rstd"], d["utok"]
            stg = d.get("stg")
            bb_ = d["bufs"]
            b_u, b_acc, b_ab16, b_zT = bb_["uext"], bb_["acc"], bb_["ab16"], bb_["zT"]
            b_mean, b_m2, b_rstd, b_utok = bb_["mean"], bb_["m2"], bb_["rstdln"], bb_["utok"]
            if s.kind == 'p':
                UE = UEXTraw[:, 0:8 * 286].rearrange("p (c t) -> p c t", t=286)

                def ucur(oc, UE=UE):
                    return UE[:, oc, 30:286]

                def utap(oc, k, UE=UE):
                    return UE[:, oc, k:k + 256]

                def accv(oc):
                    return ACC[:, oc * 256:(oc + 1) * 256]
                if s.first:
                    P.op("pool", lambda UE=UE: nc.gpsimd.memset(UE[:, :, 0:30], 0.0), reads=[b_work], writes=[b_u])
                else:
                    P.op("pool", lambda UE=UE: nc.gpsimd.tensor_copy(out=UE[:, :, 0:30], in_=UC[:]),
                         reads=[b_uc, b_work], writes=[b_u])
            else:
                UE = UEXTraw[:, 0:8 * 16 * 38].rearrange("p (c b t) -> p c b t", b=16, t=38)

                def ucur(oc, UE=UE):
                    return UE[:, oc, :, 30:38]

                def utap(oc, k, UE=UE):
                    return UE[:, oc, :, k:k + 8]

                def accv(oc):
                    return ACC[:, oc * 128:(oc + 1) * 128].rearrange("p (b i) -> p b i", i=8)
                for c4 in range(4):
                    st_, bst = stg.next()
                    src = D["scb"][4 * c4:4 * c4 + 4].rearrange("b r f -> (b r) f")
                    P.dma("sp", bst.name, lambda st_=st_, src=src: nc.sync.dma_start(out=st_[0:120, :], in_=src),
                          reads=[b_work], writes=[bst])
                    for half in range(2):
                        bk = MP[pi[0] % 2]
                        pi[0] += 1
                        pv = bank_f(bk)[:, 0:480].rearrange("p (c t) -> p c t", t=120)
                        for j in range(4):
                            oc = half * 4 + j
                            P.op("pe", lambda pv=pv, j=j, oc=oc, st_=st_: nc.tensor.transpose(
                                pv[:, j, :], st_[0:120, oc * 128:(oc + 1) * 128], ident_f[0:120, 0:120]),
                                reads=[bst, b_idf], writes=[pb[bk]])
                        dst = UE[:, half * 4:half * 4 + 4, 4 * c4:4 * c4 + 4, 0:30]
                        srcv = bank_f(bk)[:, 0:480].rearrange("p (c b t) -> p c b t", b=4, t=30)
                        P.op("act", lambda dst=dst, srcv=srcv: nc.scalar.copy(out=dst, in_=srcv),
                             reads=[pb[bk], b_work], writes=[b_u])
            acc_all = ACC[:, 0:8 * n]
            for oc in range(8):
                bk = MG[gi[0] % 3]
                gi[0] += 1
                for j, Wx in enumerate((Wga, Wgb)):
                    for kc in range(8):
                        P.op("pe", lambda oc=oc, kc=kc, bk=bk, j=j, Wx=Wx: nc.tensor.matmul(
                            bank_f(bk)[:, j * 256:j * 256 + n], lhsT=Wx[:, kc, oc * 128:(oc + 1) * 128],
                            rhs=XT[:, kc, s.tok0:s.tok0 + n], start=(kc == 0), stop=(kc == 7)),
                            reads=bw + xb, writes=[pb[bk]])
                sg_, bsg = sgs.next()
                P.op("act", lambda sg_=sg_, bk=bk: nc.scalar.activation(
                    out=sg_[:, 0:n], in_=bank_f(bk)[:, 256:256 + n], func=AF.Sigmoid),
                    reads=[pb[bk], b_work], writes=[bsg])
                if s.kind == 'p':
                    ga_v, sg_v = bank_f(bk)[:, 0:n], sg_[:, 0:n]
                else:
                    ga_v = bank_f(bk)[:, 0:n].rearrange("p (b i) -> p b i", i=8)
                    sg_v = sg_[:, 0:n].rearrange("p (b i) -> p b i", i=8)
                if s.kind == 'p':
                    P.op("dve", lambda oc=oc, ga_v=ga_v, sg_v=sg_v, ucur=ucur: nc.vector.tensor_tensor(
                        out=ucur(oc), in0=ga_v, in1=sg_v, op=ALU.mult),
                        reads=[pb[bk], bsg], writes=[b_u])
                else:
                    P.op("dve", lambda oc=oc, ga_v=ga_v, sg_v=sg_v: nc.vector.tensor_tensor(
                        out=ACC[:, oc * 128:(oc + 1) * 128].rearrange("p (b i) -> p b i", i=8), in0=ga_v, in1=sg_v,
                        op=ALU.mult),
                        reads=[pb[bk], bsg], writes=[b_acc])
            if s.kind == 's':
                P.op("pool", lambda UE=UE: nc.gpsimd.tensor_copy(
                    out=UE[:, :, :, 30:38], in_=ACC[:, 0:1024].rearrange("p (c b i) -> p c b i", c=8, i=8)),
                    reads=[b_acc], writes=[b_u])
            if s.last or s.kind == 's':
                for half in range(2):
                    bk = MP[pi[0] % 2]
                    pi[0] += 1
                    pv = bank_f(bk).rearrange("p (c t) -> p c t", t=128)
                    for j in range(4):
                        oc = half * 4 + j
                        src = UE[:, oc, 158:286] if s.kind == 'p' else ACC[:, oc * 128:(oc + 1) * 128]
                        P.op("pe", lambda pv=pv, j=j, src=src: nc.tensor.transpose(pv[:, j, :], src, ident_f[:]),
                             reads=[b_u, b_acc, b_idf], writes=[pb[bk]])
                    P.op("act", lambda half=half, bk=bk: nc.scalar.copy(
                        out=utok[:, half * 512:(half + 1) * 512], in_=bank_f(bk)),
                        reads=[pb[bk], b_work], writes=[b_utok])
                if s.kind == 'p':
                    P.dma("sp", "utok", lambda: nc.sync.dma_start(out=D["cbp"], in_=utok[98:128, :]), reads=[b_utok])
                else:
                    for b in range(16):
                        P.dma("sp", "utok", lambda b=b: nc.sync.dma_start(
                            out=D["cbs"][b, 22:30, :], in_=utok[b * 8:(b + 1) * 8, :]), reads=[b_utok])
                out_bufs.append(b_utok)
            for ocp in range(4):
                accs = []
                for oc in (2 * ocp, 2 * ocp + 1):
                    cab = 6 + oc % 2
                    if s.kind == 'p':
                        accP = bank_f(cab)[:, 0:256]
                    else:
                        accP = bank_f(cab)[:, 0:128].rearrange("p (b i) -> p b i", i=8)
                    accs.append((oc, accP, pb[cab]))
                    P.op("act", lambda oc=oc, accP=accP, ucur=ucur: nc.scalar.activation(
                        out=accP, in_=ucur(oc), func=AF.Identity,
                        scale=PARAM1[:, oc, 31:32], bias=PARAM1[:, oc, 32:33]),
                        reads=[b_u, b_par1, b_work], writes=[pb[cab]])
                npool = NPOOL if s.kind == 'p' else 0
                paccs = []
                for oc, accP, bca in accs:
                    if npool:
                        pa, bpa = d["pacc"].next()
                        pt_, bpt = d["ptmp"].next()
                        for k in range(npool):
                            if k == 0:
                                P.op("pool", lambda oc=oc, k=k, pa=pa, utap=utap: nc.gpsimd.tensor_scalar(
                                    out=pa, in0=utap(oc, k), scalar1=PARAM1[:, oc, 1 + k:2 + k], scalar2=None,
                                    op0=ALU.mult), reads=[b_u, b_par1, b_work], writes=[bpa])
                            else:
                                P.op("pool", lambda oc=oc, k=k, pt_=pt_, utap=utap: nc.gpsimd.tensor_scalar(
                                    out=pt_, in0=utap(oc, k), scalar1=PARAM1[:, oc, 1 + k:2 + k], scalar2=None,
                                    op0=ALU.mult), reads=[b_u, b_par1, b_work], writes=[bpt])
                                P.op("pool", lambda pa=pa, pt_=pt_: nc.gpsimd.tensor_tensor(
                                    out=pa, in0=pa, in1=pt_, op=ALU.add), reads=[bpa, bpt], writes=[bpa])
                        paccs.append((pa, bpa))
                npe = NPE if s.kind == 'p' else 0
                pes = []
                for oc, accP, bca in accs:
                    if not npe:
                        break
                    pbk = MP[oc % 2]
                    pep = bank_f(pbk)[:, 0:256]
                    for k in range(npe):
                        dg, bdg = d["diag"].next()
                        P.op("act", lambda dg=dg, oc=oc, k=k: nc.scalar.activation(
                            out=dg, in_=ident_f[:], func=AF.Copy, scale=PARAM1[:, oc, 1 + k:2 + k]),
                            reads=[b_idf, b_par1, b_work], writes=[bdg])
                        P.op("pe", lambda dg=dg, oc=oc, k=k, pep=pep, utap=utap: nc.tensor.matmul(
                            pep, lhsT=dg, rhs=utap(oc, k), start=(k == 0), stop=(k == npe - 1)),
                            reads=[bdg, b_u], writes=[pb[pbk]])
                    pes.append((pep, pbk))
                for k in range(max(npool, npe), 30):
                    for oc, accP, bca in accs:
                        P.op("dve", lambda oc=oc, k=k, accP=accP, utap=utap: nc.vector.scalar_tensor_tensor(
                            out=accP, in0=utap(oc, k), scalar=PARAM1[:, oc, 1 + k:2 + k], in1=accP,
                            op0=ALU.mult, op1=ALU.add),
                            reads=[b_u, bca, b_par1], writes=[bca])
                if npe:
                    for (oc, accP, bca), (pep, pbk) in zip(accs, pes):
                        pa, bpa = d["pacc"].next()
                        P.op("act", lambda pa=pa, pep=pep: nc.scalar.copy(out=pa, in_=pep),
                             reads=[pb[pbk], b_work], writes=[bpa])
                        P.op("dve", lambda accP=accP, pa=pa: nc.vector.tensor_tensor(
                            out=accP, in0=pa, in1=accP, op=ALU.add), reads=[bpa, bca], writes=[bca])
                if npool:
                    for (oc, accP, bca), (pa, bpa) in zip(accs, paccs):
                        P.op("dve", lambda accP=accP, pa=pa: nc.vector.tensor_tensor(
                            out=accP, in0=pa, in1=accP, op=ALU.add), reads=[bpa, bca], writes=[bca])
                for oc, accP, bca in accs:
                    P.op("act", lambda oc=oc, accP=accP, accv=accv: nc.scalar.copy(out=accv(oc), in_=accP),
                         reads=[bca, b_work], writes=[b_acc])
                    ab_v = AB16[:, oc * n:(oc + 1) * n]
                    if s.kind == 's':
                        ab_v = ab_v.rearrange("p (b i) -> p b i", i=8)
                    P.op("act", lambda accP=accP, ab_v=ab_v: nc.scalar.copy(out=ab_v, in_=accP),
                         reads=[bca, b_work], writes=[b_ab16])
            if s.kind == 'p':
                P.op("pool", lambda UE=UE: nc.gpsimd.tensor_copy(out=UC[:], in_=UE[:, :, 256:286]),
                     reads=[b_u], writes=[b_uc])
            for oc in range(8):
                P.op("pe", lambda oc=oc: nc.tensor.matmul(
                    bank_f(MST)[:, 0:n], lhsT=ones_bf[:], rhs=AB16[:, oc * n:(oc + 1) * n],
                    start=(oc == 0), stop=(oc == 7)),
                    reads=[b_ab16, b_ones], writes=[pb[MST]])
            P.op("act", lambda: nc.scalar.copy(out=mean[:, 0:n], in_=bank_f(MST)[:, 0:n]),
                 reads=[pb[MST], b_work], writes=[b_mean])
            P.op("act", lambda acc_all=acc_all: nc.scalar.activation(out=AB16[:, 0:8 * n], in_=acc_all, func=AF.Square),
                 reads=[b_acc], writes=[b_ab16])
            for oc in range(8):
                P.op("pe", lambda oc=oc: nc.tensor.matmul(
                    bank_f(MST)[:, 256:256 + n], lhsT=ones_bf[:], rhs=AB16[:, oc * n:(oc + 1) * n],
                    start=(oc == 0), stop=(oc == 7)),
                    reads=[b_ab16, b_ones], writes=[pb[MST]])
            P.op("dve", lambda: nc.vector.tensor_tensor(out=m2[:, 0:n], in0=mean[:, 0:n], in1=mean[:, 0:n], op=ALU.mult),
                 reads=[b_mean, b_work], writes=[b_m2])
            P.op("dve", lambda: nc.vector.tensor_tensor(out=m2[:, 0:n], in0=bank_f(MST)[:, 256:256 + n], in1=m2[:, 0:n],
                                                        op=ALU.subtract),
                 reads=[pb[MST], b_m2], writes=[b_m2])
            P.op("act", lambda: nc.scalar.activation(out=m2[:, 0:n], in_=m2[:, 0:n], func=AF.Sqrt, bias=EPS, scale=1.0),
                 reads=[b_m2], writes=[b_m2])
            P.op("dve", lambda: nc.vector.reciprocal(out=rstd[:, 0:n], in_=m2[:, 0:n]),
                 reads=[b_m2, b_work], writes=[b_rstd])
            a3 = acc_all.rearrange("p (c t) -> p c t", t=n)
            P.op("dve", lambda a3=a3: nc.vector.tensor_tensor(
                out=a3, in0=a3, in1=mean[:, 0:n].unsqueeze(1).to_broadcast([128, 8, n]), op=ALU.subtract),
                reads=[b_acc, b_mean], writes=[b_acc])
            P.op("dve", lambda a3=a3: nc.vector.tensor_tensor(
                out=a3, in0=a3, in1=rstd[:, 0:n].unsqueeze(1).to_broadcast([128, 8, n]), op=ALU.mult),
                reads=[b_acc, b_rstd], writes=[b_acc])
            for oc in range(8):
                P.op("act", lambda oc=oc: nc.scalar.activation(
                    out=zT[:, oc * n:(oc + 1) * n], in_=ACC[:, oc * n:(oc + 1) * n], func=AF.Silu,
                    scale=PARAM1[:, oc, 33:34], bias=PARAM1[:, oc, 34:35]),
                    reads=[b_acc, b_par1, b_work], writes=[b_zT])
            for half in range(4):
                bk = MP[pi[0] % 2]
                pi[0] += 1
                for j in range(2):
                    oc = half * 2 + j
                    for kc in range(8):
                        P.op("pe", lambda oc=oc, kc=kc, bk=bk, j=j: nc.tensor.matmul(
                            bank_f(bk)[:, j * 256:j * 256 + n], lhsT=Wpb[:, kc, oc * 128:(oc + 1) * 128],
                            rhs=zT[:, kc * n:(kc + 1) * n], start=(kc == 0), stop=(kc == 7)),
                            reads=bw + [b_zT], writes=[pb[bk]])
                src = bank_f(bk).rearrange("p (j t) -> p j t", t=256)[:, :, 0:n]
                P.op("act", lambda src=src, half=half, B_ap=B_ap: nc.scalar.copy(
                    out=B_ap[:, half * 2:half * 2 + 2, :], in_=src),
                    reads=[pb[bk]], writes=[bB] + bH)

    def stage_S3(sg, slot):
        stage_fence()
        Wa = slot_view(slot, 0, 8, 1024)
        Wb = slot_view(slot, 8192, 8, 1024)
        Wout = slot_view(slot, 16384, 8, 1024)
        bw = slotbuf[slot]
        cv = Carver()
        sgs = Slots([cv.f32(512).rearrange("p (j t) -> p j t", t=256) for _ in range(2)], "sg3")
        m1s = Slots([cv.f32(256) for _ in range(2)], "m1")
        m2s = Slots([cv.f32(256) for _ in range(2)], "m2")
        mT = cv.bf(8 * 256); b_mT = Buf("mT")
        xts = Slots([cv.f32(1024) for _ in range(2)], "xt3")
        MG, MH = (0, 1, 2), (3, 4, 5, 6)
        gi = [0]
        hi = [0]
        for s in sg:
            n = s.ntok
            xb = xt_bufs(s)
            bA, bB, bH = blkbufs(s)
            A_ap, B_ap = big_A(s), big_B(s)
            for oc in range(8):
                bk = MG[gi[0] % 3]
                gi[0] += 1
                for j, Wx in enumerate((Wa, Wb)):
                    for kc in range(8):
                        P.op("pe", lambda oc=oc, kc=kc, bk=bk, j=j, Wx=Wx: nc.tensor.matmul(
                            bank_f(bk)[:, j * 256:j * 256 + n], lhsT=Wx[:, kc, oc * 128:(oc + 1) * 128],
                            rhs=XT[:, kc, s.tok0:s.tok0 + n], start=(kc == 0), stop=(kc == 7)),
                            reads=bw + xb, writes=[pb[bk]])
                sg_, bsg = sgs.next()
                src = bank_f(bk).rearrange("p (j t) -> p j t", t=256)[:, :, 0:n]
                P.op("act", lambda sg_=sg_, src=src: nc.scalar.activation(out=sg_[:, :, 0:n], in_=src, func=AF.Sigmoid),
                     reads=[pb[bk], b_work], writes=[bsg])
                m1, bm1 = m1s.next()
                m2_, bm2 = m2s.next()
                P.op("pool", lambda m1=m1, sg_=sg_, oc=oc, A_ap=A_ap: nc.gpsimd.tensor_tensor(
                    out=m1[:, 0:n], in0=sg_[:, 0, 0:n], in1=A_ap[:, oc, :], op=ALU.mult),
                    reads=[bsg, bA, b_work], writes=[bm1])
                P.op("dve", lambda m2_=m2_, sg_=sg_, oc=oc, B_ap=B_ap: nc.vector.tensor_tensor(
                    out=m2_[:, 0:n], in0=sg_[:, 1, 0:n], in1=B_ap[:, oc, :], op=ALU.mult),
                    reads=[bsg, bB, b_work], writes=[bm2])
                P.op("dve", lambda m1=m1, m2_=m2_, oc=oc: nc.vector.tensor_tensor(
                    out=mT[:, oc * n:(oc + 1) * n], in0=m1[:, 0:n], in1=m2_[:, 0:n], op=ALU.add),
                    reads=[bm1, bm2, b_work], writes=[b_mT])
            for ti in range(s.nt):
                xt_, bxt = xts.next()
                P.dma("sp", bxt.name, lambda xt_=xt_, s=s, ti=ti: nc.sync.dma_start(out=xt_, in_=x_src(s, ti)),
                      reads=[b_work], writes=[bxt])
                H = big_H(s, ti)
                for hf in range(2):
                    bk = MH[hi[0] % 4]
                    hi[0] += 1
                    for kc in range(8):
                        P.op("pe", lambda kc=kc, bk=bk, ti=ti, hf=hf: nc.tensor.matmul(
                            bank_f(bk), lhsT=mT[:, kc * n + ti * 128:kc * n + (ti + 1) * 128],
                            rhs=Wout[:, kc, hf * 512:(hf + 1) * 512], start=(kc == 0), stop=(kc == 7)),
                            reads=bw + [b_mT], writes=[pb[bk]])
                    P.op("dve", lambda H=H, bk=bk, hf=hf, xt_=xt_: nc.vector.tensor_tensor(
                        out=H[:, hf * 512:(hf + 1) * 512], in0=bank_f(bk), in1=xt_[:, hf * 512:(hf + 1) * 512],
                        op=ALU.add),
                        reads=[pb[bk], bxt], writes=[bH[ti], bA, bB])

    def stage_F(sg, slot, fg):
        stage_fence()
        j0, j1 = FG[fg]
        nch = j1 - j0
        Wg = slot_view(slot, 0, 8, nch * 128)
        Wv_ = slot_view(slot, 8192, 8, nch * 128)
        Wdn = ARENA[:, slot, 16384:16384 + nch * 1024].rearrange("p (j m) -> p j m", m=1024)
        bw = slotbuf[slot]
        cv = Carver()
        upxs = Slots([cv.f32(2 * 258) for _ in range(4)], "upx")
        cgs = Slots([cv.f32(512).rearrange("p (g t) -> p g t", t=256) for _ in range(4)], "cg")
        bcg_half = {bf_.name: [Buf(bf_.name + "g"), Buf(bf_.name + "v")] for _, bf_ in cgs.items}
        sgts = Slots([cv.f32(256) for _ in range(4)], "sgf")
        actTs = Slots([cv.bf(8 * 256) for _ in range(2)], "actT")
        hstg = cv.f32(2048); b_hstg = Buf("hstg")
        HIST = cv.f32(16 * 32); b_hist = Buf("hist")
        MU, MD, MX = (0, 1, 2, 3), (4, 5, 6), 7
        b_cg = {2: [Buf("cg2g"), Buf("cg2v")], 3: [Buf("cg3g"), Buf("cg3v")]}
        ui = [0]
        di = [0]
        ci = [0]
        pending = [None]
        CF4 = CF[:].rearrange("p (g j) r -> p g j r", g=2)
        CFS4 = CFS[:].rearrange("p (g j) b r -> p g j b r", g=2)
        P2 = PARAM2[:].rearrange("p (g j) r -> p g j r", g=2)
        def make_sub(s):
            n = s.ntok
            xb = xt_bufs(s)
            bA, bB, bH = blkbufs(s)
            actT, bact = actTs.next()
            if s.kind == 's':
                scf2 = D["scf"].rearrange("b r f -> (b r) f")
                hv = hstg.rearrange("p (g f) -> p g f", g=2)
                for g_ in range(2):
                    c0 = g_ * 2816 + j0 * 128
                    P.dma("sp", "hstg", lambda g_=g_, c0=c0: nc.sync.dma_start(
                        out=hv[0:32, g_, 0:nch * 128], in_=scf2[:, c0:c0 + nch * 128]),
                        reads=[b_work], writes=[b_hstg])
                pv = bank_f(MX).rearrange("p (c t) -> p c t", t=32)
                for g_ in range(2):
                    for jj in range(nch):
                        P.op("pe", lambda g_=g_, jj=jj, pv=pv: nc.tensor.transpose(
                            pv[:, g_ * 8 + jj, :], hv[0:32, g_, jj * 128:(jj + 1) * 128], ident_f[0:32, 0:32]),
                            reads=[b_hstg, b_idf], writes=[pb[MX]])
                P.op("act", lambda: nc.scalar.copy(out=HIST, in_=bank_f(MX)), reads=[pb[MX], b_work], writes=[b_hist])
                HV = HIST.rearrange("p (g j b r) -> p g j b r", g=2, j=8, r=2)
            def stageA(jj):
                j = j0 + jj
                bk = MU[ui[0] % 4]
                ui[0] += 1
                for g_, Wx in enumerate((Wg, Wv_)):
                    for kc in range(8):
                        P.op("pe", lambda jj=jj, kc=kc, bk=bk, g_=g_, Wx=Wx: nc.tensor.matmul(
                            bank_f(bk)[:, g_ * 256:g_ * 256 + n], lhsT=Wx[:, kc, jj * 128:(jj + 1) * 128],
                            rhs=XT[:, kc, s.tok0:s.tok0 + n], start=(kc == 0), stop=(kc == 7)),
                            reads=bw + xb, writes=[pb[bk]])
                upx, bupx = upxs.next()
                src = bank_f(bk).rearrange("p (g t) -> p g t", t=256)[:, :, 0:n]
                if s.kind == 'p':
                    U = upx.rearrange("p (g t) -> p g t", t=258)
                    P.op("act", lambda U=U, src=src: nc.scalar.copy(out=U[:, :, 2:258], in_=src),
                         reads=[pb[bk], b_work], writes=[bupx])
                    if s.first:
                        P.op("pool", lambda U=U: nc.gpsimd.memset(U[:, :, 0:2], 0.0), reads=[b_work], writes=[bupx])
                    else:
                        P.op("pool", lambda U=U, j=j: nc.gpsimd.tensor_copy(out=U[:, :, 0:2], in_=CF4[:, :, j, :]),
                             reads=[b_cf, b_work], writes=[bupx])
                    P.op("pool", lambda U=U, j=j: nc.gpsimd.tensor_copy(out=CF4[:, :, j, :], in_=U[:, :, 256:258]),
                         reads=[bupx], writes=[b_cf])

                    def tap(g_, k, U=U):
                        return U[:, g_, k:k + 256]

                else:
                    U = upx[:, 0:320].rearrange("p (g b t) -> p g b t", g=2, t=10)
                    P.op("act", lambda U=U, src=src: nc.scalar.copy(
                        out=U[:, :, :, 2:10], in_=src.rearrange("p g (b i) -> p g b i", i=8)),
                        reads=[pb[bk], b_work], writes=[bupx])
                    P.op("pool", lambda U=U, jj=jj: nc.gpsimd.tensor_copy(out=U[:, :, :, 0:2], in_=HV[:, :, jj, :, :]),
                         reads=[b_hist, b_work], writes=[bupx])
                    P.op("pool", lambda U=U, j=j: nc.gpsimd.tensor_copy(out=CFS4[:, :, j, :, :], in_=U[:, :, :, 8:10]),
                         reads=[bupx], writes=[b_cfs])

                    def tap(g_, k, U=U):
                        return U[:, g_, :, k:k + 8]

                return dict(tap=tap, bupx=bupx, j=j, jj=jj)

            def stageB(st):
                tap, bupx, j, jj = st['tap'], st['bupx'], st['j'], st['jj']
                cg, bcg = cgs.next()
                if s.kind == 'p':
                    def cgP(g_, cg=cg):
                        return cg[:, g_, :]
                else:
                    def cgP(g_, cg=cg):
                        return cg[:, g_, 0:128].rearrange("p (b i) -> p b i", i=8)
                bcgh = bcg_half[bcg.name]
                for g_ in range(2):
                    P.op("act", lambda g_=g_, j=j, tap=tap, cgP=cgP: nc.scalar.activation(
                        out=cgP(g_), in_=tap(g_, 2), func=AF.Identity,
                        scale=P2[:, g_, j, 2:3], bias=P2[:, g_, j, 3:4]),
                        reads=[bupx, b_par2, b_work], writes=[bcgh[g_]])
                for k in (1, 0):
                    for g_ in range(2):
                        P.op("dve", lambda g_=g_, j=j, k=k, tap=tap, cgP=cgP: nc.vector.scalar_tensor_tensor(
                            out=cgP(g_), in0=tap(g_, k), scalar=P2[:, g_, j, k:k + 1], in1=cgP(g_),
                            op0=ALU.mult, op1=ALU.add),
                            reads=[bupx, bcgh[g_], b_par2], writes=[bcgh[g_]])
                st['cg'], st['bcgh'] = cg, bcgh

            def stageC(st):
                cg, bcgh, jj = st['cg'], st['bcgh'], st['jj']
                sgt, bsgt = sgts.next()
                P.op("act", lambda: nc.scalar.activation(
                    out=sgt[:, 0:n], in_=cg[:, 0, 0:n], func=AF.Silu),
                    reads=[bcgh[0], b_work], writes=[bsgt])
                P.op("pool", lambda: nc.gpsimd.tensor_tensor(
                    out=actT[:, jj * n:(jj + 1) * n], in0=sgt[:, 0:n], in1=cg[:, 1, 0:n], op=ALU.mult),
                    reads=[bsgt, bcgh[1], b_work], writes=[bact])
            def down():
                for ti in range(s.nt):
                    H = big_H(s, ti)
                    for hf in range(2):
                        bk = MD[di[0] % 3]
                        di[0] += 1
                        for jj in range(nch):
                            P.op("pe", lambda jj=jj, bk=bk, ti=ti, hf=hf: nc.tensor.matmul(
                                bank_f(bk), lhsT=actT[:, jj * n + ti * 128:jj * n + (ti + 1) * 128],
                                rhs=Wdn[:, jj, hf * 512:(hf + 1) * 512], start=(jj == 0), stop=(jj == nch - 1)),
                                reads=bw + [bact], writes=[pb[bk]])
                        P.op("dve", lambda H=H, bk=bk, hf=hf: nc.vector.tensor_tensor(
                            out=H[:, hf * 512:(hf + 1) * 512], in0=bank_f(bk), in1=H[:, hf * 512:(hf + 1) * 512],
                            op=ALU.add),
                            reads=[pb[bk], bH[ti]], writes=[bH[ti]])
            return stageA, stageB, stageC, down

        subs = [make_sub(s_) for s_ in sg]
        items = [(si_, jj_) for si_ in range(len(sg)) for jj_ in range(nch)]
        sts = {}
        for t in range(len(items) + 2):
            if t < len(items):
                sts[t] = subs[items[t][0]][0](items[t][1])
            if 0 <= t - 1 < len(items):
                subs[items[t - 1][0]][1](sts[t - 1])
            if 0 <= t - 2 < len(items):
                si2, jj2 = items[t - 2]
                subs[si2][2](sts[t - 2])
                if jj2 == nch - 1:
                    subs[si2][3]()

    def stage_final(sg, sgi):
        stage_fence()
        cv = Carver()
        junk = cv.bf(1024)
        tstg = Slots([cv.f32(1408) for _ in range(2)], "tstg")
        tiles = [(s, ti) for s in sg for ti in range(s.nt)]
        nt = len(tiles)
        for n, (s, ti) in enumerate(tiles):
            P.op("act", lambda s=s, ti=ti, n=n: nc.scalar.activation(
                out=junk, in_=big_H(s, ti), func=AF.Square, accum_out=RS[:, 2, n:n + 1]),
                reads=[blkbufs(s)[2][ti], b_work], writes=[b_junk, b_rs[2]])
        P.op("act", lambda: nc.scalar.activation(out=RS2[:, 0:nt], in_=RS[:, 2, 0:nt], func=AF.Sqrt,
                                                 bias=EPS, scale=1.0 / 1024.0),
             reads=[b_rs[2]], writes=[b_rs2])
        P.op("dve", lambda: nc.vector.reciprocal(out=RS[:, 2, 0:nt], in_=RS2[:, 0:nt]),
             reads=[b_rs2], writes=[b_rs[2]])
        for n, (s, ti) in enumerate(tiles):
            H = big_H(s, ti)
            bh = blkbufs(s)[2][ti]
            P.op("dve", lambda H=H, n=n: nc.vector.scalar_tensor_tensor(
                out=H, in0=H, scalar=RS[:, 2, n:n + 1], in1=gf_bc[:], op0=ALU.mult, op1=ALU.mult),
                reads=[bh, b_rs[2], b_gf], writes=[bh])
            P.dma("sp", "yout%d" % blk_index[s.blk], lambda H=H, s=s, ti=ti: nc.sync.dma_start(out=y_dst(s, ti), in_=H),
                  reads=[bh])
            out_bufs.append(bh)
        jobs = []
        if sgi == 0:
            jobs.append(("s", CFS[:].rearrange("p c b r -> p c (b r)"), 32, D["cfs"].rearrange("b r f -> (b r) f"), b_cfs))
        else:
            jobs.append(("p", CF[:], 2, D["cfp"], b_cf))
        bi = [0]
        for kind, srcT, nr, dst, bsrc in jobs:
            for rnd in range(4):
                st_, bst = tstg.next()
                for q in range(3):
                    bk = (bi[0] % 2) * 3 + q
                    nchq = min(4, 11 - q * 4)
                    for jj in range(nchq):
                        c = rnd * 11 + q * 4 + jj
                        P.op("pe", lambda bk=bk, jj=jj, c=c, srcT=srcT, nr=nr: nc.tensor.transpose(
                            bank_f(bk)[0:nr, jj * 128:(jj + 1) * 128], srcT[:, c, :], ident_f[:]),
                            reads=[bsrc, b_idf], writes=[pb[bk]])
                    P.op("act", lambda bk=bk, q=q, nchq=nchq, st_=st_, nr=nr: nc.scalar.copy(
                        out=st_[0:nr, q * 512:q * 512 + nchq * 128], in_=bank_f(bk)[0:nr, 0:nchq * 128]),
                        reads=[pb[bk], b_work], writes=[bst])
                bi[0] += 1
                P.dma("sp", bst.name, lambda st_=st_, nr=nr, dst=dst, rnd=rnd: nc.sync.dma_start(
                    out=dst[:, rnd * 1408:(rnd + 1) * 1408], in_=st_[0:nr, :]), reads=[bst])
                out_bufs.append(bst)

    import os
    nc_real = nc
    nc = _NcProxy()
    stages = []
    for sgi in range(2):
        for st in ("S1", "S2", "S3", "F0", "F1", "F2"):
            stages.append((sgi, st))
    init_consts()
    load_stage_weights(stages[0][1], 0)
    maxst = int(os.environ.get("KSTAGES", "99"))
    substop = os.environ.get("KSUB", "")
    for i, (sgi, st) in enumerate(stages):
        if i >= maxst:
            break
        slot = i % 2
        sg = SGS[sgi]
        if i + 1 < len(stages) and "w2" not in os.environ.get("KSKIP", "") and not (i == 0 and os.environ.get("KORDER", "") == "early"):
            load_stage_weights(stages[i + 1][1], (i + 1) % 2)
        if st == "S1":
            stage_normT(sg, 0)
            if substop == "X":
                break
            stage_S1(sg, slot)
        elif st == "S2":
            if sgi == 0:
                issue_d2d()
            stage_S2(sg, slot)
        elif st == "S3":
            stage_S3(sg, slot)
            stage_normT(sg, 1)
        else:
            stage_F(sg, slot, int(st[1]))
            if st == "F2":
                stage_final(sg, sgi)
    print("TOTAL_OPS", len(P.ops))
    if os.environ.get("KDUMPOP"):
        for k_ in os.environ["KDUMPOP"].split(","):
            o_ = P.ops[int(k_)]
            print("OP", k_, o_[0], o_[1].name, o_[2], o_[3][:2], [(a.name, w) for a, w in o_[4]])
            print("   ARGS", {kk: (vv.shape, vv.offset, vv.ap) if hasattr(vv, "ap") else vv for kk, vv in o_[3][3].items()})
    if P.limit is not None and P.ops:
        print("LAST_OP", P.ops[min(P.limit, len(P.ops)) - 1][1].name, P.ops[min(P.limit, len(P.ops)) - 1][3][:2])
    P.wait_all("sp", out_bufs + slotbufs[0] + slotbufs[1] + [b_gf, b_esink])
    P.emit()
    ctx.close()
    return nc_real


_NC_CACHE = {}


def kernel(**inputs):
    f = lambda a: np.ascontiguousarray(np.asarray(a, dtype=np.float32))
    xp = f(inputs["x_prompt"]); xs = f(inputs["x_sample"])
    ck = f(inputs["cache_k_win"])[0].reshape(128, 128, 256)
    cv = f(inputs["cache_v_win"])[0].reshape(128, 128, 256)
    scb = f(inputs["state_conv_b"])[0]
    scf = f(inputs["state_conv_ffn"])[0]
    shared = {
        "norm1_g": f(inputs["norm1_g"]).reshape(1, 1024), "w_in": f(inputs["w_in"])[0],
        "sinks": f(inputs["sinks"]).reshape(1, 16), "w_o": f(inputs["w_o_attn"])[0],
        "conv_b_w": f(inputs["conv_b_w"])[0], "conv_b_bias": f(inputs["conv_b_bias"]).reshape(1, 1024),
        "ln_b_g": f(inputs["ln_b_g"]).reshape(1, 1024), "ln_b_b": f(inputs["ln_b_b"]).reshape(1, 1024),
        "w_pb": f(inputs["w_pb"])[0], "w_out": f(inputs["w_out"])[0],
        "norm2_g": f(inputs["norm2_g"]).reshape(1, 1024), "w_up": f(inputs["w_up"])[0],
        "ffn_conv_w": f(inputs["ffn_conv_w"])[0], "ffn_conv_b": f(inputs["ffn_conv_b"]).reshape(1, 5632),
        "w_down": f(inputs["w_down"])[0], "norm_f_g": f(inputs["norm_f_g"]).reshape(1, 1024),
    }
    in_maps = []
    for c in range(NCORES):
        m = dict(shared)
        m["xp"] = xp[c]
        m["xs"] = xs[16 * c:16 * c + 16].reshape(128, 1024)
        m["ck"] = ck[16 * c:16 * c + 16]
        m["cv"] = cv[16 * c:16 * c + 16]
        m["scb"] = scb[16 * c:16 * c + 16]
        m["scf"] = scf[16 * c:16 * c + 16]
        in_maps.append(m)
    if "nc" not in _NC_CACHE:
        _NC_CACHE["nc"] = build_program()
    res = run_bass_kernel_spmd(_NC_CACHE["nc"], in_maps, core_ids=list(range(NCORES)))
    R = res.results
    cat = lambda k: np.concatenate([np.asarray(r[k], dtype=np.float32) for r in R], axis=0)
    y_prompt = np.stack([np.asarray(r["yp"], dtype=np.float32) for r in R], axis=0)
    y_sample = cat("ys").reshape(128, 8, 1024)
    k_win_prompt = np.stack([np.asarray(r["kwp"]) for r in R], 0).reshape(1, 8, 128, 4, 64).astype(np.float32)
    v_win_prompt = np.stack([np.asarray(r["vwp"]) for r in R], 0).reshape(1, 8, 128, 4, 64).astype(np.float32)
    conv_b_prompt = np.stack([np.asarray(r["cbp"]) for r in R], 0).reshape(1, 8, 30, 1024).astype(np.float32)
    conv_ffn_prompt = np.stack([np.asarray(r["cfp"]) for r in R], 0).reshape(1, 8, 2, 5632).astype(np.float32)
    k_win_sample = cat("kws").reshape(1, 128, 128, 4, 64)
    v_win_sample = cat("vws").reshape(1, 128, 128, 4, 64)
    conv_b_sample = cat("cbs").reshape(1, 128, 30, 1024)
    conv_ffn_sample = cat("cfs").reshape(1, 128, 2, 5632)
    return (y_prompt, y_sample, k_win_prompt, v_win_prompt, conv_b_prompt, conv_ffn_prompt,
            k_win_sample, v_win_sample, conv_b_sample, conv_ffn_sample)
```
